# Optimizing a Trainium2 kernel written in Bass

```python
import math
import jax, jax.numpy as jnp
from jax import lax
import numpy as np

D_MODEL = 1024
BATCH = 8
SEQ = 4096
DEPTH = 2
DEC_BATCH = 8
DEC_SEQ = 8192
PAST_LEN = 128

N_MEM = 256
EPS = 1e-6
D_FF = 2816
FNET_HEADS = 4
FNET_HEAD_DIM = D_MODEL // FNET_HEADS
FNET_WIDTH = FNET_HEADS * FNET_HEAD_DIM
SSM_HEAD_DIM = 64
SSM_INNER = 2 * D_MODEL
SSM_HEADS = SSM_INNER // SSM_HEAD_DIM
SSM_STATE = 128
SSM_GROUPS = 4
SSM_CONV = 5
SSM_CHUNK = 128
SSM_CONV_CH = SSM_INNER + 2 * SSM_GROUPS * SSM_STATE
DT_MIN = 0.001
DT_MAX = 0.1
AB_IN = FNET_WIDTH + SSM_INNER + SSM_CONV_CH + 2 * SSM_HEADS
AB_OUT = FNET_WIDTH + SSM_INNER
POOL_WINDOWS = (2, 4, 8, 16)
POOL_GROUPS = len(POOL_WINDOWS)
POOL_GROUP_DIM = D_MODEL // POOL_GROUPS
POOL_WIDTH = POOL_GROUPS * POOL_GROUP_DIM
DIFF_HEADS = 8
DIFF_HEAD_DIM = 64
DIFF_V_DIM = 2 * DIFF_HEAD_DIM
DIFF_QK_WIDTH = DIFF_HEADS * 2 * DIFF_HEAD_DIM
DIFF_V_WIDTH = DIFF_HEADS * DIFF_V_DIM
ROPE_THETA = 500000.0
ROT_DIM = DIFF_HEAD_DIM // 4
Q_BLOCK = 128
CD_IN = POOL_WIDTH + 2 * DIFF_QK_WIDTH + DIFF_V_WIDTH
CD_OUT = POOL_WIDTH + DIFF_V_WIDTH
CROSS_HEADS = 4
CROSS_HEAD_DIM = D_MODEL // CROSS_HEADS
N_EVEN = (DEPTH + 1) // 2
N_ODD = DEPTH // 2

kernel_name = "hybrid_bidir_fnet_ssd_pool_diffattn_encoder"


def _rms_norm(x, g):
    xf = x.astype(jnp.float32)
    y = xf * lax.rsqrt(jnp.mean(xf * xf, axis=-1, keepdims=True) + EPS)
    return (y * g.astype(jnp.float32)).astype(x.dtype)


def _swiglu(x, w_gate, w_up, w_down):
    return (jax.nn.silu(x @ w_gate) * (x @ w_up)) @ w_down


def _lambda_init(layer_idx):
    return 0.8 - 0.6 * math.exp(-0.3 * layer_idx)


def _rope_tables(length):
    inv = ROPE_THETA ** (-jnp.arange(0, ROT_DIM, 2, dtype=jnp.float32) / ROT_DIM)
    ang = jnp.arange(length, dtype=jnp.float32)[:, None] * inv[None, :]
    return jnp.cos(ang), jnp.sin(ang)


def _partial_rope(t, cos, sin):
    half = ROT_DIM // 2
    c = cos[None, :, None, None, :].astype(t.dtype)
    s = sin[None, :, None, None, :].astype(t.dtype)
    t1 = t[..., :half]
    t2 = t[..., half:ROT_DIM]
    return jnp.concatenate([t1 * c - t2 * s, t2 * c + t1 * s, t[..., ROT_DIM:]], axis=-1)


def _ssd_scan(x, dt, a, b_in, c_in):
    nb, length, nh, hp = x.shape
    ng, ns = b_in.shape[2], b_in.shape[3]
    hpg = nh // ng
    nc = length // SSM_CHUNK

    def chunks(t):
        return jnp.moveaxis(t.reshape((nb, nc, SSM_CHUNK) + t.shape[2:]), 1, 0)

    lower = jnp.tril(jnp.ones((SSM_CHUNK, SSM_CHUNK), dtype=bool))

    def step(state, inp):
        xc, dtc, bc, cc = inp
        acum = jnp.cumsum(dtc * a, axis=1)
        bh = jnp.repeat(bc, hpg, axis=2)
        ch = jnp.repeat(cc, hpg, axis=2)
        ah = jnp.swapaxes(acum, 1, 2)
        seg = ah[..., :, None] - ah[..., None, :]
        decay = jnp.exp(jnp.where(lower, seg, -jnp.inf))
        xdt = xc * dtc[..., None]
        scores = jnp.einsum('blhn,bshn->bhls', ch, bh) * decay
        y = jnp.einsum('bhls,bshp->blhp', scores, xdt)
        y = y + jnp.einsum('blhn,bhpn->blhp', ch, state) * jnp.exp(acum)[..., None]
        last = acum[:, -1]
        w_end = jnp.exp(last[:, None, :] - acum)
        state = state * jnp.exp(last)[:, :, None, None] + jnp.einsum('bsh,bshn,bshp->bhpn', w_end, bh, xdt)
        return state, y

    s0 = jnp.zeros((nb, nh, hp, ns), jnp.float32)
    _, ys = lax.scan(step, s0, (chunks(x), chunks(dt), chunks(b_in), chunks(c_in)))
    return jnp.moveaxis(ys, 0, 1).reshape(nb, length, nh, hp)


def _mixer_ab(hn, w_in, conv_w, conv_b, dt_bias, a_log, d_skip, gate_norm, w_out):
    f32 = jnp.float32
    nb, length, _ = hn.shape
    proj = hn @ w_in
    u_f, z, xbc, dt_raw = jnp.split(
        proj, [FNET_WIDTH, FNET_WIDTH + SSM_INNER, FNET_WIDTH + SSM_INNER + SSM_CONV_CH], axis=-1)
    uf = u_f.astype(f32).reshape(nb, length, FNET_HEADS, FNET_HEAD_DIM)
    y_four = jnp.fft.fftn(uf, axes=(1, 3), norm='ortho').real
    y_four = y_four.reshape(nb, length, FNET_WIDTH).astype(hn.dtype)
    xbc = lax.conv_general_dilated(
        xbc, conv_w[:, None, :].astype(xbc.dtype), window_strides=(1,),
        padding=[(SSM_CONV // 2, SSM_CONV // 2)], dimension_numbers=('NWC', 'WIO', 'NWC'),
        feature_group_count=SSM_CONV_CH)
    xbc = jax.nn.silu(xbc + conv_b).astype(f32)
    xs, bs, cs = jnp.split(xbc, [SSM_INNER, SSM_INNER + SSM_GROUPS * SSM_STATE], axis=-1)
    xs = xs.reshape(nb, length, SSM_HEADS, SSM_HEAD_DIM)
    bs = bs.reshape(nb, length, SSM_GROUPS, SSM_STATE)
    cs = cs.reshape(nb, length, SSM_GROUPS, SSM_STATE)
    dt = jax.nn.softplus(dt_raw.astype(f32).reshape(nb, length, 2, SSM_HEADS) + dt_bias.astype(f32))
    a = -jnp.exp(a_log.astype(f32))
    y_fw = _ssd_scan(xs, dt[:, :, 0], a[0], bs, cs)
    flip = lambda t: jnp.flip(t, axis=1)
    y_bw = flip(_ssd_scan(flip(xs), flip(dt[:, :, 1]), a[1], flip(bs), flip(cs)))
    y = y_fw + y_bw + d_skip.astype(f32)[:, None] * xs
    y = y.reshape(nb, length, SSM_INNER) * jax.nn.silu(z.astype(f32))
    y = _rms_norm(y.reshape(nb, length, SSM_GROUPS, SSM_INNER // SSM_GROUPS),
                  gate_norm.reshape(SSM_GROUPS, SSM_INNER // SSM_GROUPS))
    y = y.reshape(nb, length, SSM_INNER).astype(hn.dtype)
    return jnp.concatenate([y_four, y], axis=-1) @ w_out


def _mixer_cd(hn, w_in, pool_w, pool_scale, q_norm, k_norm, lq1, lk1, lq2, lk2, sub_norm, w_out, lambda_init):
    f32 = jnp.float32
    nb, length, _ = hn.shape
    proj = hn @ w_in
    u_p, q, k, v = jnp.split(
        proj, [POOL_WIDTH, POOL_WIDTH + DIFF_QK_WIDTH, POOL_WIDTH + 2 * DIFF_QK_WIDTH], axis=-1)
    upf = u_p.astype(f32).reshape(nb, length, POOL_GROUPS, POOL_GROUP_DIM)
    prefix = jnp.concatenate(
        [jnp.zeros((nb, 1, POOL_GROUPS, POOL_GROUP_DIM), f32), jnp.cumsum(upf, axis=1)], axis=1)
    pos = jnp.arange(length)
    outs = []
    for g, w in enumerate(POOL_WINDOWS):
        lo = jnp.clip(pos - w // 2, 0, length - 1)
        hi = jnp.clip(pos + w // 2 - 1, 0, length - 1)
        pg = prefix[:, :, g]
        total = jnp.take(pg, hi + 1, axis=1) - jnp.take(pg, lo, axis=1)
        mean = total / (hi - lo + 1).astype(f32)[:, None]
        outs.append(jnp.einsum('bsc,cd->bsd', (mean - upf[:, :, g]).astype(hn.dtype), pool_w[g]))
    y_pool = jnp.concatenate(outs, axis=-1) * pool_scale
    q = _rms_norm(q.reshape(nb, length, DIFF_HEADS, 2, DIFF_HEAD_DIM), q_norm)
    k = _rms_norm(k.reshape(nb, length, DIFF_HEADS, 2, DIFF_HEAD_DIM), k_norm)
    cos, sin = _rope_tables(length)
    q = _partial_rope(q, cos, sin) * (DIFF_HEAD_DIM ** -0.5)
    k = _partial_rope(k, cos, sin)
    v = v.reshape(nb, length, DIFF_HEADS, DIFF_V_DIM)
    lam = (jnp.exp(jnp.sum(lq1.astype(f32) * lk1.astype(f32)))
           - jnp.exp(jnp.sum(lq2.astype(f32) * lk2.astype(f32))) + lambda_init)
    q_blocks = jnp.moveaxis(
        q.reshape(nb, length // Q_BLOCK, Q_BLOCK, DIFF_HEADS, 2, DIFF_HEAD_DIM), 1, 0)

    def attend(q_blk):
        s = jnp.einsum('bqhcd,bkhcd->bhcqk', q_blk, k).astype(f32)
        p = jax.nn.softmax(s, axis=-1)
        wts = (p[:, :, 0] - lam * p[:, :, 1]).astype(v.dtype)
        return jnp.einsum('bhqk,bkhe->bqhe', wts, v)

    o = lax.map(attend, q_blocks)
    o = jnp.moveaxis(o, 0, 1).reshape(nb, length, DIFF_HEADS, DIFF_V_DIM)
    o = (_rms_norm(o, sub_norm) * (1.0 - lambda_init)).reshape(nb, length, DIFF_V_WIDTH)
    return jnp.concatenate([y_pool.astype(hn.dtype), o.astype(hn.dtype)], axis=-1) @ w_out


def _cross_attn(hn, mem_n, w_q, w_kv, q_norm, k_norm, w_o):
    nb, length, _ = hn.shape
    n_mem = mem_n.shape[1]
    q = _rms_norm((hn @ w_q).reshape(nb, length, CROSS_HEADS, CROSS_HEAD_DIM), q_norm)
    q = q * (CROSS_HEAD_DIM ** -0.5)
    k, v = jnp.split(mem_n @ w_kv, [D_MODEL], axis=-1)
    k = _rms_norm(k.reshape(nb, n_mem, CROSS_HEADS, CROSS_HEAD_DIM), k_norm)
    v = v.reshape(nb, n_mem, CROSS_HEADS, CROSS_HEAD_DIM)
    s = jnp.einsum('bqhd,bkhd->bhqk', q, k).astype(jnp.float32)
    p = jax.nn.softmax(s, axis=-1).astype(v.dtype)
    o = jnp.einsum('bhqk,bkhd->bqhd', p, v).reshape(nb, length, D_MODEL)
    return o @ w_o


def _trunk(x, mem, p):
    for l in range(DEPTH):
        x = x + 0.5 * _swiglu(_rms_norm(x, p['ffn1_norm'][l]), p['ffn1_w_gate'][l],
                              p['ffn1_w_up'][l], p['ffn1_w_down'][l])
        hn = _rms_norm(x, p['mix_norm'][l])
        i = l // 2
        if l % 2 == 0:
            x = x + _mixer_ab(hn, p['ab_w_in'][i], p['ab_conv_w'][i], p['ab_conv_b'][i],
                              p['ab_dt_bias'][i], p['ab_a_log'][i], p['ab_d_skip'][i],
                              p['ab_gate_norm'][i], p['ab_w_out'][i])
        else:
            x = x + _mixer_cd(hn, p['cd_w_in'][i], p['cd_pool_w'][i], p['cd_pool_scale'][i],
                              p['cd_q_norm'][i], p['cd_k_norm'][i], p['cd_lambda_q1'][i],
                              p['cd_lambda_k1'][i], p['cd_lambda_q2'][i], p['cd_lambda_k2'][i],
                              p['cd_sub_norm'][i], p['cd_w_out'][i], _lambda_init(l))
        x = x + _cross_attn(_rms_norm(x, p['cross_norm'][l]), _rms_norm(mem, p['cross_mem_norm'][l]),
                            p['cross_w_q'][l], p['cross_w_kv'][l], p['cross_q_norm'][l],
                            p['cross_k_norm'][l], p['cross_w_o'][l])
        x = x + 0.5 * _swiglu(_rms_norm(x, p['ffn2_norm'][l]), p['ffn2_w_gate'][l],
                              p['ffn2_w_up'][l], p['ffn2_w_down'][l])
    return x


def setup_inputs(seed: int = 0) -> dict:
    key = jax.random.key(seed)
    ks = iter(jax.random.split(key, 64))
    f32 = jnp.float32

    def nrm(shape, scale):
        return jax.random.normal(next(ks), shape, f32) * scale

    def gain(shape):
        return 1.0 + 0.02 * jax.random.normal(next(ks), shape, f32)

    dt0 = jnp.exp(jax.random.uniform(next(ks), (N_EVEN, 2, SSM_HEADS), f32)
                  * (math.log(DT_MAX) - math.log(DT_MIN)) + math.log(DT_MIN))
    ab_dt_bias = dt0 + jnp.log(-jnp.expm1(-dt0))
    ab_a_log = jnp.log(jax.random.uniform(next(ks), (N_EVEN, 2, SSM_HEADS), f32, 1.0, 16.0))
    return {
        'x_prompt': nrm((BATCH, SEQ, D_MODEL), 1.0),
        'x_sample': nrm((DEC_BATCH, DEC_SEQ, D_MODEL), 1.0),
        'mem_prompt': nrm((BATCH, N_MEM, D_MODEL), 1.0),
        'mem_sample': nrm((DEC_BATCH, N_MEM, D_MODEL), 1.0),
        'ffn1_norm': gain((DEPTH, D_MODEL)),
        'ffn1_w_gate': nrm((DEPTH, D_MODEL, D_FF), D_MODEL ** -0.5),
        'ffn1_w_up': nrm((DEPTH, D_MODEL, D_FF), D_MODEL ** -0.5),
        'ffn1_w_down': nrm((DEPTH, D_FF, D_MODEL), D_FF ** -0.5),
        'mix_norm': gain((DEPTH, D_MODEL)),
        'ab_w_in': nrm((N_EVEN, D_MODEL, AB_IN), D_MODEL ** -0.5),
        'ab_conv_w': nrm((N_EVEN, SSM_CONV, SSM_CONV_CH), SSM_CONV ** -0.5),
        'ab_conv_b': nrm((N_EVEN, SSM_CONV_CH), 0.02),
        'ab_dt_bias': ab_dt_bias,
        'ab_a_log': ab_a_log,
        'ab_d_skip': 1.0 + nrm((N_EVEN, SSM_HEADS), 0.1),
        'ab_gate_norm': gain((N_EVEN, SSM_INNER)),
        'ab_w_out': nrm((N_EVEN, AB_OUT, D_MODEL), AB_OUT ** -0.5),
        'cd_w_in': nrm((N_ODD, D_MODEL, CD_IN), D_MODEL ** -0.5),
        'cd_pool_w': nrm((N_ODD, POOL_GROUPS, POOL_GROUP_DIM, POOL_GROUP_DIM), POOL_GROUP_DIM ** -0.5),
        'cd_pool_scale': gain((N_ODD, POOL_WIDTH)),
        'cd_q_norm': gain((N_ODD, DIFF_HEAD_DIM)),
        'cd_k_norm': gain((N_ODD, DIFF_HEAD_DIM)),
        'cd_lambda_q1': nrm((N_ODD, DIFF_HEAD_DIM), 0.1),
        'cd_lambda_k1': nrm((N_ODD, DIFF_HEAD_DIM), 0.1),
        'cd_lambda_q2': nrm((N_ODD, DIFF_HEAD_DIM), 0.1),
        'cd_lambda_k2': nrm((N_ODD, DIFF_HEAD_DIM), 0.1),
        'cd_sub_norm': gain((N_ODD, DIFF_V_DIM)),
        'cd_w_out': nrm((N_ODD, CD_OUT, D_MODEL), CD_OUT ** -0.5),
        'cross_norm': gain((DEPTH, D_MODEL)),
        'cross_mem_norm': gain((DEPTH, D_MODEL)),
        'cross_w_q': nrm((DEPTH, D_MODEL, D_MODEL), D_MODEL ** -0.5),
        'cross_w_kv': nrm((DEPTH, D_MODEL, 2 * D_MODEL), D_MODEL ** -0.5),
        'cross_q_norm': gain((DEPTH, CROSS_HEAD_DIM)),
        'cross_k_norm': gain((DEPTH, CROSS_HEAD_DIM)),
        'cross_w_o': nrm((DEPTH, D_MODEL, D_MODEL), D_MODEL ** -0.5),
        'ffn2_norm': gain((DEPTH, D_MODEL)),
        'ffn2_w_gate': nrm((DEPTH, D_MODEL, D_FF), D_MODEL ** -0.5),
        'ffn2_w_up': nrm((DEPTH, D_MODEL, D_FF), D_MODEL ** -0.5),
        'ffn2_w_down': nrm((DEPTH, D_FF, D_MODEL), D_FF ** -0.5),
    }


def reference(x_prompt, x_sample, mem_prompt, mem_sample,
              ffn1_norm, ffn1_w_gate, ffn1_w_up, ffn1_w_down,
              mix_norm,
              ab_w_in, ab_conv_w, ab_conv_b, ab_dt_bias, ab_a_log, ab_d_skip, ab_gate_norm, ab_w_out,
              cd_w_in, cd_pool_w, cd_pool_scale, cd_q_norm, cd_k_norm, cd_lambda_q1, cd_lambda_k1,
              cd_lambda_q2, cd_lambda_k2, cd_sub_norm, cd_w_out,
              cross_norm, cross_mem_norm, cross_w_q, cross_w_kv, cross_q_norm, cross_k_norm, cross_w_o,
              ffn2_norm, ffn2_w_gate, ffn2_w_up, ffn2_w_down):
    p = {
        'ffn1_norm': ffn1_norm, 'ffn1_w_gate': ffn1_w_gate, 'ffn1_w_up': ffn1_w_up, 'ffn1_w_down': ffn1_w_down,
        'mix_norm': mix_norm,
        'ab_w_in': ab_w_in, 'ab_conv_w': ab_conv_w, 'ab_conv_b': ab_conv_b, 'ab_dt_bias': ab_dt_bias,
        'ab_a_log': ab_a_log, 'ab_d_skip': ab_d_skip, 'ab_gate_norm': ab_gate_norm, 'ab_w_out': ab_w_out,
        'cd_w_in': cd_w_in, 'cd_pool_w': cd_pool_w, 'cd_pool_scale': cd_pool_scale, 'cd_q_norm': cd_q_norm,
        'cd_k_norm': cd_k_norm, 'cd_lambda_q1': cd_lambda_q1, 'cd_lambda_k1': cd_lambda_k1,
        'cd_lambda_q2': cd_lambda_q2, 'cd_lambda_k2': cd_lambda_k2, 'cd_sub_norm': cd_sub_norm,
        'cd_w_out': cd_w_out,
        'cross_norm': cross_norm, 'cross_mem_norm': cross_mem_norm, 'cross_w_q': cross_w_q,
        'cross_w_kv': cross_w_kv, 'cross_q_norm': cross_q_norm, 'cross_k_norm': cross_k_norm,
        'cross_w_o': cross_w_o,
        'ffn2_norm': ffn2_norm, 'ffn2_w_gate': ffn2_w_gate, 'ffn2_w_up': ffn2_w_up, 'ffn2_w_down': ffn2_w_down,
    }
    y_prompt = _trunk(x_prompt, mem_prompt, p)
    y_sample = _trunk(x_sample, mem_sample, p)
    return (y_prompt, y_sample)
```

```python
import contextlib
import math
import numpy as np
import ml_dtypes
import concourse.bass as bass
import concourse.mybir as mybir
from concourse.bass_utils import run_bass_kernel_spmd

F32 = mybir.dt.float32
BF16 = mybir.dt.bfloat16
AF = mybir.ActivationFunctionType
ALU = mybir.AluOpType

D = 1024
DFF = 2816
NMEM = 256
EPS = 1e-6
AB_IN = 6208
CD_IN = 4096
LP_FULL, LS_FULL = 4096, 8192


class Clock:
    def __init__(self, sem, name):
        self.sem = sem
        self.n = 0
        self.name = name


class Eng:
    def __init__(self, name, h, clk, inorder=False):
        self.name = name
        self.h = h
        self.clk = clk
        self.seen = {}
        self.pending = False
        self.inorder = inorder


class Res:
    __slots__ = ("name", "w", "r")

    def __init__(self, name=""):
        self.name = name
        self.w = {}
        self.r = {}


class P:
    def __init__(self):
        self.nc = bass.Bass("TRN2", target_bir_lowering=False)
        self.es = contextlib.ExitStack()
        nc = self.nc
        self.nsem = 0
        self.clocks = []
        self.clock_cache = {}
        self.pe = Eng("pe", nc.tensor, self.clock("pe"), inorder=True)
        self.act = Eng("act", nc.scalar, self.clock("act"))
        self.dve = Eng("dve", nc.vector, self.clock("dve"))
        self.pool = Eng("pool", nc.gpsimd, self.clock("pool"))
        self.sp = Eng("sp", nc.sync, self.clock("sp"))
        self.engs = [self.pe, self.act, self.dve, self.pool, self.sp]
        self.uid = 0

    def clock(self, name):
        if name in self.clock_cache:
            return self.clock_cache[name]
        self.nsem += 1
        sem = self.es.enter_context(self.nc.semaphore(f"s_{name}_{self.nsem}"))
        c = Clock(sem, name)
        self.clocks.append(c)
        self.clock_cache[name] = c
        return c

    def _wait(self, E, clk, val):
        if val <= 0:
            return
        if E.seen.get(clk, 0) >= val:
            return
        E.h.wait_ge(clk.sem, val)
        E.seen[clk] = val

    def _deps(self, E, reads, writes, pwrites):
        deps = []
        for r in reads:
            deps.extend(r.w.items())
        for w in writes:
            deps.extend(w.w.items())
            deps.extend(w.r.items())
        for w in pwrites:
            deps.extend(w.r.items())
        for clk, val in deps:
            if clk is E.clk and E.inorder:
                continue
            self._wait(E, clk, val)

    def op(self, E, fn, reads=(), writes=(), inc=True):
        self._deps(E, reads, writes, ())
        ins = fn()
        val = E.clk.n + 1
        if inc:
            ins.then_inc(E.clk.sem, 1)
            E.clk.n = val
            E.pending = False
        else:
            E.pending = True
        for r in reads:
            r.r[E.clk] = val
        for w in writes:
            w.w = {E.clk: val}
            w.r = {}
        return ins

    def dma(self, Q, out, in_, clk, reads=(), writes=(), pwrites=()):
        self._deps(Q, reads, writes, pwrites)
        ins = Q.h.dma_start(out=out, in_=in_)
        ins.then_inc(clk.sem, 16)
        clk.n += 16
        for r in reads:
            r.r[clk] = clk.n
        for w in writes:
            w.w = {clk: clk.n}
            w.r = {}
        for w in pwrites:
            w.w[clk] = clk.n
        return ins

    def barrier(self):
        for E in self.engs:
            if E.pending:
                self.op(E, lambda E=E: E.h.nop(), ())
        for E in self.engs:
            for clk in self.clocks:
                if clk is E.clk:
                    continue
                self._wait(E, clk, clk.n)

    def finish(self, out_res):
        for E in self.engs:
            if E.pending:
                self.op(E, lambda E=E: E.h.nop(), ())
        for r in out_res:
            for clk, val in r.w.items():
                self._wait(self.sp, clk, val)


def col_layout():
    lay = {}
    n = 0

    def add(name, w):
        nonlocal n
        lay[name] = (n, w)
        n += w

    for l in range(2):
        for nm in ("ffn1_norm", "mix_norm", "cross_norm", "cross_mem_norm", "ffn2_norm"):
            add(f"{nm}{l}", 8)
        add(f"cross_q_norm{l}", 2)
        add(f"cross_k_norm{l}", 2)
    add("conv_w", 120)
    add("conv_b", 24)
    add("dt_bias", 1)
    add("a_log", 1)
    add("d_skip", 16)
    add("gate_norm", 16)
    add("pool_scale", 8)
    add("cd_q_norm", 1)
    add("cd_k_norm", 1)
    add("sub_norm_row", 128)
    add("sub_norm_col", 1)
    add("lam", 4)
    return lay, n


COLS, NCOL = col_layout()


def pack_cols(inp):
    a = np.zeros((128, NCOL), np.float32)

    def put(name, arr2d):
        o, w = COLS[name]
        assert arr2d.shape == (128, w), (name, arr2d.shape, w)
        a[:, o:o + w] = arr2d

    def chunked(v):
        return np.ascontiguousarray(v.reshape(-1, 128).T)

    for l in range(2):
        for nm in ("ffn1_norm", "mix_norm", "cross_norm", "cross_mem_norm", "ffn2_norm"):
            put(f"{nm}{l}", chunked(inp[nm][l]))
        put(f"cross_q_norm{l}", chunked(inp["cross_q_norm"][l]))
        put(f"cross_k_norm{l}", chunked(inp["cross_k_norm"][l]))
    cw = inp["ab_conv_w"][0]
    put("conv_w", np.concatenate([chunked(cw[k]) for k in range(5)], axis=1))
    put("conv_b", chunked(inp["ab_conv_b"][0]))
    c = np.zeros((128, 1), np.float32)
    c[:64, 0] = inp["ab_dt_bias"][0].reshape(64)
    put("dt_bias", c)
    c = np.zeros((128, 1), np.float32)
    c[:64, 0] = inp["ab_a_log"][0].reshape(64)
    put("a_log", c)
    put("d_skip", chunked(np.repeat(inp["ab_d_skip"][0], 64)))
    put("gate_norm", chunked(inp["ab_gate_norm"][0]))
    put("pool_scale", chunked(inp["cd_pool_scale"][0]))
    put("cd_q_norm", np.tile(inp["cd_q_norm"][0], 2).reshape(128, 1))
    put("cd_k_norm", np.tile(inp["cd_k_norm"][0], 2).reshape(128, 1))
    put("sub_norm_row", np.tile(inp["cd_sub_norm"][0][None, :], (128, 1)))
    put("sub_norm_col", inp["cd_sub_norm"][0].reshape(128, 1))
    lam = np.zeros((128, 4), np.float32)
    for i, nm in enumerate(("cd_lambda_q1", "cd_lambda_k1", "cd_lambda_q2", "cd_lambda_k2")):
        lam[:64, i] = inp[nm][0]
    put("lam", lam)
    return a


ROPE_THETA = 500000.0
LAMBDA_INIT1 = 0.8 - 0.6 * math.exp(-0.3 * 1)
NCONST = 896


def make_consts(Lp, Ls):
    Lmax = max(Lp, Ls)
    c = np.zeros((128, NCONST), np.float32)
    c[:, 0:128] = np.eye(128, dtype=np.float32)
    bd = np.zeros((128, 128), np.float32); bd[:64, :64] = 1; bd[64:, 64:] = 1
    c[:, 128:256] = bd
    perm = np.zeros((128, 128), np.float32)
    for m in range(128):
        r = m % 64
        if r < 8:
            perm[m + 8, m] = 1
        elif r < 16:
            perm[m - 8, m] = 1
    c[:, 256:384] = perm
    inv = (np.float32(ROPE_THETA) ** (-np.arange(0, 16, 2, dtype=np.float32) / np.float32(16))).astype(np.float32)
    ang = np.arange(Lmax, dtype=np.float32)[:, None] * inv[None, :]
    cosf = np.ones((128, Lmax), np.float32); sinf = np.zeros((128, Lmax), np.float32)
    for m in range(128):
        r = m % 64
        if r < 16:
            cosf[m] = np.cos(ang[:, r % 8])
            sinf[m] = (-1.0 if r < 8 else 1.0) * np.sin(ang[:, r % 8])
    jj = np.arange(128)[:, None]; ll = np.arange(128)[None, :]
    c[:, 384:512] = (jj <= ll); c[:, 512:640] = (jj >= ll); c[:, 640:768] = (jj > ll); c[:, 768:896] = (jj < ll)
    out = {"consts": c, "rope_cos": cosf, "rope_sin": sinf}
    bf = ml_dtypes.bfloat16
    cin = np.arange(256)[:, None]; cout = np.arange(256)[None, :]
    a = 2 * np.pi * ((cin * cout) % 256) / 256.0
    dft = np.concatenate([np.cos(a), np.sin(a)], 1)
    out["dftc"] = np.ascontiguousarray(dft.reshape(2, 128, 512).transpose(1, 0, 2)).astype(bf)
    for key, L in (("p", Lp), ("s", Ls)):
        N2 = L // 128
        l = np.arange(L, dtype=np.int64)[:, None]; k1 = np.arange(128, dtype=np.int64)[None, :]
        a = 2 * np.pi * ((l * k1) % L) / float(L)
        out["tab_" + key] = np.concatenate([-np.sin(a), np.cos(a), np.sin(a)], 1).astype(bf)
        l2 = np.arange(N2)[:, None]; k2 = np.arange(N2)[None, :]
        a = 2 * np.pi * ((l2 * k2) % N2) / float(N2)
        out["w2_" + key] = np.concatenate([np.cos(a), -np.sin(a)], 1).astype(bf)
    for key, L in (("p", Lp), ("s", Ls)):
        ic = np.zeros((4, L), np.float32)
        pos = np.arange(L)
        for g, w in enumerate((2, 4, 8, 16)):
            lo = np.clip(pos - w // 2, 0, L - 1); hi = np.clip(pos + w // 2 - 1, 0, L - 1)
            ic[g] = 1.0 / (hi - lo + 1).astype(np.float32)
        out["invcnt_" + key] = ic
    return out


WEIGHTS = [
    ("ffn1_g0", "ffn1_w_gate", 0, D, DFF), ("ffn1_u0", "ffn1_w_up", 0, D, DFF), ("ffn1_d0", "ffn1_w_down", 0, DFF, D),
    ("ab_in", "ab_w_in", 0, D, AB_IN), ("ab_out", "ab_w_out", 0, 3072, D),
    ("cq0", "cross_w_q", 0, D, D), ("ckv0", "cross_w_kv", 0, D, 2 * D), ("co0", "cross_w_o", 0, D, D),
    ("ffn2_g0", "ffn2_w_gate", 0, D, DFF), ("ffn2_u0", "ffn2_w_up", 0, D, DFF), ("ffn2_d0", "ffn2_w_down", 0, DFF, D),
    ("ffn1_g1", "ffn1_w_gate", 1, D, DFF), ("ffn1_u1", "ffn1_w_up", 1, D, DFF), ("ffn1_d1", "ffn1_w_down", 1, DFF, D),
    ("cd_in", "cd_w_in", 0, D, CD_IN), ("cd_out", "cd_w_out", 0, 2048, D),
    ("pool_w", "cd_pool_w", 0, 1024, 256),
    ("cq1", "cross_w_q", 1, D, D), ("ckv1", "cross_w_kv", 1, D, 2 * D), ("co1", "cross_w_o", 1, D, D),
    ("ffn2_g1", "ffn2_w_gate", 1, D, DFF), ("ffn2_u1", "ffn2_w_up", 1, D, DFF), ("ffn2_d1", "ffn2_w_down", 1, DFF, D),
]


def panel_w(nm):
    K, N = [(w[3], w[4]) for w in WEIGHTS if w[0] == nm][0]
    KC = K // 128
    if nm.startswith("ffn") and ("_g" in nm or "_u" in nm):
        mw = 256
    else:
        mw = min(512, max(128, (4096 // KC) // 128 * 128))
    return min(mw, N)


class Builder(P):
    def __init__(self, Lp, Ls, flags=None):
        super().__init__()
        self.Lp, self.Ls = Lp, Ls
        self.Lmax = max(Lp, Ls)
        self.flags = flags or {}
        self.T = 512
        nc = self.nc
        self.din = {}

        def ext(name, shape, dt=F32):
            self.din[name] = nc.dram_tensor(name, list(shape), dt, kind="ExternalInput").ap()
            return self.din[name]

        ext("xT_p", [D, Lp]); ext("xT_s", [D, Ls])
        ext("memT_p", [D, NMEM]); ext("memT_s", [D, NMEM])
        ext("cols", [128, NCOL])
        ext("consts", [128, NCONST])
        ext("rope_cos", [128, self.Lmax]); ext("rope_sin", [128, self.Lmax])
        ext("invcnt_p", [4, Lp]); ext("invcnt_s", [4, Ls])
        ext("dftc", [128, 2, 512], BF16)
        ext("tab_p", [Lp, 384], BF16); ext("tab_s", [Ls, 384], BF16)
        ext("w2_p", [Lp // 128, 2 * (Lp // 128)], BF16); ext("w2_s", [Ls // 128, 2 * (Ls // 128)], BF16)
        for nm, src, l, K, N in WEIGHTS:
            ext("w_" + nm, [K, N])
        self.yT = {"p": nc.dram_tensor("yT_p", [D, Lp], F32, kind="ExternalOutput").ap(),
                   "s": nc.dram_tensor("yT_s", [D, Ls], F32, kind="ExternalOutput").ap()}
        self.y_res = {"p": Res("yT_p"), "s": Res("yT_s")}
        self.ws = {}
        self.ws_res = {}
        for nm, src, l, K, N in WEIGHTS:
            self.ws[nm] = nc.dram_tensor("ws_" + nm, [128, (K // 128) * N], BF16, kind="Internal")
            self.ws_res[nm] = Res("ws_" + nm)
        L = self.Lmax
        self.XR = nc.dram_tensor("XR", [D, L], F32, kind="Internal").ap()
        self.XR_res = Res("XR")
        self.MO = nc.dram_tensor("MO", [3072, L], BF16, kind="Internal").ap()
        self.MO_res = Res("MO")
        self.UP = nc.dram_tensor("UP", [D, L], F32, kind="Internal").ap(); self.UP_res = Res("UP")
        self.QT = nc.dram_tensor("QT", [D, L], BF16, kind="Internal").ap(); self.QT_res = Res("QT")
        self.KTd = nc.dram_tensor("KTd", [D, L], BF16, kind="Internal").ap(); self.KTd_res = Res("KTd")
        self.Vd = nc.dram_tensor("Vd", [L, D], BF16, kind="Internal").ap(); self.Vd_res = Res("Vd")
        self.Zd = nc.dram_tensor("Zd", [L, 2048], BF16, kind="Internal").ap(); self.Zd_res = Res("Zd")
        self.ZS = nc.dram_tensor("ZS", [2048, L], BF16, kind="Internal").ap(); self.ZS_res = Res("ZS")
        self.XBC = nc.dram_tensor("XBC", [3072, L], BF16, kind="Internal").ap(); self.XBC_res = Res("XBC")
        self.DT = nc.dram_tensor("DT", [64, L], F32, kind="Internal").ap(); self.DT_res = Res("DT")
        self.DTA = nc.dram_tensor("DTA", [64, L], F32, kind="Internal").ap(); self.DTA_res = Res("DTA")
        self.X2D = nc.dram_tensor("X2D", [L // 128, 8, 2, 128, 128], BF16, kind="Internal").ap(); self.X2D_res = Res("X2D")
        self.YFW = nc.dram_tensor("YFW", [L, 2048], F32, kind="Internal").ap(); self.YFW_res = Res("YFW")

    def sb(self, stack, name, shape, dt):
        self.uid += 1
        return stack.enter_context(self.nc.sbuf_tensor(f"{name}_{self.uid}", list(shape), dt))

    def col(self, name, j=0, rows=128):
        o, w = COLS[name]
        return self.colsb[0:rows, o + j:o + j + 1]

    def bank(self):
        b = self.bank_i
        self.bank_i = (self.bank_i + 1) % 8
        return self.banks[b], self.bank_res[b]

    def build(self):
        nc = self.nc
        with self.es:
            top = self.es
            self.banks = [top.enter_context(nc.psum_tensor(f"bank{i}", [128, 512], F32)) for i in range(8)]
            self.bank_res = [Res(f"bank{i}") for i in range(8)]
            self.bank_i = 0
            self.colsb = self.sb(top, "colsb", [128, NCOL], F32)
            self.cols_res = Res("cols")
            self.dma(self.sp, self.colsb[:], self.din["cols"][:, :], self.clock("cols"), writes=[self.cols_res])
            self.ones_bf = self.sb(top, "ones_bf", [128, 128], BF16)
            self.const_res = Res("const")
            self.op(self.dve, lambda: nc.vector.memset(self.ones_bf[:], 1.0), writes=[self.const_res])
            self.cst = self.sb(top, "cst", [128, NCONST], F32)
            self.dma(self.sp, self.cst[:], self.din["consts"][:, :], self.clock("cst"), writes=[self.const_res])
            self.ident_f = self.cst[:, 0:128]
            self.cst_bf = self.sb(top, "cst_bf", [128, NCONST], BF16)
            self.op(self.dve, lambda: nc.vector.tensor_copy(out=self.cst_bf[:], in_=self.cst[:]), reads=[self.const_res], writes=[self.const_res])
            self.ident_bf = self.cst_bf[:, 0:128]
            self.bd64_bf = self.cst_bf[:, 128:256]
            self.perm_bf = self.cst_bf[:, 256:384]
            self.ones_f = self.sb(top, "ones_f", [128, 128], F32)
            self.op(self.dve, lambda: nc.vector.memset(self.ones_f[:], 1.0), writes=[self.const_res])
            self.one_col = self.sb(top, "onec", [128, 1], F32)
            self.op(self.dve, lambda: nc.vector.memset(self.one_col[:], 1.0), writes=[self.const_res])
            self.eps_col = self.sb(top, "eps", [128, 1], F32)
            self.op(self.dve, lambda: nc.vector.memset(self.eps_col[:], EPS), writes=[self.const_res])
            self.tri = [self.cst[:, 384:512], self.cst[:, 512:640]]
            self.strict = [self.cst[:, 640:768], self.cst[:, 768:896]]
            self.negm = self.sb(top, "negm", [128, 2, 4, 128], BF16)
            for d_ in range(2):
                self.op(self.dve, lambda d_=d_: nc.vector.tensor_scalar(out=self.negm[:, d_, :, :], in0=self.cst[:, 640 + 128 * d_:768 + 128 * d_].unsqueeze(1).broadcast_to([128, 4, 128]),
                                                                      scalar1=-30000.0, scalar2=None, op0=ALU.mult),
                        reads=[self.const_res], writes=[self.const_res])
            self.dftc = self.sb(top, "dftc", [128, 2, 512], BF16)
            self.dma(self.sp, self.dftc[:], self.din["dftc"][:, :, :], self.clock("dftc"), writes=[self.const_res])
            self.init_small()
            self.prep_weights()
            for key, L in (("p", self.Lp), ("s", self.Ls)):
                self.run_seq(key, L)
            self.finish([self.y_res["p"], self.y_res["s"]])
        return nc

    def prep_weights(self):
        nc = self.nc
        with contextlib.ExitStack() as st:
            NB = 2
            stg = [self.sb(st, "wstg", [128, AB_IN], F32) for _ in range(NB)]
            stb = [self.sb(st, "wstb", [128, AB_IN], BF16) for _ in range(NB)]
            stg_res = [Res("wstg") for _ in range(NB)]
            stb_res = [Res("wstb") for _ in range(NB)]
            ldclk = [self.clock(f"wld{i}") for i in range(NB)]
            stclk = [self.clock(f"wst{i}") for i in range(NB)]
            i = 0
            casters = [self.dve, self.act, self.pool]
            for nm, src, l, K, N in WEIGHTS:
                w = self.din["w_" + nm]
                for kc in range(K // 128):
                    b = i % NB
                    self.dma(self.sp, stg[b][:, 0:N], w[kc * 128:(kc + 1) * 128, :], ldclk[b], writes=[stg_res[b]])
                    E = casters[i % 3]
                    if E is self.act:
                        fn = lambda b=b, N=N: nc.scalar.copy(out=stb[b][:, 0:N], in_=stg[b][:, 0:N])
                    elif E is self.dve:
                        fn = lambda b=b, N=N: nc.vector.tensor_copy(out=stb[b][:, 0:N], in_=stg[b][:, 0:N])
                    else:
                        fn = lambda b=b, N=N: nc.gpsimd.tensor_copy(out=stb[b][:, 0:N], in_=stg[b][:, 0:N])
                    self.op(E, fn, reads=[stg_res[b]], writes=[stb_res[b]])
                    KC = K // 128
                    MW = panel_w(nm)
                    nfull = N // MW
                    dst = bass.AP(self.ws[nm], kc * MW, [[KC * N, 128], [KC * MW, nfull], [1, MW]])
                    self.dma(self.sp, dst, stb[b][:, 0:nfull * MW].rearrange("p (a m) -> p a m", m=MW), stclk[b], reads=[stb_res[b]],
                             pwrites=[self.ws_res[nm]])
                    rem = N - nfull * MW
                    if rem:
                        dst = bass.AP(self.ws[nm], KC * nfull * MW + kc * rem, [[KC * N, 128], [1, rem]])
                        self.dma(self.sp, dst, stb[b][:, nfull * MW:N], stclk[b], reads=[stb_res[b]], pwrites=[self.ws_res[nm]])
                    i += 1
            self.barrier()

    def wstream(self, nm, c0, c1, mw=None):
        return WStream(self, nm, c0, c1, mw)

    def rms_fm(self, src, nch, T, gain, out, src_res, out_res, group=None, scale=1.0, lhsT=None, n_norm=None):
        nc = self.nc
        group = group or nch
        lhsT = lhsT if lhsT is not None else self.ones_bf
        n_norm = n_norm or group * 128
        for g0 in range(0, nch, group):
            bk, bres = self.bank()
            for jj in range(group):
                j = g0 + jj
                self.op(self.act, lambda j=j, jj=jj: nc.scalar.activation(out=self.sq[:, jj % 2, 0:T], in_=src(j), func=AF.Square),
                        reads=src_res, writes=[self.sq_res[jj % 2]])
                self.op(self.pe, lambda jj=jj: nc.tensor.matmul(bk[:, 0:T], lhsT[:], self.sq[:, jj % 2, 0:T],
                                                                 start=(jj == 0), stop=(jj == group - 1)),
                        reads=[self.sq_res[jj % 2], self.const_res], writes=[bres], inc=True)
            self.op(self.act, lambda: nc.scalar.activation(out=bk[:, 0:T], in_=bk[:, 0:T], func=AF.Sqrt,
                                                           bias=self.eps_col[:, 0:1], scale=1.0 / n_norm),
                    reads=[bres, self.const_res], writes=[bres])
            self.op(self.dve, lambda: nc.vector.reciprocal(out=bk[:, 0:T], in_=bk[:, 0:T]),
                    reads=[bres], writes=[bres])
            for jj in range(group):
                j = g0 + jj
                if scale == 1.0:
                    fn = lambda j=j: nc.vector.scalar_tensor_tensor(out=out(j), in0=src(j), scalar=gain(j), in1=bk[:, 0:T],
                                                                    op0=ALU.mult, op1=ALU.mult)
                    self.op(self.dve, fn, reads=list(src_res) + [bres, self.cols_res], writes=out_res)
                else:
                    self.op(self.dve, lambda j=j: nc.vector.tensor_scalar(out=self.tmpn[:, 0:T], in0=src(j), scalar1=gain(j), scalar2=float(scale),
                                                                      op0=ALU.mult, op1=ALU.mult),
                            reads=list(src_res) + [self.cols_res], writes=[self.tmpn_res])
                    self.op(self.dve, lambda j=j: nc.vector.tensor_tensor(out=out(j), in0=self.tmpn[:, 0:T], in1=bk[:, 0:T], op=ALU.mult),
                            reads=[self.tmpn_res, bres], writes=out_res)

    def linear_fm(self, ws, rhs, rhs_res, T, evac, KC=None):
        nc = self.nc
        m = 0
        for pi in range(ws.npanels):
            wt, wres, mw = ws.get(pi)
            kcs = list(KC) if KC is not None else list(range(ws.KC))
            for mm in range(0, mw, 128):
                mwid = min(128, mw - mm)
                bk, bres = self.bank()
                for kc in kcs:
                    self.op(self.pe, lambda kc=kc, mm=mm, mwid=mwid: nc.tensor.matmul(bk[0:mwid, 0:T], wt[:, kc, mm:mm + mwid], rhs(kc),
                                                                                    start=(kc == kcs[0]), stop=(kc == kcs[-1])),
                            reads=[wres] + list(rhs_res), writes=[bres], inc=(kc == kcs[-1]))
                evac(m, bk, bres, mwid)
                m += 1
        ws.close()

    def ffn(self, l, which, T):
        nc = self.nc
        x, xres = self.x, self.x_res
        gname = f"ffn{which}_norm{l}"
        self.rms_fm(lambda j: x[:, j, 0:T], 8, T, lambda j: self.col(gname, j), lambda j: self.xn[:, j, 0:T], [xres], [self.xn_res])
        if self.flags.get("dbg") == 1:
            for j in range(8):
                self.op(self.dve, lambda j=j: nc.vector.tensor_copy(out=x[:, j, 0:T], in_=self.xn[:, j, 0:T]), reads=[self.xn_res], writes=[xres])
            return
        wg = self.wstream(f"ffn{which}_g{l}", 0, DFF, 256)
        wu = self.wstream(f"ffn{which}_u{l}", 0, DFF, 256)
        wd = self.wstream(f"ffn{which}_d{l}", 0, D)
        wg.prime(); wu.prime()
        m = 0
        for pi in range(wg.npanels):
            gt, gres, mw = wg.get(pi)
            ut, ures, _ = wu.get(pi)
            for mm in range(0, mw, 128):
                bg, bgres = self.bank()
                bu, bures = self.bank()
                for kc in range(8):
                    self.op(self.pe, lambda kc=kc, mm=mm: nc.tensor.matmul(bg[:, 0:T], gt[:, kc, mm:mm + 128], self.xn[:, kc, 0:T],
                                                                         start=(kc == 0), stop=(kc == 7)),
                            reads=[gres, self.xn_res], writes=[bgres], inc=(kc == 7))
                for kc in range(8):
                    self.op(self.pe, lambda kc=kc, mm=mm: nc.tensor.matmul(bu[:, 0:T], ut[:, kc, mm:mm + 128], self.xn[:, kc, 0:T],
                                                                         start=(kc == 0), stop=(kc == 7)),
                            reads=[ures, self.xn_res], writes=[bures], inc=(kc == 7))
                sb_i = m % 2
                self.op(self.act, lambda sb_i=sb_i: nc.scalar.activation(out=self.sg[:, sb_i, 0:T], in_=bg[:, 0:T], func=AF.Silu),
                        reads=[bgres], writes=[self.sg_res[sb_i]])
                if self.flags.get("dbg4"):
                    self.op(self.dve, lambda m=m, sb_i=sb_i: nc.vector.tensor_copy(out=self.h[:, m, 0:T], in_=bu[:, 0:T]),
                            reads=[bures, self.sg_res[sb_i]], writes=[self.h_res])
                else:
                    self.op(self.dve, lambda m=m, sb_i=sb_i: nc.vector.tensor_tensor(out=self.h[:, m, 0:T], in0=bu[:, 0:T], in1=self.sg[:, sb_i, 0:T], op=ALU.mult),
                            reads=[bures, self.sg_res[sb_i]], writes=[self.h_res])
                m += 1
            if pi == wg.npanels - 1:
                wg.close(); wu.close()
                wd.prime()
        if self.flags.get("dbg") == 2:
            for j in range(8):
                self.op(self.dve, lambda j=j: nc.vector.tensor_copy(out=x[:, j, 0:T], in_=self.h[:, j, 0:T]), reads=[self.h_res], writes=[xres])
            return

        def evac(j, bk, bres, mwid):
            self.op(self.dve, lambda: nc.vector.scalar_tensor_tensor(out=x[:, j, 0:T], in0=bk[:, 0:T], scalar=0.5, in1=x[:, j, 0:T],
                                                                     op0=ALU.mult, op1=ALU.add),
                    reads=[bres, xres], writes=[xres])
        self.linear_fm(wd, lambda kc: self.h[:, kc, 0:T], [self.h_res], T, evac)

    def cross_prep(self, key, l, st):
        nc = self.nc
        M = NMEM
        memf = self.sb(st, "memf", [128, 8, M], F32)
        memf_res = Res("memf")
        kf = self.sb(st, "kf", [128, 8, M], F32)
        kf_res = Res("kf")
        mn = self.sb(st, "mn", [128, 8, M], BF16)
        mn_res = Res("mn")
        self.dma(self.sp, memf[:], self.din["memT_" + key].rearrange("(j p) m -> p j m", p=128), self.clock("memld"), writes=[memf_res])
        self.rms_fm(lambda j: memf[:, j, :], 8, M, lambda j: self.col(f"cross_mem_norm{l}", j), lambda j: mn[:, j, :], [memf_res], [mn_res])
        wk = self.wstream(f"ckv{l}", 0, D)

        def evk(j, bk, bres, mwid):
            self.op(self.act, lambda: nc.scalar.copy(out=kf[:, j, :], in_=bk[:, 0:M]), reads=[bres], writes=[kf_res])
        self.linear_fm(wk, lambda kc: mn[:, kc, :], [mn_res], M, evk)
        self.rms_fm(lambda j: kf[:, j, :], 8, M, lambda j: self.col(f"cross_k_norm{l}", j % 2), lambda j: self.KT[:, j, :], [kf_res], [self.KT_res], group=2)
        wv = self.wstream(f"ckv{l}", D, 2 * D)
        for pi in range(wv.npanels):
            vt, vres, mw = wv.get(pi)
            for mc in range(2):
                bk, bres = self.bank()
                for kc in range(8):
                    self.op(self.pe, lambda kc=kc, mc=mc: nc.tensor.matmul(bk[:, 0:mw], mn[:, kc, mc * 128:(mc + 1) * 128], vt[:, kc, 0:mw],
                                                                         start=(kc == 0), stop=(kc == 7)),
                            reads=[vres, mn_res], writes=[bres], inc=(kc == 7))
                self.op(self.act, lambda mc=mc, pi=pi: nc.scalar.copy(out=self.Vm[:, mc, pi * 512:pi * 512 + mw], in_=bk[:, 0:mw]),
                        reads=[bres], writes=[self.Vm_res])
        wv.close()

    def cross(self, l, T):
        nc = self.nc
        x, xres = self.x, self.x_res
        self.rms_fm(lambda j: x[:, j, 0:T], 8, T, lambda j: self.col(f"cross_norm{l}", j), lambda j: self.xn[:, j, 0:T], [xres], [self.xn_res])
        wq = self.wstream(f"cq{l}", 0, D)
        qf, qf_res = self.big, self.big_res

        def evq(j, bk, bres, mwid):
            self.op(self.act, lambda: nc.scalar.copy(out=qf[:, j, 0:T], in_=bk[:, 0:T]), reads=[bres], writes=[qf_res])
        self.linear_fm(wq, lambda kc: self.xn[:, kc, 0:T], [self.xn_res], T, evq)
        qn, qn_res = self.h, self.h_res
        self.rms_fm(lambda j: qf[:, j, 0:T], 8, T, lambda j: self.col(f"cross_q_norm{l}", j % 2), lambda j: qn[:, j, 0:T], [qf_res], [qn_res],
                    group=2, scale=1.0 / 16.0)
        wo = self.wstream(f"co{l}", 0, D)
        wo.prime()
        oT, oT_res = self.xn, self.xn_res
        for hd in range(4):
            ebuf = []
            for mc in range(2):
                bk, bres = self.bank()
                for dc in range(2):
                    self.op(self.pe, lambda dc=dc, mc=mc: nc.tensor.matmul(bk[:, 0:T], self.KT[:, 2 * hd + dc, mc * 128:(mc + 1) * 128], qn[:, 2 * hd + dc, 0:T],
                                                                         start=(dc == 0), stop=(dc == 1)),
                            reads=[self.KT_res, qn_res], writes=[bres], inc=(dc == 1))
                self.op(self.act, lambda mc=mc: nc.scalar.activation(out=self.ex[:, mc, 0:T], in_=bk[:, 0:T], func=AF.Exp),
                        reads=[bres], writes=[self.ex_res[mc]])
            bd, bdres = self.bank()
            for mc in range(2):
                self.op(self.pe, lambda mc=mc: nc.tensor.matmul(bd[:, 0:T], self.ones_bf[:], self.ex[:, mc, 0:T], start=(mc == 0), stop=(mc == 1)),
                        reads=[self.ex_res[mc], self.const_res], writes=[bdres], inc=(mc == 1))
            self.op(self.dve, lambda: nc.vector.reciprocal(out=self.rstd[:, 0:T], in_=bd[:, 0:T]), reads=[bdres], writes=[self.rstd_res])
            for ec in range(2):
                bo, bores = self.bank()
                for mc in range(2):
                    self.op(self.pe, lambda mc=mc, ec=ec: nc.tensor.matmul(bo[:, 0:T], self.Vm[:, mc, hd * 256 + ec * 128:hd * 256 + (ec + 1) * 128], self.ex[:, mc, 0:T],
                                                                         start=(mc == 0), stop=(mc == 1)),
                            reads=[self.ex_res[mc], self.Vm_res], writes=[bores], inc=(mc == 1))
                self.op(self.dve, lambda ec=ec: nc.vector.tensor_tensor(out=oT[:, 2 * hd + ec, 0:T], in0=bo[:, 0:T], in1=self.rstd[:, 0:T], op=ALU.mult),
                        reads=[bores, self.rstd_res], writes=[oT_res])

        def evo(j, bk, bres, mwid):
            self.op(self.dve, lambda: nc.vector.tensor_tensor(out=x[:, j, 0:T], in0=bk[:, 0:T], in1=x[:, j, 0:T], op=ALU.add),
                    reads=[bres, xres], writes=[xres])
        self.linear_fm(wo, lambda kc: oT[:, kc, 0:T], [oT_res], T, evo)

    def mixer_out(self, wname, KC, t0, T):
        nc = self.nc
        x, xres = self.x, self.x_res
        ws = self.wstream(wname, 0, D)
        ws.prime()
        mo, mo_res = self.h, self.h_res
        self.dma(self.sp, mo[:, 0:KC, 0:T], self.MO[0:KC * 128, t0:t0 + T].rearrange("(j p) t -> p j t", p=128), self.clock("mold"),
                 reads=[self.MO_res], writes=[mo_res])

        def ev(j, bk, bres, mwid):
            self.op(self.dve, lambda: nc.vector.tensor_tensor(out=x[:, j, 0:T], in0=bk[:, 0:T], in1=x[:, j, 0:T], op=ALU.add),
                    reads=[bres, xres], writes=[xres])
        kcs = None
        if self.flags.get("noattn"): kcs = range(0, 8)
        if self.flags.get("nopool"): kcs = range(8, 16)
        if wname == "ab_out":
            kcs = None
            if self.flags.get("nossd"): kcs = range(0, 8)
            if self.flags.get("nofnet"): kcs = range(8, 24)
        self.linear_fm(ws, lambda kc: mo[:, kc, 0:T], [mo_res], T, ev, KC=kcs)

    def init_small(self):
        nc = self.nc
        top = self.es
        self.neglam = self.sb(top, "neglam", [128, 1], F32)
        self.small_res = Res("small")
        prod = self.sb(top, "lamprod", [128, 2], F32)
        o, _ = COLS["lam"]
        cb = self.colsb
        self.op(self.dve, lambda: nc.vector.tensor_tensor(out=prod[:, 0:1], in0=cb[:, o:o + 1], in1=cb[:, o + 1:o + 2], op=ALU.mult),
                reads=[self.cols_res], writes=[self.small_res])
        self.op(self.dve, lambda: nc.vector.tensor_tensor(out=prod[:, 1:2], in0=cb[:, o + 2:o + 3], in1=cb[:, o + 3:o + 4], op=ALU.mult),
                reads=[self.cols_res, self.small_res], writes=[self.small_res])
        bk, bres = self.bank()
        self.op(self.pe, lambda: nc.tensor.matmul(bk[:, 0:2], self.ones_f[:], prod[:, 0:2], start=True, stop=True),
                reads=[self.small_res, self.const_res], writes=[bres])
        ex = self.sb(top, "lamex", [128, 2], F32)
        self.op(self.act, lambda: nc.scalar.activation(out=ex[:, 0:2], in_=bk[:, 0:2], func=AF.Exp), reads=[bres], writes=[self.small_res])
        self.op(self.dve, lambda: nc.vector.scalar_tensor_tensor(out=self.neglam[:, 0:1], in0=ex[:, 1:2], scalar=-LAMBDA_INIT1, in1=ex[:, 0:1],
                                                                 op0=ALU.add, op1=ALU.subtract),
                reads=[self.small_res], writes=[self.small_res])
        self.acol = self.sb(top, "acol", [128, 1], F32)
        oa, _ = COLS["a_log"]
        self.op(self.act, lambda: nc.scalar.activation(out=self.acol[:, 0:1], in_=cb[:, oa:oa + 1], func=AF.Exp), reads=[self.cols_res], writes=[self.small_res])
        self.op(self.dve, lambda: nc.vector.tensor_scalar(out=self.acol[:, 0:1], in0=self.acol[:, 0:1], scalar1=-1.0, scalar2=None, op0=ALU.mult),
                reads=[self.small_res], writes=[self.small_res])
        self.subrow = self.sb(top, "subrow", [128, 128], F32)
        o2, _ = COLS["sub_norm_row"]
        self.op(self.dve, lambda: nc.vector.tensor_scalar(out=self.subrow[:], in0=cb[:, o2:o2 + 128], scalar1=1.0 - LAMBDA_INIT1, scalar2=None, op0=ALU.mult),
                reads=[self.cols_res], writes=[self.small_res])

    def mixer_in_ab(self, key, t0, T, first, last):
        nc = self.nc
        x, xres = self.x, self.x_res
        self.rms_fm(lambda j: x[:, j, 0:T], 8, T, lambda j: self.col("mix_norm0", j), lambda j: self.xn[:, j, 0:T], [xres], [self.xn_res])
        rhs = lambda kc: self.xn[:, kc, 0:T]
        h, h_res = self.h, self.h_res
        nb = T // 128
        def ev_uf(j, bk, bres, mwid):
            self.op(self.act, lambda: nc.scalar.copy(out=h[:, j, 0:T], in_=bk[:, 0:T]), reads=[bres], writes=[h_res])
        self.linear_fm(self.wstream("ab_in", 0, 1024), rhs, [self.xn_res], T, ev_uf)
        zview = bass.AP(h, 8 * T, [[24 * T, 128], [2048, nb], [1, 2048]])
        for tb in range(nb):
            for hd in range(4):
                bk, bres = self.bank()
                for k2 in range(2):
                    self.op(self.pe, lambda k2=k2, tb=tb, hd=hd: nc.tensor.matmul(bk[:, 0:512], h[:, 2 * hd + k2, tb * 128:(tb + 1) * 128], self.dftc[:, k2, :],
                                                                                start=(k2 == 0), stop=(k2 == 1)),
                            reads=[h_res, self.const_res], writes=[bres], inc=(k2 == 1))
                self.op(self.act, lambda tb=tb, hd=hd: nc.scalar.copy(out=zview[:, tb, hd * 512:(hd + 1) * 512], in_=bk[:, 0:512]), reads=[bres], writes=[h_res])
        self.dma(self.sp, self.Zd[t0:t0 + T, :].rearrange("(tb p) f -> p tb f", p=128), zview, self.stclk[0], reads=[h_res], pwrites=[self.Zd_res])
        def ev_z(j, bk, bres, mwid):
            self.op(self.act, lambda: nc.scalar.activation(out=h[:, j, 0:T], in_=bk[:, 0:T], func=AF.Silu), reads=[bres], writes=[h_res])
        self.linear_fm(self.wstream("ab_in", 1024, 3072), rhs, [self.xn_res], T, ev_z)
        self.dma(self.sp, self.ZS[:, t0:t0 + T].rearrange("(j p) t -> p j t", p=128), h[:, 0:16, 0:T], self.stclk[1], reads=[h_res], pwrites=[self.ZS_res])
        cb, cb_res = self.cb, self.cb_res
        if first:
            self.op(self.dve, lambda: nc.vector.memset(cb[:], 0.0), writes=cb_res)
        a_col = 2 if first else 0
        b_col = T + 2 if last else T
        ow, _ = COLS["conv_w"]

        def ev_x(j, bk, bres, mwid):
            self.op(self.act, lambda: nc.scalar.copy(out=cb[:, j, 4:4 + T], in_=bk[:, 0:T]), reads=[bres], writes=[cb_res[j]])
            acc = self.cacc[j % 2]; acc_res = self.cacc_res[j % 2]
            self.op(self.dve, lambda: nc.vector.tensor_scalar(out=acc[:, 0:T + 2], in0=cb[:, j, 0:T + 2], scalar1=self.colsb[:, ow + j:ow + j + 1], scalar2=None, op0=ALU.mult),
                    reads=[cb_res[j], self.cols_res], writes=[acc_res])
            for k in range(1, 5):
                self.op(self.dve, lambda k=k: nc.vector.scalar_tensor_tensor(out=acc[:, 0:T + 2], in0=cb[:, j, k:k + T + 2], scalar=self.colsb[:, ow + 24 * k + j:ow + 24 * k + j + 1],
                                                                           in1=acc[:, 0:T + 2], op0=ALU.mult, op1=ALU.add),
                        reads=[cb_res[j], self.cols_res, acc_res], writes=[acc_res])
            xo = self.xo; xo_res = self.xo_res
            self.op(self.act, lambda: nc.scalar.activation(out=xo[:, j % 8, 0:T + 2], in_=acc[:, 0:T + 2], func=AF.Silu, bias=self.col("conv_b", j)),
                    reads=[acc_res, self.cols_res], writes=[xo_res])
            self.op(self.dve, lambda: nc.vector.tensor_copy(out=cb[:, j, 0:4], in_=cb[:, j, T:T + 4]), reads=[cb_res[j]], writes=[cb_res[j]])
            if j % 8 == 7:
                j0 = j - 7
                self.dma(self.sp, self.XBC[j0 * 128:(j0 + 8) * 128, t0 - 2 + a_col:t0 - 2 + b_col].rearrange("(j p) t -> p j t", p=128), xo[:, :, a_col:b_col],
                         self.stclk[2], reads=[xo_res], pwrites=[self.XBC_res])
        self.linear_fm(self.wstream("ab_in", 3072, 6144), rhs, [self.xn_res], T, ev_x)
        big, big_res = self.big, self.big_res

        def ev_dt(j, bk, bres, mwid):
            R = 64
            self.op(self.dve, lambda: nc.vector.tensor_scalar(out=big[0:R, 0, 0:T], in0=bk[0:R, 0:T], scalar1=self.col("dt_bias", 0, R), scalar2=None, op0=ALU.add),
                    reads=[bres, self.cols_res], writes=[big_res])
            self.op(self.dve, lambda: nc.vector.scalar_tensor_tensor(out=big[0:R, 1, 0:T], in0=big[0:R, 0, 0:T], scalar=-1.0, in1=big[0:R, 0, 0:T], op0=ALU.mult, op1=ALU.max),
                    reads=[big_res], writes=[big_res])
            self.op(self.act, lambda: nc.scalar.activation(out=big[0:R, 2, 0:T], in_=big[0:R, 1, 0:T], func=AF.Exp, scale=-1.0), reads=[big_res], writes=[big_res])
            self.op(self.act, lambda: nc.scalar.activation(out=big[0:R, 3, 0:T], in_=big[0:R, 2, 0:T], func=AF.Ln, bias=self.one_col[0:R, 0:1]), reads=[big_res, self.const_res], writes=[big_res])
            self.op(self.dve, lambda: nc.vector.scalar_tensor_tensor(out=big[0:R, 4, 0:T], in0=big[0:R, 0, 0:T], scalar=0.0, in1=big[0:R, 3, 0:T], op0=ALU.max, op1=ALU.add),
                    reads=[big_res], writes=[big_res])
            self.op(self.dve, lambda: nc.vector.tensor_scalar(out=big[0:R, 5, 0:T], in0=big[0:R, 4, 0:T], scalar1=self.acol[0:R, 0:1], scalar2=None, op0=ALU.mult),
                    reads=[big_res, self.small_res], writes=[big_res])
        self.linear_fm(self.wstream("ab_in", 6144, 6208), rhs, [self.xn_res], T, ev_dt)
        self.dma(self.sp, self.DT[:, t0:t0 + T], big[0:64, 4, 0:T], self.stclk[3], reads=[big_res], pwrites=[self.DT_res])
        self.dma(self.sp, self.DTA[:, t0:t0 + T], big[0:64, 5, 0:T], self.stclk[3], reads=[big_res], pwrites=[self.DTA_res])

    def pass_fnet(self, key, L):
        nc = self.nc
        N2 = L // 128
        tab = self.din["tab_" + key]
        with contextlib.ExitStack() as st:
            zl = [self.sb(st, "zl", [128, 2048], BF16) for _ in range(2)]; zl_res = [Res("zl0"), Res("zl1")]
            tl = [self.sb(st, "tl", [128, 384], BF16) for _ in range(2)]; tl_res = [Res("tl0"), Res("tl1")]
            x2s = [self.sb(st, "x2s", [128, 8, 2, 128], BF16) for _ in range(2)]; x2_res = [Res("x2s0"), Res("x2s1")]
            zclk = [self.clock("zl0"), self.clock("zl1")]; x2clk = [self.clock("x2a"), self.clock("x2b")]
            zsrc = self.Zd[0:L, :].rearrange("(l1 l2) f -> l1 l2 f", l2=N2)
            tsrc = tab.rearrange("(l1 l2) f -> l1 l2 f", l2=N2)
            for l2 in range(N2):
                b = l2 % 2
                self.dma(self.sp, zl[b][:], zsrc[:, l2, :], zclk[b], reads=[self.Zd_res], writes=[zl_res[b]])
                self.dma(self.sp, tl[b][:], tsrc[:, l2, :], zclk[b], writes=[tl_res[b]])
                for cp in range(4):
                    bk, bres = self.bank()
                    for ci in range(2):
                        cj = 2 * cp + ci
                        hd, half = cj // 2, cj % 2
                        a0 = hd * 512 + half * 128
                        self.op(self.pe, lambda b=b, a0=a0, ci=ci: nc.tensor.matmul(bk[:, ci * 256:(ci + 1) * 256], zl[b][:, a0:a0 + 128], tl[b][:, 128:384], start=True, stop=False),
                                reads=[zl_res[b], tl_res[b]], writes=[bres], inc=False)
                        self.op(self.pe, lambda b=b, a0=a0, ci=ci: nc.tensor.matmul(bk[:, ci * 256:(ci + 1) * 256], zl[b][:, a0 + 256:a0 + 384], tl[b][:, 0:256], start=False, stop=True),
                                reads=[zl_res[b], tl_res[b]], writes=[bres], inc=(ci == 1))
                    E = self.act if cp % 2 == 0 else self.dve
                    if E is self.act:
                        self.op(E, lambda b=b, cp=cp: nc.scalar.copy(out=x2s[b][:, 2 * cp:2 * cp + 2, :, :], in_=bk[:, 0:512]), reads=[bres], writes=[x2_res[b]])
                    else:
                        self.op(E, lambda b=b, cp=cp: nc.vector.tensor_copy(out=x2s[b][:, 2 * cp:2 * cp + 2, :, :], in_=bk[:, 0:512]), reads=[bres], writes=[x2_res[b]])
                for hf in range(2):
                    self.dma(self.sp, self.X2D[l2, 4 * hf:4 * hf + 4].rearrange("cj ri c k -> c cj ri k"), x2s[b][:, 4 * hf:4 * hf + 4, :, :], x2clk[b],
                             reads=[x2_res[b]], pwrites=[self.X2D_res])
            self.barrier()
        with contextlib.ExitStack() as st:
            xc = self.sb(st, "xc", [128, 2, 128, 128], BF16); xc_res = Res("xc")
            w2 = self.sb(st, "w2", [128, 2 * N2], BF16); w2_res = Res("w2")
            yt = [self.sb(st, "yt", [128, L], BF16) for _ in range(2)]; yt_res = [Res("yt0"), Res("yt1")]
            xclk = self.clock("xc"); yclk = [self.clock("yt0"), self.clock("yt1")]
            self.dma(self.sp, w2[0:N2, :], self.din["w2_" + key][:, :], self.clock("w2"), writes=[w2_res])
            G = min(128, 512 // N2)
            scale = 1.0 / math.sqrt(256.0 * L)
            for cj in range(8):
                self.dma(self.sp, xc[0:N2], self.X2D[0:N2, cj], xclk, reads=[self.X2D_res], writes=[xc_res])
                y = yt[cj % 2]; yres = yt_res[cj % 2]
                for k0 in range(0, 128, G):
                    bk, bres = self.bank()
                    for g in range(G):
                        k1 = k0 + g
                        self.op(self.pe, lambda k1=k1, g=g: nc.tensor.matmul(bk[:, g * N2:(g + 1) * N2], xc[0:N2, 0, :, k1], w2[0:N2, 0:N2], start=True, stop=False),
                                reads=[xc_res, w2_res], writes=[bres], inc=False)
                        self.op(self.pe, lambda k1=k1, g=g: nc.tensor.matmul(bk[:, g * N2:(g + 1) * N2], xc[0:N2, 1, :, k1], w2[0:N2, N2:2 * N2], start=False, stop=True),
                                reads=[xc_res, w2_res], writes=[bres], inc=(g == G - 1))
                    outap = bass.AP(y, k0, [[L, 128], [1, G], [128, N2]])
                    inap = bass.AP(bk, 0, [[512, 128], [N2, G], [1, N2]])
                    self.op(self.act, lambda outap=outap, inap=inap: nc.scalar.activation(out=outap, in_=inap, func=AF.Copy, scale=scale), reads=[bres], writes=[yres])
                self.dma(self.sp, self.MO[cj * 128:(cj + 1) * 128, 0:L], y[:, 0:L], yclk[cj % 2], reads=[yres], pwrites=[self.MO_res])
            self.barrier()

    def pass_ssd(self, key, L):
        nc = self.nc
        ST = min(512, L)
        cps = ST // 128
        nst = L // ST
        with contextlib.ExitStack() as st:
            xbc = self.sb(st, "xbc", [128, 24, ST], BF16); xbc_res = Res("xbc")
            dtf = self.sb(st, "dtf", [128, 2, ST], F32); dtf_res = Res("dtf")
            zs = self.sb(st, "zs", [128, 16, ST], BF16); zs_res = Res("zs")
            yfw = self.sb(st, "yfw", [128, 2048], F32); yfw_res = Res("yfw")
            state = self.sb(st, "state", [128, 4, 512], F32); state_res = [Res(f"state{g}") for g in range(4)]
            stbf = self.sb(st, "stbf", [128, 4, 512], BF16); stbf_res = [Res(f"stbf{g}") for g in range(4)]
            xdt2 = [self.sb(st, "xdt", [128, 2048], BF16) for _ in range(2)]; xdt_res2 = [Res("xdt0"), Res("xdt1")]
            xw = self.sb(st, "xw", [128, 2048], BF16); xw_res = Res("xw")
            BT2 = [self.sb(st, "BT", [128, 512], BF16) for _ in range(2)]; BT_res2 = [Res("BT0"), Res("BT1")]
            dtT2 = [self.sb(st, "dtT", [128, 128], F32) for _ in range(2)]; dtT_res2 = [Res("dtT0"), Res("dtT1")]
            GT2 = [self.sb(st, "GT", [128, 4, 128], F32) for _ in range(2)]; GT_res2 = [Res("GT0"), Res("GT1")]
            EXPS2 = [self.sb(st, "EXPS", [128, 96], F32) for _ in range(2)]; EXPS_res2 = [Res("EXPS0"), Res("EXPS1")]
            RA = self.sb(st, "RA", [128, 32, 128], F32); RA_res = Res("RA")
            DEC = [self.sb(st, "DEC", [128, 512], F32) for _ in range(2)]; DEC_res = [Res("DEC0"), Res("DEC1")]
            STb = [self.sb(st, "STb", [128, 4, 128], BF16) for _ in range(2)]; STb_res = [Res("STb0"), Res("STb1")]
            Y2 = [self.sb(st, "Y", [128, 2048], F32) for _ in range(2)]; Y_res2 = [Res("Y0"), Res("Y1")]
            tmpY2 = [self.sb(st, "tmpY", [128, 512], F32) for _ in range(2)]; tmpY_res2 = [Res("tmpY0"), Res("tmpY1")]
            tmpS = self.sb(st, "tmpS", [128, 512], F32); tmpS_res = Res("tmpS")
            yg = self.sb(st, "yg", [128, 16, 128], F32); yg_res = Res("yg")
            ysT = self.sb(st, "ysT", [128, 16, ST], BF16); ysT_res = Res("ysT")
            negf = self.sb(st, "negf", [128, 2, 4, 128], F32);
            for d_ in range(2):
                self.op(self.dve, lambda d_=d_: nc.vector.tensor_scalar(out=negf[:, d_, :, :], in0=self.strict[d_].unsqueeze(1).broadcast_to([128, 4, 128]),
                                                                      scalar1=-30000.0, scalar2=None, op0=ALU.mult),
                        reads=[self.const_res], writes=[self.const_res])
            ldx = self.clock("ssd_ldx"); ldd = self.clock("ssd_ldd"); ldz = self.clock("ssd_ldz"); ldy = self.clock("ssd_ldy")
            sty = self.clock("ssd_sty"); sto = self.clock("ssd_sto")

            def bf_view(bk, ncols):
                return bk[:, 0:ncols // 2].bitcast(BF16)

            di = 0
            chunk_i = 0
            for d in range(2):
                for g in range(4):
                    self.op(self.dve, lambda g=g: nc.vector.memset(state[:, g, :], 0.0), writes=[state_res[g]])
                    self.op(self.dve, lambda g=g: nc.vector.memset(stbf[:, g, :], 0.0), writes=[stbf_res[g]])
                st_order = range(nst) if d == 0 else range(nst - 1, -1, -1)
                for si in st_order:
                    t0 = si * ST
                    self.dma(self.sp, xbc[:], self.XBC[:, t0:t0 + ST].rearrange("(j p) t -> p j t", p=128), ldx, reads=[self.XBC_res], writes=[xbc_res])
                    self.dma(self.sp, dtf[0:64, 0, :], self.DT[:, t0:t0 + ST], ldd, reads=[self.DT_res], writes=[dtf_res])
                    self.dma(self.sp, dtf[0:64, 1, :], self.DTA[:, t0:t0 + ST], ldd, reads=[self.DTA_res], writes=[])
                    dtf_res.w[ldd] = ldd.n
                    if d == 1:
                        self.dma(self.sp, zs[:], self.ZS[:, t0:t0 + ST].rearrange("(j p) t -> p j t", p=128), ldz, reads=[self.ZS_res], writes=[zs_res])
                    c_order = range(cps) if d == 0 else range(cps - 1, -1, -1)
                    for ci in c_order:
                        cs = ci * 128
                        c0 = t0 + cs
                        pc = chunk_i % 2; chunk_i += 1
                        xdt, xdt_res = xdt2[pc], xdt_res2[pc]
                        BT, BT_res = BT2[pc], BT_res2[pc]
                        dtT, dtT_res = dtT2[pc], dtT_res2[pc]
                        GT, GT_res = GT2[pc], GT_res2[pc]
                        EXPS, EXPS_res = EXPS2[pc], EXPS_res2[pc]
                        Y, Y_res = Y2[pc], Y_res2[pc]
                        tmpY, tmpY_res = tmpY2[pc], tmpY_res2[pc]
                        if d == 1:
                            self.dma(self.sp, yfw[:], self.YFW[c0:c0 + 128, :], ldy, reads=[self.YFW_res], writes=[yfw_res])
                        bkx = []
                        for half in range(2):
                            bk, bres = self.bank()
                            bv = bf_view(bk, 1024)
                            for jj in range(8):
                                j = half * 8 + jj
                                self.op(self.pe, lambda bv=bv, jj=jj, j=j: nc.tensor.transpose(bv[:, jj * 128:(jj + 1) * 128], xbc[:, j, cs:cs + 128], self.ident_bf),
                                        reads=[xbc_res, self.const_res], writes=[bres], inc=(jj == 7))
                            bkx.append((bv, bres))
                        bkB, bBres = self.bank()
                        bvB = bf_view(bkB, 512)
                        for g in range(4):
                            self.op(self.pe, lambda g=g: nc.tensor.transpose(bvB[:, g * 128:(g + 1) * 128], xbc[:, 16 + g, cs:cs + 128], self.ident_bf),
                                    reads=[xbc_res, self.const_res], writes=[bBres], inc=(g == 3))
                        bkD, bDres = self.bank()
                        for q in range(2):
                            self.op(self.pe, lambda q=q: nc.tensor.transpose(bkD[:, q * 64:(q + 1) * 64], dtf[0:64, q, cs:cs + 128], self.cst[0:64, 0:64]),
                                    reads=[dtf_res, self.const_res], writes=[bDres], inc=(q == 1))
                        self.op(self.act, lambda: nc.scalar.copy(out=dtT[:, 0:128], in_=bkD[:, 0:128]), reads=[bDres], writes=[dtT_res])
                        self.op(self.act, lambda: nc.scalar.copy(out=BT[:, 0:512], in_=bvB[:, 0:512]), reads=[bBres], writes=[BT_res])
                        for half in range(2):
                            bv, bres = bkx[half]
                            h0 = d * 32 + half * 16
                            self.op(self.dve, lambda bv=bv, half=half, h0=h0: nc.vector.tensor_tensor(
                                out=xdt[:, half * 1024:(half + 1) * 1024].rearrange("p (h e) -> p h e", e=64),
                                in0=bv[:, 0:1024].rearrange("p (h e) -> p h e", e=64),
                                in1=dtT[:, h0:h0 + 16].unsqueeze(2).broadcast_to([128, 16, 64]), op=ALU.mult),
                                reads=[bres, dtT_res], writes=[xdt_res])
                        dta = dtT[:, 64 + d * 32:64 + d * 32 + 32]
                        bkG, bGres = self.bank()
                        for g in range(4):
                            self.op(self.pe, lambda g=g: nc.tensor.matmul(bkG[:, g * 128:(g + 1) * 128], xbc[:, 16 + g, cs:cs + 128], xbc[:, 20 + g, cs:cs + 128], start=True, stop=True),
                                    reads=[xbc_res], writes=[bGres], inc=(g == 3))
                        self.op(self.act, lambda: nc.scalar.copy(out=GT[:, :, :], in_=bkG[:, 0:512]), reads=[bGres], writes=[GT_res])
                        bkT, bTres = self.bank()
                        for q, lh in enumerate((self.tri[d], self.strict[d], self.ones_f[:, :])):
                            self.op(self.pe, lambda q=q, lh=lh: nc.tensor.matmul(bkT[:, q * 32:(q + 1) * 32], lh, dta, start=True, stop=True),
                                    reads=[dtT_res, self.const_res], writes=[bTres], inc=(q == 2))
                        self.op(self.act, lambda: nc.scalar.activation(out=EXPS[:, 0:96], in_=bkT[:, 0:96], func=AF.Exp), reads=[bTres], writes=[EXPS_res])
                        self.op(self.pool, lambda: nc.gpsimd.tensor_tensor(out=RA[:, :, :], in0=self.tri[d].unsqueeze(1).broadcast_to([128, 32, 128]),
                                                                           in1=dta.unsqueeze(2).broadcast_to([128, 32, 128]), op=ALU.mult),
                                reads=[dtT_res, self.const_res], writes=[RA_res])
                        self.op(self.pool, lambda: nc.gpsimd.tensor_tensor(out=xw[:, :].rearrange("p (h e) -> p h e", e=64), in0=xdt[:, :].rearrange("p (h e) -> p h e", e=64),
                                                                           in1=EXPS[:, 32:64].unsqueeze(2).broadcast_to([128, 32, 64]), op=ALU.mult),
                                reads=[xdt_res, EXPS_res], writes=[xw_res])
                        for half in range(2):
                            yI = [self.bank(), self.bank()]
                            yT_ = [self.bank(), self.bank()]
                            for gi in range(2):
                                g = 2 * half + gi
                                self.op(self.pe, lambda g=g, gi=gi: nc.tensor.matmul(yT_[gi][0][:, 0:512], xbc[:, 20 + g, cs:cs + 128], stbf[:, g, :], start=True, stop=True),
                                        reads=[xbc_res, stbf_res[g]], writes=[yT_[gi][1]])
                            for hb in range(4):
                                h0 = 16 * half + 4 * hb
                                g = h0 // 8
                                bkS, bSres = self.bank()
                                self.op(self.pe, lambda h0=h0: nc.tensor.matmul(bkS[:, 0:512], self.strict[d], RA[:, h0:h0 + 4, :], start=True, stop=False),
                                        reads=[RA_res, self.const_res], writes=[bSres], inc=False)
                                self.op(self.pe, lambda: nc.tensor.matmul(bkS[:, 0:512], self.ident_f, negf[:, d, :, :], start=False, stop=True),
                                        reads=[self.const_res], writes=[bSres])
                                i2 = di % 2; di += 1
                                self.op(self.act, lambda i2=i2: nc.scalar.activation(out=DEC[i2][:, 0:512], in_=bkS[:, 0:512], func=AF.Exp), reads=[bSres], writes=[DEC_res[i2]])
                                self.op(self.dve, lambda i2=i2, g=g: nc.vector.tensor_tensor(out=STb[i2][:, :, :], in0=DEC[i2][:, 0:512].rearrange("p (h l) -> p h l", l=128),
                                                                                         in1=GT[:, g:g + 1, :].broadcast_to([128, 4, 128]), op=ALU.mult),
                                        reads=[DEC_res[i2], GT_res], writes=[STb_res[i2]])
                                for hh in range(4):
                                    h = h0 + hh
                                    yb, ybres = yI[(h - 16 * half) // 8]
                                    slot = h % 8
                                    self.op(self.pe, lambda i2=i2, hh=hh, h=h, yb=yb, slot=slot: nc.tensor.matmul(yb[:, slot * 64:(slot + 1) * 64], STb[i2][:, hh, :], xdt[:, h * 64:(h + 1) * 64],
                                                                                                                 start=True, stop=True),
                                            reads=[STb_res[i2], xdt_res], writes=[ybres], inc=(hh == 3))
                            for gi in range(2):
                                g = 2 * half + gi
                                self.op(self.dve, lambda g=g, gi=gi: nc.vector.tensor_tensor(out=tmpY[:, 0:512].rearrange("p (h e) -> p h e", e=64),
                                                                                         in0=yT_[gi][0][:, 0:512].rearrange("p (h e) -> p h e", e=64),
                                                                                         in1=EXPS[:, 8 * g:8 * g + 8].unsqueeze(2).broadcast_to([128, 8, 64]), op=ALU.mult),
                                        reads=[yT_[gi][1], EXPS_res], writes=[tmpY_res])
                                self.op(self.dve, lambda g=g, gi=gi: nc.vector.tensor_tensor(out=Y[:, g * 512:(g + 1) * 512], in0=yI[gi][0][:, 0:512], in1=tmpY[:, 0:512], op=ALU.add),
                                        reads=[yI[gi][1], tmpY_res], writes=[Y_res])
                        for g in range(4):
                            bkN, bNres = self.bank()
                            self.op(self.pe, lambda g=g: nc.tensor.matmul(bkN[:, 0:512], BT[:, g * 128:(g + 1) * 128], xw[:, g * 512:(g + 1) * 512], start=True, stop=True),
                                    reads=[BT_res, xw_res], writes=[bNres])
                            self.op(self.pool, lambda g=g: nc.gpsimd.tensor_tensor(out=tmpS[:, 0:512].rearrange("p (h e) -> p h e", e=64),
                                                                                   in0=state[:, g, :].rearrange("p (h e) -> p h e", e=64),
                                                                                   in1=EXPS[:, 64 + 8 * g:64 + 8 * g + 8].unsqueeze(2).broadcast_to([128, 8, 64]), op=ALU.mult),
                                    reads=[state_res[g], EXPS_res], writes=[tmpS_res])
                            self.op(self.dve, lambda g=g: nc.vector.tensor_tensor(out=state[:, g, :], in0=bkN[:, 0:512], in1=tmpS[:, 0:512], op=ALU.add),
                                    reads=[bNres, tmpS_res], writes=[state_res[g]])
                            self.op(self.act, lambda g=g: nc.scalar.copy(out=stbf[:, g, :], in_=state[:, g, :]), reads=[state_res[g]], writes=[stbf_res[g]])
                        if d == 0:
                            self.dma(self.sp, self.YFW[c0:c0 + 128, :], Y[:, :], sty, reads=[Y_res], pwrites=[self.YFW_res])
                            continue
                        self.op(self.dve, lambda: nc.vector.tensor_tensor(out=Y[:, :], in0=Y[:, :], in1=yfw[:, :], op=ALU.add), reads=[Y_res, yfw_res], writes=[Y_res])
                        for jb in range(4):
                            bk, bres = self.bank()
                            for jj in range(4):
                                j = jb * 4 + jj
                                self.op(self.pe, lambda j=j, jj=jj: nc.tensor.transpose(bk[:, jj * 128:(jj + 1) * 128], Y[:, j * 128:(j + 1) * 128], self.ident_f),
                                        reads=[Y_res, self.const_res], writes=[bres], inc=(jj == 3))
                            for jj in range(4):
                                j = jb * 4 + jj
                                self.op(self.dve, lambda j=j, jj=jj: nc.vector.scalar_tensor_tensor(out=yg[:, j, :], in0=xbc[:, j, cs:cs + 128], scalar=self.col("d_skip", j),
                                                                                                  in1=bk[:, jj * 128:(jj + 1) * 128], op0=ALU.mult, op1=ALU.add),
                                        reads=[xbc_res, bres, self.cols_res], writes=[yg_res])
                                self.op(self.pool, lambda j=j: nc.gpsimd.tensor_tensor(out=yg[:, j, :], in0=yg[:, j, :], in1=zs[:, j, cs:cs + 128], op=ALU.mult),
                                        reads=[yg_res, zs_res], writes=[yg_res])
                        self.rms_fm(lambda j: yg[:, j, :], 16, 128, lambda j: self.col("gate_norm", j), lambda j: ysT[:, j, cs:cs + 128], [yg_res], [ysT_res], group=4)
                    if d == 1:
                        self.dma(self.sp, self.MO[1024:3072, t0:t0 + ST].rearrange("(j p) t -> p j t", p=128), ysT[:, :, :], sto, reads=[ysT_res], pwrites=[self.MO_res])
            self.barrier()

    def mixer_in_cd(self, key, t0, T):
        nc = self.nc
        x, xres = self.x, self.x_res
        self.rms_fm(lambda j: x[:, j, 0:T], 8, T, lambda j: self.col("mix_norm1", j), lambda j: self.xn[:, j, 0:T], [xres], [self.xn_res])
        rhs = lambda kc: self.xn[:, kc, 0:T]
        self.dma(self.sp, self.ropec[:, 0:T], self.din["rope_cos"][:, t0:t0 + T], self.ropeclk, writes=[self.rope_res])
        self.dma(self.sp, self.ropes[:, 0:T], self.din["rope_sin"][:, t0:t0 + T], self.ropeclk, writes=[self.rope_res])
        big, big_res = self.big, self.big_res

        def ev_up(j, bk, bres, mwid):
            self.op(self.act, lambda: nc.scalar.copy(out=big[:, j, 0:T], in_=bk[:, 0:T]), reads=[bres], writes=[big_res])
        self.linear_fm(self.wstream("cd_in", 0, 1024), rhs, [self.xn_res], T, ev_up)
        self.dma(self.sp, self.UP[:, t0:t0 + T].rearrange("(j p) t -> p j t", p=128), big[:, :, 0:T], self.stclk[0], reads=[big_res], pwrites=[self.UP_res])
        for which, c0, gname, scale, dst, dres, ci in (("q", 1024, "cd_q_norm", 0.125, self.QT, self.QT_res, 1), ("k", 2048, "cd_k_norm", 1.0, self.KTd, self.KTd_res, 2)):
            b2, b2_res = self.big2, self.big2_res

            def ev_q(j, bk, bres, mwid):
                self.op(self.act, lambda: nc.scalar.copy(out=b2[:, j, 0:T], in_=bk[:, 0:T]), reads=[bres], writes=[b2_res])
            self.linear_fm(self.wstream("cd_in", c0, c0 + 1024), rhs, [self.xn_res], T, ev_q)
            qn, qn_res = self.h, self.h_res
            self.rms_fm(lambda j: b2[:, j, 0:T], 8, T, lambda j: self.col(gname, 0), lambda j: qn[:, j, 0:T], [b2_res], [qn_res],
                        group=1, scale=scale, lhsT=self.bd64_bf, n_norm=64)
            for j in range(8):
                bk, bres = self.bank()
                self.op(self.pe, lambda j=j: nc.tensor.matmul(bk[:, 0:T], self.perm_bf[:], qn[:, j, 0:T], start=True, stop=True),
                        reads=[qn_res, self.const_res], writes=[bres])
                self.op(self.dve, lambda: nc.vector.tensor_tensor(out=self.tmpn[:, 0:T], in0=bk[:, 0:T], in1=self.ropes[:, 0:T], op=ALU.mult),
                        reads=[bres, self.rope_res], writes=[self.tmpn_res])
                self.op(self.pool, lambda j=j: nc.gpsimd.tensor_tensor(out=self.tmpp[:, 0:T], in0=qn[:, j, 0:T], in1=self.ropec[:, 0:T], op=ALU.mult),
                        reads=[qn_res, self.rope_res], writes=[self.tmpp_res])
                self.op(self.dve, lambda j=j: nc.vector.tensor_tensor(out=qn[:, 8 + j, 0:T], in0=self.tmpn[:, 0:T], in1=self.tmpp[:, 0:T], op=ALU.add),
                        reads=[self.tmpn_res, self.tmpp_res], writes=[qn_res])
            self.dma(self.sp, dst[:, t0:t0 + T].rearrange("(j p) t -> p j t", p=128), qn[:, 8:16, 0:T], self.stclk[ci], reads=[qn_res], pwrites=[dres])
        wv = self.wstream("cd_in", 3072, 4096, 512)
        vt, vt_res = self.h, self.h_res
        nb = T // 128
        vview = bass.AP(self.h, 16 * T, [[24 * T, 128], [1024, nb], [1, 1024]])
        for pi in range(wv.npanels):
            wt, wres, mw = wv.get(pi)
            for tb in range(nb):
                bk, bres = self.bank()
                for kc in range(8):
                    self.op(self.pe, lambda kc=kc, tb=tb: nc.tensor.matmul(bk[:, 0:512], self.xn[:, kc, tb * 128:(tb + 1) * 128], wt[:, kc, 0:512],
                                                                         start=(kc == 0), stop=(kc == 7)),
                            reads=[wres, self.xn_res], writes=[bres], inc=(kc == 7))
                self.op(self.act, lambda tb=tb, pi=pi: nc.scalar.copy(out=vview[:, tb, pi * 512:(pi + 1) * 512], in_=bk[:, 0:512]),
                        reads=[bres], writes=[vt_res])
        wv.close()
        self.dma(self.sp, self.Vd[t0:t0 + T, :].rearrange("(tb p) e -> p tb e", p=128), vview, self.stclk[3], reads=[vt_res], pwrites=[self.Vd_res])

    def pass_d(self, key, L):
        nc = self.nc
        T = min(self.T, L)
        nt = L // T
        with contextlib.ExitStack() as st:
            W = T + 16
            pu = self.sb(st, "pu", [128, 8, W], F32); pu_res = Res("pu")
            sa = self.sb(st, "sa", [128, W], F32); sa_res = Res("sa")
            sbb = self.sb(st, "sbb", [128, W], F32); sbb_res = Res("sbb")
            icn = self.sb(st, "icn", [128, 4, T], F32); icn_res = Res("icn")
            dd = self.sb(st, "dd", [128, 8, T], BF16); dd_res = Res("dd")
            po = self.sb(st, "po", [128, 8, T], BF16); po_res = Res("po")
            ldc = self.clock("pld"); stc = self.clock("pst")
            self.alloc_wring(st, 1)
            wp = self.wstream("pool_w", 0, 256, 256)
            pw, pwres, _ = wp.get(0)
            for ti in range(nt):
                t0 = ti * T
                lo = max(0, t0 - 8); hi = min(L, t0 + T + 8)
                if lo != t0 - 8 or hi != t0 + T + 8:
                    self.op(self.dve, lambda: nc.vector.memset(pu[:], 0.0), writes=[pu_res])
                self.dma(self.sp, pu[:, :, lo - (t0 - 8):hi - (t0 - 8)], self.UP[:, lo:hi].rearrange("(j p) t -> p j t", p=128), ldc,
                         reads=[self.UP_res], writes=[pu_res])
                self.dma(self.sp, icn[:], self.din["invcnt_" + key][:, t0:t0 + T].partition_broadcast(128), self.clock("picn"), writes=[icn_res])
                for j in range(8):
                    g = j // 2
                    u = pu[:, j, :]
                    E1, E2 = (self.dve, self.pool) if j % 2 == 0 else (self.pool, self.dve)

                    def tt(E, out, a, b, op, reads, writes):
                        if E is self.dve:
                            self.op(E, lambda: nc.vector.tensor_tensor(out=out, in0=a, in1=b, op=op), reads=reads, writes=writes)
                        else:
                            self.op(E, lambda: nc.gpsimd.tensor_tensor(out=out, in0=a, in1=b, op=op), reads=reads, writes=writes)
                    E = self.dve
                    tt(E, sa[:, 1:W], u[:, 0:W - 1], u[:, 1:W], ALU.add, [pu_res], [sa_res])
                    cur, cur_res, oth, oth_res = sa, sa_res, sbb, sbb_res
                    lo_v, hi_v = 1, W
                    for sh in (1, 2, 4)[:g]:
                        a0, a1 = lo_v + sh, hi_v - sh
                        tt(E, oth[:, a0:a1], cur[:, a0 - sh:a1 - sh], cur[:, a0 + sh:a1 + sh], ALU.add, [cur_res], [oth_res])
                        cur, cur_res, oth, oth_res = oth, oth_res, cur, cur_res
                        lo_v, hi_v = a0, a1
                    assert lo_v <= 8 and hi_v >= 8 + T
                    tt(E, oth[:, 8:8 + T], cur[:, 8:8 + T], icn[:, g, :], ALU.mult, [cur_res, icn_res], [oth_res])
                    tt(E, dd[:, j, :], oth[:, 8:8 + T], u[:, 8:8 + T], ALU.subtract, [oth_res, pu_res], [dd_res])
                for g in range(4):
                    for m in range(2):
                        bk, bres = self.bank()
                        for kc in range(2):
                            self.op(self.pe, lambda kc=kc, g=g, m=m: nc.tensor.matmul(bk[:, 0:T], pw[:, 2 * g + kc, m * 128:(m + 1) * 128], dd[:, 2 * g + kc, :],
                                                                                   start=(kc == 0), stop=(kc == 1)),
                                    reads=[pwres, dd_res], writes=[bres], inc=(kc == 1))
                        jj = 2 * g + m
                        self.op(self.act, lambda jj=jj: nc.scalar.activation(out=po[:, jj, :], in_=bk[:, 0:T], func=AF.Copy, scale=self.col("pool_scale", jj)),
                                reads=[bres, self.cols_res], writes=[po_res])
                self.dma(self.sp, self.MO[0:1024, t0:t0 + T].rearrange("(j p) t -> p j t", p=128), po[:], stc, reads=[po_res], pwrites=[self.MO_res])
            wp.close()
            self.barrier()
        if self.flags.get("noattn"):
            return
        nk = L // 128
        with contextlib.ExitStack() as st:
            kth = self.sb(st, "kth", [128, L], BF16); kth_res = Res("kth")
            va = self.sb(st, "va", [128, nk, 132], BF16); va_res = Res("va")
            qh = [self.sb(st, "qh", [128, T], BF16) for _ in range(2)]; qh_res = [Res("qh0"), Res("qh1")]
            NE = 4
            qzb = [[self.sb(st, "qz", [128, T], BF16) for _ in range(2)] for _ in range(2)]
            for a_ in range(2):
                for b_ in range(2):
                    self.op(self.dve, lambda a_=a_, b_=b_: nc.vector.memset(qzb[a_][b_][:], 0.0), writes=[qh_res[a_]])
            eb = [self.sb(st, "eb", [128, T], BF16) for _ in range(NE)]; eb_res = [Res(f"eb{i}") for i in range(NE)]
            esA = [self.sb(st, "esA", [128, T], F32) for _ in range(2)]; esA_res = [Res("esA0"), Res("esA1")]
            esB = [self.sb(st, "esB", [128, T], F32) for _ in range(2)]; esB_res = [Res("esB0"), Res("esB1")]
            rden = [self.sb(st, "rden", [128, T], F32) for _ in range(2)]; rden_res = [Res("rden0"), Res("rden1")]
            of = self.sb(st, "of", [128, 1, T], F32); of_res = Res("of")
            o1 = self.sb(st, "o1", [128, T], F32); o1_res = Res("o1")
            ot = [self.sb(st, "ot", [128, T], BF16) for _ in range(2)]; ot_res = [Res("ot0"), Res("ot1")]
            kclk = self.clock("kld"); vclk = self.clock("vld"); qclk = [self.clock("qld0"), self.clock("qld1")]
            oclk = [self.clock("ost0"), self.clock("ost1")]
            acc = [(self.banks[i], self.bank_res[i]) for i in range(2)]
            den = [(self.banks[2 + i], self.bank_res[2 + i]) for i in range(2)]
            srot = [(self.banks[4 + i], self.bank_res[4 + i]) for i in range(4)]
            si = 0
            ei = 0
            qi = 0
            for hd in range(8):
                self.dma(self.sp, kth[:], self.KTd[hd * 128:(hd + 1) * 128, 0:L], kclk, reads=[self.KTd_res], writes=[kth_res])
                deps_done = False
                for k0 in range(0, nk, 16):
                    k1 = min(nk, k0 + 16)
                    self._deps(self.sp, [self.Vd_res], [va_res] if not deps_done else [], ())
                    deps_done = True
                    ins = nc.sync.dma_start(out=va[:, k0:k1, 0:128], in_=self.Vd[k0 * 128:k1 * 128, hd * 128:(hd + 1) * 128].rearrange("(kt p) e -> p kt e", p=128))
                    ins.then_inc(vclk.sem, 16)
                    vclk.n += 16
                va_res.w = {vclk: vclk.n}
                va_res.r = {}
                for qt in range(L // T):
                    qz = qzb[qi % 2]; qres = qh_res[qi % 2]
                    self.dma(self.sp, qz[0][0:64, :], self.QT[hd * 128:hd * 128 + 64, qt * T:(qt + 1) * T], qclk[qi % 2], reads=[self.QT_res], writes=[qres])
                    self.dma(self.sp, qz[1][64:128, :], self.QT[hd * 128 + 64:(hd + 1) * 128, qt * T:(qt + 1) * T], qclk[qi % 2], reads=[self.QT_res], writes=[qres])
                    steps = [(c, kt) for c in range(2) for kt in range(nk)]
                    LA = 2
                    pend = {}
                    for i in range(len(steps) + LA):
                        if i < len(steps):
                            c, kt = steps[i]
                            sbk, sres = srot[si % 4]; si += 1
                            self.op(self.pe, lambda c=c, kt=kt, sbk=sbk, qz=qz: nc.tensor.matmul(sbk[:, 0:T], kth[:, kt * 128:(kt + 1) * 128], qz[c][:, 0:T],
                                                                                             start=True, stop=True),
                                    reads=[kth_res, qres], writes=[sres])
                            e = eb[ei % NE]; eres = eb_res[ei % NE]; ei += 1
                            self.op(self.act, lambda e=e, sbk=sbk: nc.scalar.activation(out=e[:, 0:T], in_=sbk[:, 0:T], func=AF.Exp), reads=[sres], writes=[eres])
                            pend[i] = (e, eres)
                        j = i - LA
                        if j < 0:
                            continue
                        c, kt = steps[j]
                        e, eres = pend.pop(j)
                        ab, ares = acc[c]
                        self.op(self.pe, lambda e=e, kt=kt, ab=ab: nc.tensor.matmul(ab[:, 0:T], va[:, kt, 0:128], e[:, 0:T], start=(kt == 0), stop=(kt == nk - 1)),
                                reads=[eres, va_res], writes=[ares])
                        db, dres = den[c]
                        self.op(self.pe, lambda e=e, kt=kt, db=db: nc.tensor.matmul(db[:, 0:T], self.ones_bf[:], e[:, 0:T], start=(kt == 0), stop=(kt == nk - 1)),
                                reads=[eres, self.const_res], writes=[dres])
                        if kt == nk - 1:
                            self.op(self.dve, lambda c=c, db=db: nc.vector.reciprocal(out=rden[c][:, 0:T], in_=db[:, 0:T]), reads=[dres], writes=[rden_res[c]])
                    self.op(self.dve, lambda: nc.vector.tensor_tensor(out=of[:, 0, 0:T], in0=acc[0][0][:, 0:T], in1=rden[0][:, 0:T], op=ALU.mult),
                            reads=[acc[0][1], rden_res[0]], writes=[of_res])
                    self.op(self.dve, lambda: nc.vector.tensor_tensor(out=o1[:, 0:T], in0=acc[1][0][:, 0:T], in1=rden[1][:, 0:T], op=ALU.mult),
                            reads=[acc[1][1], rden_res[1]], writes=[o1_res])
                    self.op(self.dve, lambda: nc.vector.scalar_tensor_tensor(out=of[:, 0, 0:T], in0=o1[:, 0:T], scalar=self.neglam[:, 0:1], in1=of[:, 0, 0:T], op0=ALU.mult, op1=ALU.add),
                            reads=[o1_res, of_res, self.small_res], writes=[of_res])
                    o_t = ot[qi % 2]; o_res = ot_res[qi % 2]
                    self.bank_i = 4
                    self.rms_fm(lambda j: of[:, 0, 0:T], 1, T, lambda j: self.col("sub_norm_col", 0), lambda j: o_t[:, 0:T], [of_res], [o_res], scale=1.0 - LAMBDA_INIT1)
                    self.dma(self.sp, self.MO[1024 + hd * 128:1024 + (hd + 1) * 128, qt * T:(qt + 1) * T], o_t[:, 0:T], oclk[qi % 2], reads=[o_res], pwrites=[self.MO_res])
                    qi += 1
            self.barrier()

    def alloc_wring(self, stack, n=6):
        self.wring = [self.sb(stack, "wring", [128, 4096], BF16) for _ in range(n)]
        self.wring_res = [Res(f"wring{i}") for i in range(n)]
        self.wring_clk = [self.clock(f"wring{i}") for i in range(n)]
        self.wring_owner = [None] * n
        self.wring_i = 0

    def run_seq(self, key, L):
        nc = self.nc
        T = min(self.T, L)
        nt = L // T
        xin = self.din["xT_" + key]
        fl = self.flags
        with contextlib.ExitStack() as st0:
            self.sq = self.sb(st0, "sq", [128, 2, T], BF16); self.sq_res = [Res("sq0"), Res("sq1")]
            self.rstd = self.sb(st0, "rstd", [128, T], F32); self.rstd_res = Res("rstd")
            self.tmpn = self.sb(st0, "tmpn", [128, T], F32); self.tmpn_res = Res("tmpn")
            self.tmpp = self.sb(st0, "tmpp", [128, T], F32); self.tmpp_res = Res("tmpp")
            xld = self.clock("xld")
            xst = self.clock("xst")
            self.stclk = [self.clock(f"st{i}") for i in range(4)]
            self.ropeclk = self.clock("rope")

            def token_pass(stages, src, src_res, dst, dst_res, prep_layer=None, ab=False):
                with contextlib.ExitStack() as st:
                    self.alloc_wring(st, 6)
                    if ab:
                        self.cb = self.sb(st, "cb", [128, 24, T + 6], BF16); self.cb_res = [Res(f"cb{j}") for j in range(24)]
                        self.xo = self.sb(st, "xo", [128, 8, T + 2], BF16); self.xo_res = Res("xo")
                        self.cacc = [self.sb(st, "cacc", [128, T + 2], F32) for _ in range(2)]; self.cacc_res = [Res("cacc0"), Res("cacc1")]
                    self.x = self.sb(st, "x", [128, 8, T], F32); self.x_res = Res("x")
                    self.xn = self.sb(st, "xn", [128, 8, T], BF16); self.xn_res = Res("xn")
                    self.h = self.sb(st, "h", [128, 24, T], BF16); self.h_res = Res("h")
                    self.big = self.sb(st, "big", [128, 8, T], F32); self.big_res = Res("big")
                    if not ab:
                        self.big2 = self.sb(st, "big2", [128, 8, T], F32); self.big2_res = Res("big2")
                    self.sg = self.sb(st, "sg", [128, 2, T], F32); self.sg_res = [Res("sg0"), Res("sg1")]
                    self.ex = self.sb(st, "ex", [128, 2, T], BF16); self.ex_res = [Res("ex0"), Res("ex1")]
                    if not ab:
                        self.ropec = self.sb(st, "ropec", [128, T], F32); self.ropes = self.sb(st, "ropes", [128, T], F32); self.rope_res = Res("rope")
                        self.KT = self.sb(st, "KT", [128, 8, NMEM], BF16); self.KT_res = Res("KT")
                        self.Vm = self.sb(st, "Vm", [128, 2, D], BF16); self.Vm_res = Res("Vm")
                    if prep_layer is not None and not fl.get("skip_cross"):
                        with contextlib.ExitStack() as st2:
                            self.cross_prep(key, prep_layer, st2)
                            self.barrier()
                    for ti in range(nt):
                        t0 = ti * T
                        self.dma(self.sp, self.x[:], src[:, t0:t0 + T].rearrange("(j p) t -> p j t", p=128), xld, reads=[src_res], writes=[self.x_res])
                        for stg in stages:
                            stg(t0)
                        self.dma(self.sp, dst[:, t0:t0 + T].rearrange("(j p) t -> p j t", p=128), self.x[:], xst, reads=[self.x_res], pwrites=[dst_res])
                    self.barrier()

            sk_ffn = fl.get("skip_ffn"); sk_cr = fl.get("skip_cross")
            A = []
            if not sk_ffn: A.append(lambda t0: self.ffn(0, 1, T))
            if not fl.get("noab"): A.append(lambda t0: self.mixer_in_ab(key, t0, T, t0 == 0, t0 + T == L))
            token_pass(A, xin, Res("xin"), self.XR, self.XR_res, ab=True)
            if not fl.get("noab"):
                if not fl.get("nofnet"):
                    self.pass_fnet(key, L)
                if not fl.get("nossd"):
                    self.pass_ssd(key, L)
            C = []
            if not fl.get("noab"): C.append(lambda t0: self.mixer_out("ab_out", 24, t0, T))
            if not sk_cr: C.append(lambda t0: self.cross(0, T))
            if not sk_ffn:
                C.append(lambda t0: self.ffn(0, 2, T)); C.append(lambda t0: self.ffn(1, 1, T))
            if not fl.get("nocd"): C.append(lambda t0: self.mixer_in_cd(key, t0, T))
            token_pass(C, self.XR, self.XR_res, self.XR, self.XR_res, prep_layer=0)
            if not fl.get("nocd") and not fl.get("cd_in_only"):
                self.pass_d(key, L)
            E = []
            if not fl.get("nocd") and not fl.get("cd_in_only"): E.append(lambda t0: self.mixer_out("cd_out", 16, t0, T))
            if not sk_cr: E.append(lambda t0: self.cross(1, T))
            if not sk_ffn: E.append(lambda t0: self.ffn(1, 2, T))
            token_pass(E, self.XR, self.XR_res, self.yT[key], self.y_res[key], prep_layer=1)
            self.barrier()


class WStream:
    DEPTH = 2

    def __init__(self, b, nm, c0, c1, mw=None):
        self.b = b
        self.nm = nm
        K, N = [(w[3], w[4]) for w in WEIGHTS if w[0] == nm][0]
        self.KC = K // 128
        self.N = N
        MW = panel_w(nm)
        assert c0 % MW == 0, (nm, c0, MW)
        assert self.KC * MW <= 4096
        self.panels = [(c, min(MW, c1 - c, N - c)) for c in range(c0, c1, MW)]
        for c, w in self.panels:
            assert w == MW or c + w == N, (nm, c, w)
        self.npanels = len(self.panels)
        self.issued = 0
        self.cur = -1
        self.closed = False
        self.slots = {}

    def _issue(self, i):
        b = self.b
        s = b.wring_i
        b.wring_i = (b.wring_i + 1) % len(b.wring)
        own = b.wring_owner[s]
        if own is not None:
            st, idx = own
            assert st.closed or st.cur > idx, "weight ring over-subscribed"
        b.wring_owner[s] = (self, i)
        c, mw = self.panels[i]
        KC = self.KC
        src = bass.AP(b.ws[self.nm], KC * c, [[KC * self.N, 128], [1, KC * mw]])
        b.dma(b.sp, b.wring[s][:, 0:KC * mw], src, b.wring_clk[s], reads=[b.ws_res[self.nm]], writes=[b.wring_res[s]])
        view = bass.AP(b.wring[s], 0, [[4096, 128], [mw, KC], [1, mw]])
        self.slots[i] = (view, b.wring_res[s], mw)

    def prime(self):
        while self.issued < min(self.DEPTH, self.npanels):
            self._issue(self.issued)
            self.issued += 1

    def get(self, i):
        self.cur = i
        while self.issued < min(i + self.DEPTH, self.npanels):
            self._issue(self.issued)
            self.issued += 1
        return self.slots.pop(i)

    def close(self):
        self.closed = True


_CACHE = {}


def make_in_maps(inp, Lp, Ls, ncores):
    cols = pack_cols(inp)
    shared = {"cols": cols}
    shared.update(make_consts(Lp, Ls))
    for nm, src, l, K, N in WEIGHTS:
        w = np.asarray(inp[src][l], np.float32)
        shared["w_" + nm] = np.ascontiguousarray(w.reshape(K, N))
    maps = []
    for c in range(ncores):
        m = dict(shared)
        m["xT_p"] = np.ascontiguousarray(np.asarray(inp["x_prompt"][c], np.float32).T)
        m["xT_s"] = np.ascontiguousarray(np.asarray(inp["x_sample"][c], np.float32).T)
        m["memT_p"] = np.ascontiguousarray(np.asarray(inp["mem_prompt"][c], np.float32).T)
        m["memT_s"] = np.ascontiguousarray(np.asarray(inp["mem_sample"][c], np.float32).T)
        maps.append(m)
    return maps


def run(inp, Lp, Ls, ncores, flags=None):
    key = (Lp, Ls, tuple(sorted((flags or {}).items())))
    if key not in _CACHE:
        _CACHE[key] = Builder(Lp, Ls, flags).build()
    nc = _CACHE[key]
    maps = make_in_maps(inp, Lp, Ls, ncores)
    res = run_bass_kernel_spmd(nc, maps, core_ids=list(range(ncores)))
    yp = np.stack([np.ascontiguousarray(r["yT_p"].T) for r in res.results], 0)
    ys = np.stack([np.ascontiguousarray(r["yT_s"].T) for r in res.results], 0)
    return yp.astype(np.float32), ys.astype(np.float32)


def kernel(**inputs):
    inp = {k: np.asarray(v) for k, v in inputs.items()}
    return run(inp, LP_FULL, LS_FULL, 8)
```

```python
import contextlib
import math
import numpy as np
import ml_dtypes
import concourse.bass as bass
import concourse.mybir as mybir
from concourse.bass_utils import run_bass_kernel_spmd

F32 = mybir.dt.float32
BF16 = mybir.dt.bfloat16
AF = mybir.ActivationFunctionType
ALU = mybir.AluOpType

D = 1024
DFF = 2816
NMEM = 256
EPS = 1e-6
AB_IN = 6208
CD_IN = 4096
LP_FULL, LS_FULL = 4096, 8192


class Clock:
    def __init__(self, sem, name):
        self.sem = sem
        self.n = 0
        self.name = name


class Eng:
    def __init__(self, name, h, clk, inorder=False):
        self.name = name
        self.h = h
        self.clk = clk
        self.seen = {}
        self.pending = False
        self.inorder = inorder


class Res:
    __slots__ = ("name", "w", "r")

    def __init__(self, name=""):
        self.name = name
        self.w = {}
        self.r = {}


class P:
    def __init__(self):
        self.nc = bass.Bass("TRN2", target_bir_lowering=False)
        self.es = contextlib.ExitStack()
        nc = self.nc
        self.nsem = 0
        self.clocks = []
        self.clock_cache = {}
        self.pe = Eng("pe", nc.tensor, self.clock("pe"), inorder=True)
        self.act = Eng("act", nc.scalar, self.clock("act"))
        self.dve = Eng("dve", nc.vector, self.clock("dve"))
        self.pool = Eng("pool", nc.gpsimd, self.clock("pool"))
        self.sp = Eng("sp", nc.sync, self.clock("sp"))
        self.engs = [self.pe, self.act, self.dve, self.pool, self.sp]
        self.uid = 0

    def clock(self, name):
        if name in self.clock_cache:
            return self.clock_cache[name]
        self.nsem += 1
        sem = self.es.enter_context(self.nc.semaphore(f"s_{name}_{self.nsem}"))
        c = Clock(sem, name)
        self.clocks.append(c)
        self.clock_cache[name] = c
        return c

    def _wait(self, E, clk, val):
        if val <= 0:
            return
        if E.seen.get(clk, 0) >= val:
            return
        E.h.wait_ge(clk.sem, val)
        E.seen[clk] = val

    def _deps(self, E, reads, writes, pwrites):
        deps = []
        for r in reads:
            deps.extend(r.w.items())
        for w in writes:
            deps.extend(w.w.items())
            deps.extend(w.r.items())
        for w in pwrites:
            deps.extend(w.r.items())
        for clk, val in deps:
            if clk is E.clk and E.inorder:
                continue
            self._wait(E, clk, val)

    def op(self, E, fn, reads=(), writes=(), inc=True):
        self._deps(E, reads, writes, ())
        ins = fn()
        val = E.clk.n + 1
        if inc:
            ins.then_inc(E.clk.sem, 1)
            E.clk.n = val
            E.pending = False
        else:
            E.pending = True
        for r in reads:
            r.r[E.clk] = val
        for w in writes:
            w.w = {E.clk: val}
            w.r = {}
        return ins

    def dma(self, Q, out, in_, clk, reads=(), writes=(), pwrites=()):
        self._deps(Q, reads, writes, pwrites)
        ins = Q.h.dma_start(out=out, in_=in_)
        ins.then_inc(clk.sem, 16)
        clk.n += 16
        for r in reads:
            r.r[clk] = clk.n
        for w in writes:
            w.w = {clk: clk.n}
            w.r = {}
        for w in pwrites:
            w.w[clk] = clk.n
        return ins

    def barrier(self):
        for E in self.engs:
            if E.pending:
                self.op(E, lambda E=E: E.h.nop(), ())
        for E in self.engs:
            for clk in self.clocks:
                if clk is E.clk:
                    continue
                self._wait(E, clk, clk.n)

    def finish(self, out_res):
        for E in self.engs:
            if E.pending:
                self.op(E, lambda E=E: E.h.nop(), ())
        for r in out_res:
            for clk, val in r.w.items():
                self._wait(self.sp, clk, val)


def col_layout():
    lay = {}
    n = 0

    def add(name, w):
        nonlocal n
        lay[name] = (n, w)
        n += w

    for l in range(2):
        for nm in ("ffn1_norm", "mix_norm", "cross_norm", "cross_mem_norm", "ffn2_norm"):
            add(f"{nm}{l}", 8)
        add(f"cross_q_norm{l}", 2)
        add(f"cross_k_norm{l}", 2)
    add("conv_w", 120)
    add("conv_b", 24)
    add("dt_bias", 1)
    add("a_log", 1)
    add("d_skip", 16)
    add("gate_norm", 16)
    add("pool_scale", 8)
    add("cd_q_norm", 1)
    add("cd_k_norm", 1)
    add("sub_norm_row", 128)
    add("sub_norm_col", 1)
    add("lam", 4)
    return lay, n


COLS, NCOL = col_layout()


def pack_cols(inp):
    a = np.zeros((128, NCOL), np.float32)

    def put(name, arr2d):
        o, w = COLS[name]
        assert arr2d.shape == (128, w), (name, arr2d.shape, w)
        a[:, o:o + w] = arr2d

    def chunked(v):
        return np.ascontiguousarray(v.reshape(-1, 128).T)

    for l in range(2):
        for nm in ("ffn1_norm", "mix_norm", "cross_norm", "cross_mem_norm", "ffn2_norm"):
            put(f"{nm}{l}", chunked(inp[nm][l]))
        put(f"cross_q_norm{l}", chunked(inp["cross_q_norm"][l]))
        put(f"cross_k_norm{l}", chunked(inp["cross_k_norm"][l]))
    cw = inp["ab_conv_w"][0]
    put("conv_w", np.concatenate([chunked(cw[k]) for k in range(5)], axis=1))
    put("conv_b", chunked(inp["ab_conv_b"][0]))
    c = np.zeros((128, 1), np.float32)
    c[:64, 0] = inp["ab_dt_bias"][0].reshape(64)
    put("dt_bias", c)
    c = np.zeros((128, 1), np.float32)
    c[:64, 0] = inp["ab_a_log"][0].reshape(64)
    put("a_log", c)
    put("d_skip", chunked(np.repeat(inp["ab_d_skip"][0], 64)))
    put("gate_norm", chunked(inp["ab_gate_norm"][0]))
    put("pool_scale", chunked(inp["cd_pool_scale"][0]))
    put("cd_q_norm", np.tile(inp["cd_q_norm"][0], 2).reshape(128, 1))
    put("cd_k_norm", np.tile(inp["cd_k_norm"][0], 2).reshape(128, 1))
    put("sub_norm_row", np.tile(inp["cd_sub_norm"][0][None, :], (128, 1)))
    put("sub_norm_col", inp["cd_sub_norm"][0].reshape(128, 1))
    lam = np.zeros((128, 4), np.float32)
    for i, nm in enumerate(("cd_lambda_q1", "cd_lambda_k1", "cd_lambda_q2", "cd_lambda_k2")):
        lam[:64, i] = inp[nm][0]
    put("lam", lam)
    return a


ROPE_THETA = 500000.0
LAMBDA_INIT1 = 0.8 - 0.6 * math.exp(-0.3 * 1)
NCONST = 896


def make_consts(Lp, Ls):
    Lmax = max(Lp, Ls)
    c = np.zeros((128, NCONST), np.float32)
    c[:, 0:128] = np.eye(128, dtype=np.float32)
    bd = np.zeros((128, 128), np.float32); bd[:64, :64] = 1; bd[64:, 64:] = 1
    c[:, 128:256] = bd
    perm = np.zeros((128, 128), np.float32)
    for m in range(128):
        r = m % 64
        if r < 8:
            perm[m + 8, m] = 1
        elif r < 16:
            perm[m - 8, m] = 1
    c[:, 256:384] = perm
    inv = (np.float32(ROPE_THETA) ** (-np.arange(0, 16, 2, dtype=np.float32) / np.float32(16))).astype(np.float32)
    ang = np.arange(Lmax, dtype=np.float32)[:, None] * inv[None, :]
    cosf = np.ones((128, Lmax), np.float32); sinf = np.zeros((128, Lmax), np.float32)
    for m in range(128):
        r = m % 64
        if r < 16:
            cosf[m] = np.cos(ang[:, r % 8])
            sinf[m] = (-1.0 if r < 8 else 1.0) * np.sin(ang[:, r % 8])
    jj = np.arange(128)[:, None]; ll = np.arange(128)[None, :]
    c[:, 384:512] = (jj <= ll); c[:, 512:640] = (jj >= ll); c[:, 640:768] = (jj > ll); c[:, 768:896] = (jj < ll)
    out = {"consts": c, "rope_cos": cosf, "rope_sin": sinf}
    bf = ml_dtypes.bfloat16
    cin = np.arange(256)[:, None]; cout = np.arange(256)[None, :]
    a = 2 * np.pi * ((cin * cout) % 256) / 256.0
    dft = np.concatenate([np.cos(a), np.sin(a)], 1)
    out["dftc"] = np.ascontiguousarray(dft.reshape(2, 128, 512).transpose(1, 0, 2)).astype(bf)
    for key, L in (("p", Lp), ("s", Ls)):
        N2 = L // 128
        l = np.arange(L, dtype=np.int64)[:, None]; k1 = np.arange(128, dtype=np.int64)[None, :]
        a = 2 * np.pi * ((l * k1) % L) / float(L)
        out["tab_" + key] = np.concatenate([-np.sin(a), np.cos(a), np.sin(a)], 1).astype(bf)
        l2 = np.arange(N2)[:, None]; k2 = np.arange(N2)[None, :]
        a = 2 * np.pi * ((l2 * k2) % N2) / float(N2)
        out["w2_" + key] = np.concatenate([np.cos(a), -np.sin(a)], 1).astype(bf)
    for key, L in (("p", Lp), ("s", Ls)):
        ic = np.zeros((4, L), np.float32)
        pos = np.arange(L)
        for g, w in enumerate((2, 4, 8, 16)):
            lo = np.clip(pos - w // 2, 0, L - 1); hi = np.clip(pos + w // 2 - 1, 0, L - 1)
            ic[g] = 1.0 / (hi - lo + 1).astype(np.float32)
        out["invcnt_" + key] = ic
    return out


WEIGHTS = [
    ("ffn1_g0", "ffn1_w_gate", 0, D, DFF), ("ffn1_u0", "ffn1_w_up", 0, D, DFF), ("ffn1_d0", "ffn1_w_down", 0, DFF, D),
    ("ab_in", "ab_w_in", 0, D, AB_IN), ("ab_out", "ab_w_out", 0, 3072, D),
    ("cq0", "cross_w_q", 0, D, D), ("ckv0", "cross_w_kv", 0, D, 2 * D), ("co0", "cross_w_o", 0, D, D),
    ("ffn2_g0", "ffn2_w_gate", 0, D, DFF), ("ffn2_u0", "ffn2_w_up", 0, D, DFF), ("ffn2_d0", "ffn2_w_down", 0, DFF, D),
    ("ffn1_g1", "ffn1_w_gate", 1, D, DFF), ("ffn1_u1", "ffn1_w_up", 1, D, DFF), ("ffn1_d1", "ffn1_w_down", 1, DFF, D),
    ("cd_in", "cd_w_in", 0, D, CD_IN), ("cd_out", "cd_w_out", 0, 2048, D),
    ("pool_w", "cd_pool_w", 0, 1024, 256),
    ("cq1", "cross_w_q", 1, D, D), ("ckv1", "cross_w_kv", 1, D, 2 * D), ("co1", "cross_w_o", 1, D, D),
    ("ffn2_g1", "ffn2_w_gate", 1, D, DFF), ("ffn2_u1", "ffn2_w_up", 1, D, DFF), ("ffn2_d1", "ffn2_w_down", 1, DFF, D),
]


def panel_w(nm):
    K, N = [(w[3], w[4]) for w in WEIGHTS if w[0] == nm][0]
    KC = K // 128
    if nm.startswith("ffn") and ("_g" in nm or "_u" in nm):
        mw = 256
    else:
        mw = min(512, max(128, (4096 // KC) // 128 * 128))
    return min(mw, N)


class Builder(P):
    def __init__(self, Lp, Ls, flags=None):
        super().__init__()
        self.Lp, self.Ls = Lp, Ls
        self.Lmax = max(Lp, Ls)
        self.flags = flags or {}
        self.T = 512
        nc = self.nc
        self.din = {}

        def ext(name, shape, dt=F32):
            self.din[name] = nc.dram_tensor(name, list(shape), dt, kind="ExternalInput").ap()
            return self.din[name]

        ext("xT_p", [D, Lp]); ext("xT_s", [D, Ls])
        ext("memT_p", [D, NMEM]); ext("memT_s", [D, NMEM])
        ext("cols", [128, NCOL])
        ext("consts", [128, NCONST])
        ext("rope_cos", [128, self.Lmax]); ext("rope_sin", [128, self.Lmax])
        ext("invcnt_p", [4, Lp]); ext("invcnt_s", [4, Ls])
        ext("dftc", [128, 2, 512], BF16)
        ext("tab_p", [Lp, 384], BF16); ext("tab_s", [Ls, 384], BF16)
        ext("w2_p", [Lp // 128, 2 * (Lp // 128)], BF16); ext("w2_s", [Ls // 128, 2 * (Ls // 128)], BF16)
        for nm, src, l, K, N in WEIGHTS:
            ext("w_" + nm, [K, N])
        self.yT = {"p": nc.dram_tensor("yT_p", [D, Lp], F32, kind="ExternalOutput").ap(),
                   "s": nc.dram_tensor("yT_s", [D, Ls], F32, kind="ExternalOutput").ap()}
        self.y_res = {"p": Res("yT_p"), "s": Res("yT_s")}
        self.ws = {}
        self.ws_res = {}
        for nm, src, l, K, N in WEIGHTS:
            self.ws[nm] = nc.dram_tensor("ws_" + nm, [128, (K // 128) * N], BF16, kind="Internal")
            self.ws_res[nm] = Res("ws_" + nm)
        L = self.Lmax
        self.XR = nc.dram_tensor("XR", [D, L], F32, kind="Internal").ap()
        self.XR_res = Res("XR")
        self.MO = nc.dram_tensor("MO", [3072, L], BF16, kind="Internal").ap()
        self.MO_res = Res("MO")
        self.UP = nc.dram_tensor("UP", [D, L], F32, kind="Internal").ap(); self.UP_res = Res("UP")
        self.QT = nc.dram_tensor("QT", [D, L], BF16, kind="Internal").ap(); self.QT_res = Res("QT")
        self.KTd = nc.dram_tensor("KTd", [D, L], BF16, kind="Internal").ap(); self.KTd_res = Res("KTd")
        self.Vd = nc.dram_tensor("Vd", [L, D], BF16, kind="Internal").ap(); self.Vd_res = Res("Vd")
        self.Zd = nc.dram_tensor("Zd", [L, 2048], BF16, kind="Internal").ap(); self.Zd_res = Res("Zd")
        self.ZS = nc.dram_tensor("ZS", [2048, L], BF16, kind="Internal").ap(); self.ZS_res = Res("ZS")
        self.XBC = nc.dram_tensor("XBC", [3072, L], BF16, kind="Internal").ap(); self.XBC_res = Res("XBC")
        self.DT = nc.dram_tensor("DT", [64, L], F32, kind="Internal").ap(); self.DT_res = Res("DT")
        self.DTA = nc.dram_tensor("DTA", [64, L], F32, kind="Internal").ap(); self.DTA_res = Res("DTA")
        self.X2D = nc.dram_tensor("X2D", [L // 128, 8, 2, 128, 128], BF16, kind="Internal").ap(); self.X2D_res = Res("X2D")
        self.YFW = nc.dram_tensor("YFW", [L, 2048], F32, kind="Internal").ap(); self.YFW_res = Res("YFW")

    def sb(self, stack, name, shape, dt):
        self.uid += 1
        return stack.enter_context(self.nc.sbuf_tensor(f"{name}_{self.uid}", list(shape), dt))

    def col(self, name, j=0, rows=128):
        o, w = COLS[name]
        return self.colsb[0:rows, o + j:o + j + 1]

    def bank(self):
        b = self.bank_i
        self.bank_i = (self.bank_i + 1) % 8
        return self.banks[b], self.bank_res[b]

    def build(self):
        nc = self.nc
        with self.es:
            top = self.es
            self.banks = [top.enter_context(nc.psum_tensor(f"bank{i}", [128, 512], F32)) for i in range(8)]
            self.bank_res = [Res(f"bank{i}") for i in range(8)]
            self.bank_i = 0
            self.colsb = self.sb(top, "colsb", [128, NCOL], F32)
            self.cols_res = Res("cols")
            self.dma(self.sp, self.colsb[:], self.din["cols"][:, :], self.clock("cols"), writes=[self.cols_res])
            self.ones_bf = self.sb(top, "ones_bf", [128, 128], BF16)
            self.const_res = Res("const")
            self.op(self.dve, lambda: nc.vector.memset(self.ones_bf[:], 1.0), writes=[self.const_res])
            self.cst = self.sb(top, "cst", [128, NCONST], F32)
            self.dma(self.sp, self.cst[:], self.din["consts"][:, :], self.clock("cst"), writes=[self.const_res])
            self.ident_f = self.cst[:, 0:128]
            self.cst_bf = self.sb(top, "cst_bf", [128, NCONST], BF16)
            self.op(self.dve, lambda: nc.vector.tensor_copy(out=self.cst_bf[:], in_=self.cst[:]), reads=[self.const_res], writes=[self.const_res])
            self.ident_bf = self.cst_bf[:, 0:128]
            self.bd64_bf = self.cst_bf[:, 128:256]
            self.perm_bf = self.cst_bf[:, 256:384]
            self.ones_f = self.sb(top, "ones_f", [128, 128], F32)
            self.op(self.dve, lambda: nc.vector.memset(self.ones_f[:], 1.0), writes=[self.const_res])
            self.one_col = self.sb(top, "onec", [128, 1], F32)
            self.op(self.dve, lambda: nc.vector.memset(self.one_col[:], 1.0), writes=[self.const_res])
            self.eps_col = self.sb(top, "eps", [128, 1], F32)
            self.op(self.dve, lambda: nc.vector.memset(self.eps_col[:], EPS), writes=[self.const_res])
            self.tri = [self.cst[:, 384:512], self.cst[:, 512:640]]
            self.strict = [self.cst[:, 640:768], self.cst[:, 768:896]]
            self.negm = self.sb(top, "negm", [128, 2, 4, 128], BF16)
            for d_ in range(2):
                self.op(self.dve, lambda d_=d_: nc.vector.tensor_scalar(out=self.negm[:, d_, :, :], in0=self.cst[:, 640 + 128 * d_:768 + 128 * d_].unsqueeze(1).broadcast_to([128, 4, 128]),
                                                                      scalar1=-30000.0, scalar2=None, op0=ALU.mult),
                        reads=[self.const_res], writes=[self.const_res])
            self.dftc = self.sb(top, "dftc", [128, 2, 512], BF16)
            self.dma(self.sp, self.dftc[:], self.din["dftc"][:, :, :], self.clock("dftc"), writes=[self.const_res])
            self.init_small()
            self.prep_weights()
            for key, L in (("p", self.Lp), ("s", self.Ls)):
                self.run_seq(key, L)
            self.finish([self.y_res["p"], self.y_res["s"]])
        return nc

    def prep_weights(self):
        nc = self.nc
        with contextlib.ExitStack() as st:
            NB = 2
            stg = [self.sb(st, "wstg", [128, AB_IN], F32) for _ in range(NB)]
            stb = [self.sb(st, "wstb", [128, AB_IN], BF16) for _ in range(NB)]
            stg_res = [Res("wstg") for _ in range(NB)]
            stb_res = [Res("wstb") for _ in range(NB)]
            ldclk = [self.clock(f"wld{i}") for i in range(NB)]
            stclk = [self.clock(f"wst{i}") for i in range(NB)]
            i = 0
            casters = [self.dve, self.act]
            for nm, src, l, K, N in WEIGHTS:
                w = self.din["w_" + nm]
                for kc in range(K // 128):
                    b = i % NB
                    self.dma(self.sp, stg[b][:, 0:N], w[kc * 128:(kc + 1) * 128, :], ldclk[b], writes=[stg_res[b]])
                    E = casters[i % 2]
                    if E is self.act:
                        fn = lambda b=b, N=N: nc.scalar.copy(out=stb[b][:, 0:N], in_=stg[b][:, 0:N])
                    elif E is self.dve:
                        fn = lambda b=b, N=N: nc.vector.tensor_copy(out=stb[b][:, 0:N], in_=stg[b][:, 0:N])
                    else:
                        fn = lambda b=b, N=N: nc.gpsimd.tensor_copy(out=stb[b][:, 0:N], in_=stg[b][:, 0:N])
                    self.op(E, fn, reads=[stg_res[b]], writes=[stb_res[b]])
                    KC = K // 128
                    MW = panel_w(nm)
                    nfull = N // MW
                    dst = bass.AP(self.ws[nm], kc * MW, [[KC * N, 128], [KC * MW, nfull], [1, MW]])
                    self.dma(self.sp, dst, stb[b][:, 0:nfull * MW].rearrange("p (a m) -> p a m", m=MW), stclk[b], reads=[stb_res[b]],
                             pwrites=[self.ws_res[nm]])
                    rem = N - nfull * MW
                    if rem:
                        dst = bass.AP(self.ws[nm], KC * nfull * MW + kc * rem, [[KC * N, 128], [1, rem]])
                        self.dma(self.sp, dst, stb[b][:, nfull * MW:N], stclk[b], reads=[stb_res[b]], pwrites=[self.ws_res[nm]])
                    i += 1
            self.barrier()

    def wstream(self, nm, c0, c1, mw=None):
        return WStream(self, nm, c0, c1, mw)

    def rms_fm(self, src, nch, T, gain, out, src_res, out_res, group=None, scale=1.0, lhsT=None, n_norm=None):
        nc = self.nc
        group = group or nch
        lhsT = lhsT if lhsT is not None else self.ones_bf
        n_norm = n_norm or group * 128
        for g0 in range(0, nch, group):
            bk, bres = self.bank()
            for jj in range(group):
                j = g0 + jj
                self.op(self.act, lambda j=j, jj=jj: nc.scalar.activation(out=self.sq[:, jj % 2, 0:T], in_=src(j), func=AF.Square),
                        reads=src_res, writes=[self.sq_res[jj % 2]])
                self.op(self.pe, lambda jj=jj: nc.tensor.matmul(bk[:, 0:T], lhsT[:], self.sq[:, jj % 2, 0:T],
                                                                 start=(jj == 0), stop=(jj == group - 1)),
                        reads=[self.sq_res[jj % 2], self.const_res], writes=[bres], inc=True)
            self.op(self.act, lambda: nc.scalar.activation(out=bk[:, 0:T], in_=bk[:, 0:T], func=AF.Sqrt,
                                                           bias=self.eps_col[:, 0:1], scale=1.0 / n_norm),
                    reads=[bres, self.const_res], writes=[bres])
            self.op(self.dve, lambda: nc.vector.reciprocal(out=bk[:, 0:T], in_=bk[:, 0:T]),
                    reads=[bres], writes=[bres])
            for jj in range(group):
                j = g0 + jj
                if scale == 1.0:
                    fn = lambda j=j: nc.vector.scalar_tensor_tensor(out=out(j), in0=src(j), scalar=gain(j), in1=bk[:, 0:T],
                                                                    op0=ALU.mult, op1=ALU.mult)
                    self.op(self.dve, fn, reads=list(src_res) + [bres, self.cols_res], writes=out_res)
                else:
                    self.op(self.dve, lambda j=j: nc.vector.tensor_scalar(out=self.tmpn[:, 0:T], in0=src(j), scalar1=gain(j), scalar2=float(scale),
                                                                      op0=ALU.mult, op1=ALU.mult),
                            reads=list(src_res) + [self.cols_res], writes=[self.tmpn_res])
                    self.op(self.dve, lambda j=j: nc.vector.tensor_tensor(out=out(j), in0=self.tmpn[:, 0:T], in1=bk[:, 0:T], op=ALU.mult),
                            reads=[self.tmpn_res, bres], writes=out_res)

    def linear_fm(self, ws, rhs, rhs_res, T, evac, KC=None):
        nc = self.nc
        m = 0
        for pi in range(ws.npanels):
            wt, wres, mw = ws.get(pi)
            kcs = list(KC) if KC is not None else list(range(ws.KC))
            for mm in range(0, mw, 128):
                mwid = min(128, mw - mm)
                bk, bres = self.bank()
                for kc in kcs:
                    self.op(self.pe, lambda kc=kc, mm=mm, mwid=mwid: nc.tensor.matmul(bk[0:mwid, 0:T], wt[:, kc, mm:mm + mwid], rhs(kc),
                                                                                    start=(kc == kcs[0]), stop=(kc == kcs[-1])),
                            reads=[wres] + list(rhs_res), writes=[bres], inc=(kc == kcs[-1]))
                evac(m, bk, bres, mwid)
                m += 1
        ws.close()

    def ffn(self, l, which, T):
        nc = self.nc
        x, xres = self.x, self.x_res
        gname = f"ffn{which}_norm{l}"
        self.rms_fm(lambda j: x[:, j, 0:T], 8, T, lambda j: self.col(gname, j), lambda j: self.xn[:, j, 0:T], [xres], [self.xn_res])
        if self.flags.get("dbg") == 1:
            for j in range(8):
                self.op(self.dve, lambda j=j: nc.vector.tensor_copy(out=x[:, j, 0:T], in_=self.xn[:, j, 0:T]), reads=[self.xn_res], writes=[xres])
            return
        wg = self.wstream(f"ffn{which}_g{l}", 0, DFF, 256)
        wu = self.wstream(f"ffn{which}_u{l}", 0, DFF, 256)
        wd = self.wstream(f"ffn{which}_d{l}", 0, D)
        wg.prime(); wu.prime()
        m = 0
        for pi in range(wg.npanels):
            gt, gres, mw = wg.get(pi)
            ut, ures, _ = wu.get(pi)
            for mm in range(0, mw, 128):
                bg, bgres = self.bank()
                bu, bures = self.bank()
                for kc in range(8):
                    self.op(self.pe, lambda kc=kc, mm=mm: nc.tensor.matmul(bg[:, 0:T], gt[:, kc, mm:mm + 128], self.xn[:, kc, 0:T],
                                                                         start=(kc == 0), stop=(kc == 7)),
                            reads=[gres, self.xn_res], writes=[bgres], inc=(kc == 7))
                for kc in range(8):
                    self.op(self.pe, lambda kc=kc, mm=mm: nc.tensor.matmul(bu[:, 0:T], ut[:, kc, mm:mm + 128], self.xn[:, kc, 0:T],
                                                                         start=(kc == 0), stop=(kc == 7)),
                            reads=[ures, self.xn_res], writes=[bures], inc=(kc == 7))
                sb_i = m % 2
                self.op(self.act, lambda sb_i=sb_i: nc.scalar.activation(out=self.sg[:, sb_i, 0:T], in_=bg[:, 0:T], func=AF.Silu),
                        reads=[bgres], writes=[self.sg_res[sb_i]])
                if self.flags.get("dbg4"):
                    self.op(self.dve, lambda m=m, sb_i=sb_i: nc.vector.tensor_copy(out=self.h[:, m, 0:T], in_=bu[:, 0:T]),
                            reads=[bures, self.sg_res[sb_i]], writes=[self.h_res])
                else:
                    self.op(self.dve, lambda m=m, sb_i=sb_i: nc.vector.tensor_tensor(out=self.h[:, m, 0:T], in0=bu[:, 0:T], in1=self.sg[:, sb_i, 0:T], op=ALU.mult),
                            reads=[bures, self.sg_res[sb_i]], writes=[self.h_res])
                m += 1
            if pi == wg.npanels - 1:
                wg.close(); wu.close()
                wd.prime()
        if self.flags.get("dbg") == 2:
            for j in range(8):
                self.op(self.dve, lambda j=j: nc.vector.tensor_copy(out=x[:, j, 0:T], in_=self.h[:, j, 0:T]), reads=[self.h_res], writes=[xres])
            return

        def evac(j, bk, bres, mwid):
            self.op(self.dve, lambda: nc.vector.scalar_tensor_tensor(out=x[:, j, 0:T], in0=bk[:, 0:T], scalar=0.5, in1=x[:, j, 0:T],
                                                                     op0=ALU.mult, op1=ALU.add),
                    reads=[bres, xres], writes=[xres])
        self.linear_fm(wd, lambda kc: self.h[:, kc, 0:T], [self.h_res], T, evac)

    def cross_prep(self, key, l, st):
        nc = self.nc
        M = NMEM
        memf = self.sb(st, "memf", [128, 8, M], F32)
        memf_res = Res("memf")
        kf = self.sb(st, "kf", [128, 8, M], F32)
        kf_res = Res("kf")
        mn = self.sb(st, "mn", [128, 8, M], BF16)
        mn_res = Res("mn")
        self.dma(self.sp, memf[:], self.din["memT_" + key].rearrange("(j p) m -> p j m", p=128), self.clock("memld"), writes=[memf_res])
        self.rms_fm(lambda j: memf[:, j, :], 8, M, lambda j: self.col(f"cross_mem_norm{l}", j), lambda j: mn[:, j, :], [memf_res], [mn_res])
        wk = self.wstream(f"ckv{l}", 0, D)

        def evk(j, bk, bres, mwid):
            self.op(self.act, lambda: nc.scalar.copy(out=kf[:, j, :], in_=bk[:, 0:M]), reads=[bres], writes=[kf_res])
        self.linear_fm(wk, lambda kc: mn[:, kc, :], [mn_res], M, evk)
        self.rms_fm(lambda j: kf[:, j, :], 8, M, lambda j: self.col(f"cross_k_norm{l}", j % 2), lambda j: self.KT[:, j, :], [kf_res], [self.KT_res], group=2)
        wv = self.wstream(f"ckv{l}", D, 2 * D)
        for pi in range(wv.npanels):
            vt, vres, mw = wv.get(pi)
            for mc in range(2):
                bk, bres = self.bank()
                for kc in range(8):
                    self.op(self.pe, lambda kc=kc, mc=mc: nc.tensor.matmul(bk[:, 0:mw], mn[:, kc, mc * 128:(mc + 1) * 128], vt[:, kc, 0:mw],
                                                                         start=(kc == 0), stop=(kc == 7)),
                            reads=[vres, mn_res], writes=[bres], inc=(kc == 7))
                self.op(self.act, lambda mc=mc, pi=pi: nc.scalar.copy(out=self.Vm[:, mc, pi * 512:pi * 512 + mw], in_=bk[:, 0:mw]),
                        reads=[bres], writes=[self.Vm_res])
        wv.close()

    def cross(self, l, T):
        nc = self.nc
        x, xres = self.x, self.x_res
        self.rms_fm(lambda j: x[:, j, 0:T], 8, T, lambda j: self.col(f"cross_norm{l}", j), lambda j: self.xn[:, j, 0:T], [xres], [self.xn_res])
        wq = self.wstream(f"cq{l}", 0, D)
        qf, qf_res = self.big, self.big_res

        def evq(j, bk, bres, mwid):
            self.op(self.act, lambda: nc.scalar.copy(out=qf[:, j, 0:T], in_=bk[:, 0:T]), reads=[bres], writes=[qf_res])
        self.linear_fm(wq, lambda kc: self.xn[:, kc, 0:T], [self.xn_res], T, evq)
        qn, qn_res = self.h, self.h_res
        self.rms_fm(lambda j: qf[:, j, 0:T], 8, T, lambda j: self.col(f"cross_q_norm{l}", j % 2), lambda j: qn[:, j, 0:T], [qf_res], [qn_res],
                    group=2, scale=1.0 / 16.0)
        wo = self.wstream(f"co{l}", 0, D)
        wo.prime()
        oT, oT_res = self.xn, self.xn_res
        for hd in range(4):
            ebuf = []
            for mc in range(2):
                bk, bres = self.bank()
                for dc in range(2):
                    self.op(self.pe, lambda dc=dc, mc=mc: nc.tensor.matmul(bk[:, 0:T], self.KT[:, 2 * hd + dc, mc * 128:(mc + 1) * 128], qn[:, 2 * hd + dc, 0:T],
                                                                         start=(dc == 0), stop=(dc == 1)),
                            reads=[self.KT_res, qn_res], writes=[bres], inc=(dc == 1))
                self.op(self.act, lambda mc=mc: nc.scalar.activation(out=self.ex[:, mc, 0:T], in_=bk[:, 0:T], func=AF.Exp),
                        reads=[bres], writes=[self.ex_res[mc]])
            bd, bdres = self.bank()
            for mc in range(2):
                self.op(self.pe, lambda mc=mc: nc.tensor.matmul(bd[:, 0:T], self.ones_bf[:], self.ex[:, mc, 0:T], start=(mc == 0), stop=(mc == 1)),
                        reads=[self.ex_res[mc], self.const_res], writes=[bdres], inc=(mc == 1))
            self.op(self.dve, lambda: nc.vector.reciprocal(out=self.rstd[:, 0:T], in_=bd[:, 0:T]), reads=[bdres], writes=[self.rstd_res])
            for ec in range(2):
                bo, bores = self.bank()
                for mc in range(2):
                    self.op(self.pe, lambda mc=mc, ec=ec: nc.tensor.matmul(bo[:, 0:T], self.Vm[:, mc, hd * 256 + ec * 128:hd * 256 + (ec + 1) * 128], self.ex[:, mc, 0:T],
                                                                         start=(mc == 0), stop=(mc == 1)),
                            reads=[self.ex_res[mc], self.Vm_res], writes=[bores], inc=(mc == 1))
                self.op(self.dve, lambda ec=ec: nc.vector.tensor_tensor(out=oT[:, 2 * hd + ec, 0:T], in0=bo[:, 0:T], in1=self.rstd[:, 0:T], op=ALU.mult),
                        reads=[bores, self.rstd_res], writes=[oT_res])

        def evo(j, bk, bres, mwid):
            self.op(self.dve, lambda: nc.vector.tensor_tensor(out=x[:, j, 0:T], in0=bk[:, 0:T], in1=x[:, j, 0:T], op=ALU.add),
                    reads=[bres, xres], writes=[xres])
        self.linear_fm(wo, lambda kc: oT[:, kc, 0:T], [oT_res], T, evo)

    def mixer_out(self, wname, KC, t0, T):
        nc = self.nc
        x, xres = self.x, self.x_res
        ws = self.wstream(wname, 0, D)
        ws.prime()
        mo, mo_res = self.h, self.h_res
        self.dma(self.sp, mo[:, 0:KC, 0:T], self.MO[0:KC * 128, t0:t0 + T].rearrange("(j p) t -> p j t", p=128), self.clock("mold"),
                 reads=[self.MO_res], writes=[mo_res])

        def ev(j, bk, bres, mwid):
            self.op(self.dve, lambda: nc.vector.tensor_tensor(out=x[:, j, 0:T], in0=bk[:, 0:T], in1=x[:, j, 0:T], op=ALU.add),
                    reads=[bres, xres], writes=[xres])
        kcs = None
        if self.flags.get("noattn"): kcs = range(0, 8)
        if self.flags.get("nopool"): kcs = range(8, 16)
        if wname == "ab_out":
            kcs = None
            if self.flags.get("nossd"): kcs = range(0, 8)
            if self.flags.get("nofnet"): kcs = range(8, 24)
        self.linear_fm(ws, lambda kc: mo[:, kc, 0:T], [mo_res], T, ev, KC=kcs)

    def init_small(self):
        nc = self.nc
        top = self.es
        self.neglam = self.sb(top, "neglam", [128, 1], F32)
        self.small_res = Res("small")
        prod = self.sb(top, "lamprod", [128, 2], F32)
        o, _ = COLS["lam"]
        cb = self.colsb
        self.op(self.dve, lambda: nc.vector.tensor_tensor(out=prod[:, 0:1], in0=cb[:, o:o + 1], in1=cb[:, o + 1:o + 2], op=ALU.mult),
                reads=[self.cols_res], writes=[self.small_res])
        self.op(self.dve, lambda: nc.vector.tensor_tensor(out=prod[:, 1:2], in0=cb[:, o + 2:o + 3], in1=cb[:, o + 3:o + 4], op=ALU.mult),
                reads=[self.cols_res, self.small_res], writes=[self.small_res])
        bk, bres = self.bank()
        self.op(self.pe, lambda: nc.tensor.matmul(bk[:, 0:2], self.ones_f[:], prod[:, 0:2], start=True, stop=True),
                reads=[self.small_res, self.const_res], writes=[bres])
        ex = self.sb(top, "lamex", [128, 2], F32)
        self.op(self.act, lambda: nc.scalar.activation(out=ex[:, 0:2], in_=bk[:, 0:2], func=AF.Exp), reads=[bres], writes=[self.small_res])
        self.op(self.dve, lambda: nc.vector.scalar_tensor_tensor(out=self.neglam[:, 0:1], in0=ex[:, 1:2], scalar=-LAMBDA_INIT1, in1=ex[:, 0:1],
                                                                 op0=ALU.add, op1=ALU.subtract),
                reads=[self.small_res], writes=[self.small_res])
        self.acol = self.sb(top, "acol", [128, 1], F32)
        oa, _ = COLS["a_log"]
        self.op(self.act, lambda: nc.scalar.activation(out=self.acol[:, 0:1], in_=cb[:, oa:oa + 1], func=AF.Exp), reads=[self.cols_res], writes=[self.small_res])
        self.op(self.dve, lambda: nc.vector.tensor_scalar(out=self.acol[:, 0:1], in0=self.acol[:, 0:1], scalar1=-1.0, scalar2=None, op0=ALU.mult),
                reads=[self.small_res], writes=[self.small_res])
        self.subrow = self.sb(top, "subrow", [128, 128], F32)
        o2, _ = COLS["sub_norm_row"]
        self.op(self.dve, lambda: nc.vector.tensor_scalar(out=self.subrow[:], in0=cb[:, o2:o2 + 128], scalar1=1.0 - LAMBDA_INIT1, scalar2=None, op0=ALU.mult),
                reads=[self.cols_res], writes=[self.small_res])

    def mixer_in_ab(self, key, t0, T, first, last):
        nc = self.nc
        x, xres = self.x, self.x_res
        self.rms_fm(lambda j: x[:, j, 0:T], 8, T, lambda j: self.col("mix_norm0", j), lambda j: self.xn[:, j, 0:T], [xres], [self.xn_res])
        rhs = lambda kc: self.xn[:, kc, 0:T]
        h, h_res = self.h, self.h_res
        nb = T // 128
        def ev_uf(j, bk, bres, mwid):
            self.op(self.act, lambda: nc.scalar.copy(out=h[:, j, 0:T], in_=bk[:, 0:T]), reads=[bres], writes=[h_res])
        self.linear_fm(self.wstream("ab_in", 0, 1024), rhs, [self.xn_res], T, ev_uf)
        zview = bass.AP(h, 8 * T, [[24 * T, 128], [2048, nb], [1, 2048]])
        for tb in range(nb):
            for hd in range(4):
                bk, bres = self.bank()
                for k2 in range(2):
                    self.op(self.pe, lambda k2=k2, tb=tb, hd=hd: nc.tensor.matmul(bk[:, 0:512], h[:, 2 * hd + k2, tb * 128:(tb + 1) * 128], self.dftc[:, k2, :],
                                                                                start=(k2 == 0), stop=(k2 == 1)),
                            reads=[h_res, self.const_res], writes=[bres], inc=(k2 == 1))
                self.op(self.act, lambda tb=tb, hd=hd: nc.scalar.copy(out=zview[:, tb, hd * 512:(hd + 1) * 512], in_=bk[:, 0:512]), reads=[bres], writes=[h_res])
        self.dma(self.sp, self.Zd[t0:t0 + T, :].rearrange("(tb p) f -> p tb f", p=128), zview, self.stclk[0], reads=[h_res], pwrites=[self.Zd_res])
        def ev_z(j, bk, bres, mwid):
            self.op(self.act, lambda: nc.scalar.activation(out=h[:, j, 0:T], in_=bk[:, 0:T], func=AF.Silu), reads=[bres], writes=[h_res])
        self.linear_fm(self.wstream("ab_in", 1024, 3072), rhs, [self.xn_res], T, ev_z)
        self.dma(self.sp, self.ZS[:, t0:t0 + T].rearrange("(j p) t -> p j t", p=128), h[:, 0:16, 0:T], self.stclk[1], reads=[h_res], pwrites=[self.ZS_res])
        cb, cb_res = self.cb, self.cb_res
        if first:
            self.op(self.dve, lambda: nc.vector.memset(cb[:], 0.0), writes=cb_res)
        a_col = 2 if first else 0
        b_col = T + 2 if last else T
        ow, _ = COLS["conv_w"]

        def ev_x(j, bk, bres, mwid):
            self.op(self.act, lambda: nc.scalar.copy(out=cb[:, j, 4:4 + T], in_=bk[:, 0:T]), reads=[bres], writes=[cb_res[j]])
            acc = self.cacc[j % 2]; acc_res = self.cacc_res[j % 2]
            self.op(self.dve, lambda: nc.vector.tensor_scalar(out=acc[:, 0:T + 2], in0=cb[:, j, 0:T + 2], scalar1=self.colsb[:, ow + j:ow + j + 1], scalar2=None, op0=ALU.mult),
                    reads=[cb_res[j], self.cols_res], writes=[acc_res])
            for k in range(1, 5):
                self.op(self.dve, lambda k=k: nc.vector.scalar_tensor_tensor(out=acc[:, 0:T + 2], in0=cb[:, j, k:k + T + 2], scalar=self.colsb[:, ow + 24 * k + j:ow + 24 * k + j + 1],
                                                                           in1=acc[:, 0:T + 2], op0=ALU.mult, op1=ALU.add),
                        reads=[cb_res[j], self.cols_res, acc_res], writes=[acc_res])
            xo = self.xo; xo_res = self.xo_res
            self.op(self.act, lambda: nc.scalar.activation(out=xo[:, j % 8, 0:T + 2], in_=acc[:, 0:T + 2], func=AF.Silu, bias=self.col("conv_b", j)),
                    reads=[acc_res, self.cols_res], writes=[xo_res])
            self.op(self.dve, lambda: nc.vector.tensor_copy(out=cb[:, j, 0:4], in_=cb[:, j, T:T + 4]), reads=[cb_res[j]], writes=[cb_res[j]])
            if j % 8 == 7:
                j0 = j - 7
                self.dma(self.sp, self.XBC[j0 * 128:(j0 + 8) * 128, t0 - 2 + a_col:t0 - 2 + b_col].rearrange("(j p) t -> p j t", p=128), xo[:, :, a_col:b_col],
                         self.stclk[2], reads=[xo_res], pwrites=[self.XBC_res])
        self.linear_fm(self.wstream("ab_in", 3072, 6144), rhs, [self.xn_res], T, ev_x)
        big, big_res = self.big, self.big_res

        def ev_dt(j, bk, bres, mwid):
            R = 64
            self.op(self.dve, lambda: nc.vector.tensor_scalar(out=big[0:R, 0, 0:T], in0=bk[0:R, 0:T], scalar1=self.col("dt_bias", 0, R), scalar2=None, op0=ALU.add),
                    reads=[bres, self.cols_res], writes=[big_res])
            self.op(self.dve, lambda: nc.vector.scalar_tensor_tensor(out=big[0:R, 1, 0:T], in0=big[0:R, 0, 0:T], scalar=-1.0, in1=big[0:R, 0, 0:T], op0=ALU.mult, op1=ALU.max),
                    reads=[big_res], writes=[big_res])
            self.op(self.act, lambda: nc.scalar.activation(out=big[0:R, 2, 0:T], in_=big[0:R, 1, 0:T], func=AF.Exp, scale=-1.0), reads=[big_res], writes=[big_res])
            self.op(self.act, lambda: nc.scalar.activation(out=big[0:R, 3, 0:T], in_=big[0:R, 2, 0:T], func=AF.Ln, bias=self.one_col[0:R, 0:1]), reads=[big_res, self.const_res], writes=[big_res])
            self.op(self.dve, lambda: nc.vector.scalar_tensor_tensor(out=big[0:R, 4, 0:T], in0=big[0:R, 0, 0:T], scalar=0.0, in1=big[0:R, 3, 0:T], op0=ALU.max, op1=ALU.add),
                    reads=[big_res], writes=[big_res])
            self.op(self.dve, lambda: nc.vector.tensor_scalar(out=big[0:R, 5, 0:T], in0=big[0:R, 4, 0:T], scalar1=self.acol[0:R, 0:1], scalar2=None, op0=ALU.mult),
                    reads=[big_res, self.small_res], writes=[big_res])
        self.linear_fm(self.wstream("ab_in", 6144, 6208), rhs, [self.xn_res], T, ev_dt)
        self.dma(self.sp, self.DT[:, t0:t0 + T], big[0:64, 4, 0:T], self.stclk[3], reads=[big_res], pwrites=[self.DT_res])
        self.dma(self.sp, self.DTA[:, t0:t0 + T], big[0:64, 5, 0:T], self.stclk[3], reads=[big_res], pwrites=[self.DTA_res])

    def pass_fnet(self, key, L):
        nc = self.nc
        N2 = L // 128
        tab = self.din["tab_" + key]
        with contextlib.ExitStack() as st:
            zl = [self.sb(st, "zl", [128, 2048], BF16) for _ in range(2)]; zl_res = [Res("zl0"), Res("zl1")]
            tl = [self.sb(st, "tl", [128, 384], BF16) for _ in range(2)]; tl_res = [Res("tl0"), Res("tl1")]
            x2s = [self.sb(st, "x2s", [128, 8, 2, 128], BF16) for _ in range(2)]; x2_res = [Res("x2s0"), Res("x2s1")]
            zclk = [self.clock("zl0"), self.clock("zl1")]; x2clk = [self.clock("x2a"), self.clock("x2b")]
            zsrc = self.Zd[0:L, :].rearrange("(l1 l2) f -> l1 l2 f", l2=N2)
            tsrc = tab.rearrange("(l1 l2) f -> l1 l2 f", l2=N2)
            for l2 in range(N2):
                b = l2 % 2
                self.dma(self.sp, zl[b][:], zsrc[:, l2, :], zclk[b], reads=[self.Zd_res], writes=[zl_res[b]])
                self.dma(self.sp, tl[b][:], tsrc[:, l2, :], zclk[b], writes=[tl_res[b]])
                for cp in range(4):
                    bk, bres = self.bank()
                    for ci in range(2):
                        cj = 2 * cp + ci
                        hd, half = cj // 2, cj % 2
                        a0 = hd * 512 + half * 128
                        self.op(self.pe, lambda b=b, a0=a0, ci=ci: nc.tensor.matmul(bk[:, ci * 256:(ci + 1) * 256], zl[b][:, a0:a0 + 128], tl[b][:, 128:384], start=True, stop=False),
                                reads=[zl_res[b], tl_res[b]], writes=[bres], inc=False)
                        self.op(self.pe, lambda b=b, a0=a0, ci=ci: nc.tensor.matmul(bk[:, ci * 256:(ci + 1) * 256], zl[b][:, a0 + 256:a0 + 384], tl[b][:, 0:256], start=False, stop=True),
                                reads=[zl_res[b], tl_res[b]], writes=[bres], inc=(ci == 1))
                    E = self.act if cp % 2 == 0 else self.dve
                    if E is self.act:
                        self.op(E, lambda b=b, cp=cp: nc.scalar.copy(out=x2s[b][:, 2 * cp:2 * cp + 2, :, :], in_=bk[:, 0:512]), reads=[bres], writes=[x2_res[b]])
                    else:
                        self.op(E, lambda b=b, cp=cp: nc.vector.tensor_copy(out=x2s[b][:, 2 * cp:2 * cp + 2, :, :], in_=bk[:, 0:512]), reads=[bres], writes=[x2_res[b]])
                for hf in range(2):
                    self.dma(self.sp, self.X2D[l2, 4 * hf:4 * hf + 4].rearrange("cj ri c k -> c cj ri k"), x2s[b][:, 4 * hf:4 * hf + 4, :, :], x2clk[b],
                             reads=[x2_res[b]], pwrites=[self.X2D_res])
            self.barrier()
        with contextlib.ExitStack() as st:
            xc = self.sb(st, "xc", [128, 2, 128, 128], BF16); xc_res = Res("xc")
            w2 = self.sb(st, "w2", [128, 2 * N2], BF16); w2_res = Res("w2")
            yt = [self.sb(st, "yt", [128, L], BF16) for _ in range(2)]; yt_res = [Res("yt0"), Res("yt1")]
            xclk = self.clock("xc"); yclk = [self.clock("yt0"), self.clock("yt1")]
            self.dma(self.sp, w2[0:N2, :], self.din["w2_" + key][:, :], self.clock("w2"), writes=[w2_res])
            G = min(128, 512 // N2)
            scale = 1.0 / math.sqrt(256.0 * L)
            for cj in range(8):
                self.dma(self.sp, xc[0:N2], self.X2D[0:N2, cj], xclk, reads=[self.X2D_res], writes=[xc_res])
                y = yt[cj % 2]; yres = yt_res[cj % 2]
                for k0 in range(0, 128, G):
                    bk, bres = self.bank()
                    for g in range(G):
                        k1 = k0 + g
                        self.op(self.pe, lambda k1=k1, g=g: nc.tensor.matmul(bk[:, g * N2:(g + 1) * N2], xc[0:N2, 0, :, k1], w2[0:N2, 0:N2], start=True, stop=False),
                                reads=[xc_res, w2_res], writes=[bres], inc=False)
                        self.op(self.pe, lambda k1=k1, g=g: nc.tensor.matmul(bk[:, g * N2:(g + 1) * N2], xc[0:N2, 1, :, k1], w2[0:N2, N2:2 * N2], start=False, stop=True),
                                reads=[xc_res, w2_res], writes=[bres], inc=(g == G - 1))
                    outap = bass.AP(y, k0, [[L, 128], [1, G], [128, N2]])
                    inap = bass.AP(bk, 0, [[512, 128], [N2, G], [1, N2]])
                    self.op(self.act, lambda outap=outap, inap=inap: nc.scalar.activation(out=outap, in_=inap, func=AF.Copy, scale=scale), reads=[bres], writes=[yres])
                self.dma(self.sp, self.MO[cj * 128:(cj + 1) * 128, 0:L], y[:, 0:L], yclk[cj % 2], reads=[yres], pwrites=[self.MO_res])
            self.barrier()

    def pass_ssd(self, key, L):
        nc = self.nc
        ST = min(512, L)
        cps = ST // 128
        nst = L // ST
        with contextlib.ExitStack() as st:
            xbc = self.sb(st, "xbc", [128, 24, ST], BF16); xbc_res = Res("xbc")
            dtf = self.sb(st, "dtf", [128, 2, ST], F32); dtf_res = Res("dtf")
            zs = self.sb(st, "zs", [128, 16, ST], BF16); zs_res = Res("zs")
            yfw = self.sb(st, "yfw", [128, 2048], F32); yfw_res = Res("yfw")
            state = self.sb(st, "state", [128, 4, 512], F32); state_res = [Res(f"state{g}") for g in range(4)]
            stbf = self.sb(st, "stbf", [128, 4, 512], BF16); stbf_res = [Res(f"stbf{g}") for g in range(4)]
            xdt2 = [self.sb(st, "xdt", [128, 2048], BF16) for _ in range(2)]; xdt_res2 = [Res("xdt0"), Res("xdt1")]
            xw = self.sb(st, "xw", [128, 2048], BF16); xw_res = Res("xw")
            BT2 = [self.sb(st, "BT", [128, 512], BF16) for _ in range(2)]; BT_res2 = [Res("BT0"), Res("BT1")]
            dtT2 = [self.sb(st, "dtT", [128, 128], F32) for _ in range(2)]; dtT_res2 = [Res("dtT0"), Res("dtT1")]
            GT2 = [self.sb(st, "GT", [128, 4, 128], F32) for _ in range(2)]; GT_res2 = [Res("GT0"), Res("GT1")]
            EXPS2 = [self.sb(st, "EXPS", [128, 96], F32) for _ in range(2)]; EXPS_res2 = [Res("EXPS0"), Res("EXPS1")]
            RA = self.sb(st, "RA", [128, 32, 128], F32); RA_res = Res("RA")
            DEC = [self.sb(st, "DEC", [128, 512], F32) for _ in range(2)]; DEC_res = [Res("DEC0"), Res("DEC1")]
            STb = [self.sb(st, "STb", [128, 4, 128], BF16) for _ in range(2)]; STb_res = [Res("STb0"), Res("STb1")]
            Y2 = [self.sb(st, "Y", [128, 2048], F32) for _ in range(2)]; Y_res2 = [Res("Y0"), Res("Y1")]
            tmpY2 = [self.sb(st, "tmpY", [128, 512], F32) for _ in range(2)]; tmpY_res2 = [Res("tmpY0"), Res("tmpY1")]
            tmpS = self.sb(st, "tmpS", [128, 512], F32); tmpS_res = Res("tmpS")
            yg = self.sb(st, "yg", [128, 16, 128], F32); yg_res = Res("yg")
            ysT = self.sb(st, "ysT", [128, 16, ST], BF16); ysT_res = Res("ysT")
            negf = self.sb(st, "negf", [128, 2, 4, 128], F32);
            for d_ in range(2):
                self.op(self.dve, lambda d_=d_: nc.vector.tensor_scalar(out=negf[:, d_, :, :], in0=self.strict[d_].unsqueeze(1).broadcast_to([128, 4, 128]),
                                                                      scalar1=-30000.0, scalar2=None, op0=ALU.mult),
                        reads=[self.const_res], writes=[self.const_res])
            ldx = self.clock("ssd_ldx"); ldd = self.clock("ssd_ldd"); ldz = self.clock("ssd_ldz"); ldy = self.clock("ssd_ldy")
            sty = self.clock("ssd_sty"); sto = self.clock("ssd_sto")

            def bf_view(bk, ncols):
                return bk[:, 0:ncols // 2].bitcast(BF16)

            di = 0
            chunk_i = 0
            for d in range(2):
                for g in range(4):
                    self.op(self.dve, lambda g=g: nc.vector.memset(state[:, g, :], 0.0), writes=[state_res[g]])
                    self.op(self.dve, lambda g=g: nc.vector.memset(stbf[:, g, :], 0.0), writes=[stbf_res[g]])
                st_order = range(nst) if d == 0 else range(nst - 1, -1, -1)
                for si in st_order:
                    t0 = si * ST
                    self.dma(self.sp, xbc[:], self.XBC[:, t0:t0 + ST].rearrange("(j p) t -> p j t", p=128), ldx, reads=[self.XBC_res], writes=[xbc_res])
                    self.dma(self.sp, dtf[0:64, 0, :], self.DT[:, t0:t0 + ST], ldd, reads=[self.DT_res], writes=[dtf_res])
                    self.dma(self.sp, dtf[0:64, 1, :], self.DTA[:, t0:t0 + ST], ldd, reads=[self.DTA_res], writes=[])
                    dtf_res.w[ldd] = ldd.n
                    if d == 1:
                        self.dma(self.sp, zs[:], self.ZS[:, t0:t0 + ST].rearrange("(j p) t -> p j t", p=128), ldz, reads=[self.ZS_res], writes=[zs_res])
                    c_order = range(cps) if d == 0 else range(cps - 1, -1, -1)
                    for ci in c_order:
                        cs = ci * 128
                        c0 = t0 + cs
                        pc = chunk_i % 2; chunk_i += 1
                        xdt, xdt_res = xdt2[pc], xdt_res2[pc]
                        BT, BT_res = BT2[pc], BT_res2[pc]
                        dtT, dtT_res = dtT2[pc], dtT_res2[pc]
                        GT, GT_res = GT2[pc], GT_res2[pc]
                        EXPS, EXPS_res = EXPS2[pc], EXPS_res2[pc]
                        Y, Y_res = Y2[pc], Y_res2[pc]
                        tmpY, tmpY_res = tmpY2[pc], tmpY_res2[pc]
                        if d == 1:
                            self.dma(self.sp, yfw[:], self.YFW[c0:c0 + 128, :], ldy, reads=[self.YFW_res], writes=[yfw_res])
                        bkx = []
                        for half in range(2):
                            bk, bres = self.bank()
                            bv = bf_view(bk, 1024)
                            for jj in range(8):
                                j = half * 8 + jj
                                self.op(self.pe, lambda bv=bv, jj=jj, j=j: nc.tensor.transpose(bv[:, jj * 128:(jj + 1) * 128], xbc[:, j, cs:cs + 128], self.ident_bf),
                                        reads=[xbc_res, self.const_res], writes=[bres], inc=(jj == 7))
                            bkx.append((bv, bres))
                        bkB, bBres = self.bank()
                        bvB = bf_view(bkB, 512)
                        for g in range(4):
                            self.op(self.pe, lambda g=g: nc.tensor.transpose(bvB[:, g * 128:(g + 1) * 128], xbc[:, 16 + g, cs:cs + 128], self.ident_bf),
                                    reads=[xbc_res, self.const_res], writes=[bBres], inc=(g == 3))
                        bkD, bDres = self.bank()
                        for q in range(2):
                            self.op(self.pe, lambda q=q: nc.tensor.transpose(bkD[:, q * 64:(q + 1) * 64], dtf[0:64, q, cs:cs + 128], self.cst[0:64, 0:64]),
                                    reads=[dtf_res, self.const_res], writes=[bDres], inc=(q == 1))
                        self.op(self.act, lambda: nc.scalar.copy(out=dtT[:, 0:128], in_=bkD[:, 0:128]), reads=[bDres], writes=[dtT_res])
                        self.op(self.act, lambda: nc.scalar.copy(out=BT[:, 0:512], in_=bvB[:, 0:512]), reads=[bBres], writes=[BT_res])
                        for half in range(2):
                            bv, bres = bkx[half]
                            h0 = d * 32 + half * 16
                            self.op(self.dve, lambda bv=bv, half=half, h0=h0: nc.vector.tensor_tensor(
                                out=xdt[:, half * 1024:(half + 1) * 1024].rearrange("p (h e) -> p h e", e=64),
                                in0=bv[:, 0:1024].rearrange("p (h e) -> p h e", e=64),
                                in1=dtT[:, h0:h0 + 16].unsqueeze(2).broadcast_to([128, 16, 64]), op=ALU.mult),
                                reads=[bres, dtT_res], writes=[xdt_res])
                        dta = dtT[:, 64 + d * 32:64 + d * 32 + 32]
                        bkG, bGres = self.bank()
                        for g in range(4):
                            self.op(self.pe, lambda g=g: nc.tensor.matmul(bkG[:, g * 128:(g + 1) * 128], xbc[:, 16 + g, cs:cs + 128], xbc[:, 20 + g, cs:cs + 128], start=True, stop=True),
                                    reads=[xbc_res], writes=[bGres], inc=(g == 3))
                        self.op(self.act, lambda: nc.scalar.copy(out=GT[:, :, :], in_=bkG[:, 0:512]), reads=[bGres], writes=[GT_res])
                        bkT, bTres = self.bank()
                        for q, lh in enumerate((self.tri[d], self.strict[d], self.ones_f[:, :])):
                            self.op(self.pe, lambda q=q, lh=lh: nc.tensor.matmul(bkT[:, q * 32:(q + 1) * 32], lh, dta, start=True, stop=True),
                                    reads=[dtT_res, self.const_res], writes=[bTres], inc=(q == 2))
                        self.op(self.act, lambda: nc.scalar.activation(out=EXPS[:, 0:96], in_=bkT[:, 0:96], func=AF.Exp), reads=[bTres], writes=[EXPS_res])
                        self.op(self.pool, lambda: nc.gpsimd.tensor_tensor(out=RA[:, :, :], in0=self.tri[d].unsqueeze(1).broadcast_to([128, 32, 128]),
                                                                           in1=dta.unsqueeze(2).broadcast_to([128, 32, 128]), op=ALU.mult),
                                reads=[dtT_res, self.const_res], writes=[RA_res])
                        self.op(self.pool, lambda: nc.gpsimd.tensor_tensor(out=xw[:, :].rearrange("p (h e) -> p h e", e=64), in0=xdt[:, :].rearrange("p (h e) -> p h e", e=64),
                                                                           in1=EXPS[:, 32:64].unsqueeze(2).broadcast_to([128, 32, 64]), op=ALU.mult),
                                reads=[xdt_res, EXPS_res], writes=[xw_res])
                        for half in range(2):
                            yI = [self.bank(), self.bank()]
                            yT_ = [self.bank(), self.bank()]
                            for gi in range(2):
                                g = 2 * half + gi
                                self.op(self.pe, lambda g=g, gi=gi: nc.tensor.matmul(yT_[gi][0][:, 0:512], xbc[:, 20 + g, cs:cs + 128], stbf[:, g, :], start=True, stop=True),
                                        reads=[xbc_res, stbf_res[g]], writes=[yT_[gi][1]])
                            for hb in range(4):
                                h0 = 16 * half + 4 * hb
                                g = h0 // 8
                                bkS, bSres = self.bank()
                                self.op(self.pe, lambda h0=h0: nc.tensor.matmul(bkS[:, 0:512], self.strict[d], RA[:, h0:h0 + 4, :], start=True, stop=False),
                                        reads=[RA_res, self.const_res], writes=[bSres], inc=False)
                                self.op(self.pe, lambda: nc.tensor.matmul(bkS[:, 0:512], self.ident_f, negf[:, d, :, :], start=False, stop=True),
                                        reads=[self.const_res], writes=[bSres])
                                i2 = di % 2; di += 1
                                self.op(self.act, lambda i2=i2: nc.scalar.activation(out=DEC[i2][:, 0:512], in_=bkS[:, 0:512], func=AF.Exp), reads=[bSres], writes=[DEC_res[i2]])
                                self.op(self.dve, lambda i2=i2, g=g: nc.vector.tensor_tensor(out=STb[i2][:, :, :], in0=DEC[i2][:, 0:512].rearrange("p (h l) -> p h l", l=128),
                                                                                         in1=GT[:, g:g + 1, :].broadcast_to([128, 4, 128]), op=ALU.mult),
                                        reads=[DEC_res[i2], GT_res], writes=[STb_res[i2]])
                                for hh in range(4):
                                    h = h0 + hh
                                    yb, ybres = yI[(h - 16 * half) // 8]
                                    slot = h % 8
                                    self.op(self.pe, lambda i2=i2, hh=hh, h=h, yb=yb, slot=slot: nc.tensor.matmul(yb[:, slot * 64:(slot + 1) * 64], STb[i2][:, hh, :], xdt[:, h * 64:(h + 1) * 64],
                                                                                                                 start=True, stop=True),
                                            reads=[STb_res[i2], xdt_res], writes=[ybres], inc=(hh == 3))
                            for gi in range(2):
                                g = 2 * half + gi
                                self.op(self.dve, lambda g=g, gi=gi: nc.vector.tensor_tensor(out=tmpY[:, 0:512].rearrange("p (h e) -> p h e", e=64),
                                                                                         in0=yT_[gi][0][:, 0:512].rearrange("p (h e) -> p h e", e=64),
                                                                                         in1=EXPS[:, 8 * g:8 * g + 8].unsqueeze(2).broadcast_to([128, 8, 64]), op=ALU.mult),
                                        reads=[yT_[gi][1], EXPS_res], writes=[tmpY_res])
                                self.op(self.dve, lambda g=g, gi=gi: nc.vector.tensor_tensor(out=Y[:, g * 512:(g + 1) * 512], in0=yI[gi][0][:, 0:512], in1=tmpY[:, 0:512], op=ALU.add),
                                        reads=[yI[gi][1], tmpY_res], writes=[Y_res])
                        for g in range(4):
                            bkN, bNres = self.bank()
                            self.op(self.pe, lambda g=g: nc.tensor.matmul(bkN[:, 0:512], BT[:, g * 128:(g + 1) * 128], xw[:, g * 512:(g + 1) * 512], start=True, stop=True),
                                    reads=[BT_res, xw_res], writes=[bNres])
                            self.op(self.pool, lambda g=g: nc.gpsimd.tensor_tensor(out=tmpS[:, 0:512].rearrange("p (h e) -> p h e", e=64),
                                                                                   in0=state[:, g, :].rearrange("p (h e) -> p h e", e=64),
                                                                                   in1=EXPS[:, 64 + 8 * g:64 + 8 * g + 8].unsqueeze(2).broadcast_to([128, 8, 64]), op=ALU.mult),
                                    reads=[state_res[g], EXPS_res], writes=[tmpS_res])
                            self.op(self.dve, lambda g=g: nc.vector.tensor_tensor(out=state[:, g, :], in0=bkN[:, 0:512], in1=tmpS[:, 0:512], op=ALU.add),
                                    reads=[bNres, tmpS_res], writes=[state_res[g]])
                            self.op(self.act, lambda g=g: nc.scalar.copy(out=stbf[:, g, :], in_=state[:, g, :]), reads=[state_res[g]], writes=[stbf_res[g]])
                        if d == 0:
                            self.dma(self.sp, self.YFW[c0:c0 + 128, :], Y[:, :], sty, reads=[Y_res], pwrites=[self.YFW_res])
                            continue
                        self.op(self.dve, lambda: nc.vector.tensor_tensor(out=Y[:, :], in0=Y[:, :], in1=yfw[:, :], op=ALU.add), reads=[Y_res, yfw_res], writes=[Y_res])
                        for jb in range(4):
                            bk, bres = self.bank()
                            for jj in range(4):
                                j = jb * 4 + jj
                                self.op(self.pe, lambda j=j, jj=jj: nc.tensor.transpose(bk[:, jj * 128:(jj + 1) * 128], Y[:, j * 128:(j + 1) * 128], self.ident_f),
                                        reads=[Y_res, self.const_res], writes=[bres], inc=(jj == 3))
                            for jj in range(4):
                                j = jb * 4 + jj
                                self.op(self.dve, lambda j=j, jj=jj: nc.vector.scalar_tensor_tensor(out=yg[:, j, :], in0=xbc[:, j, cs:cs + 128], scalar=self.col("d_skip", j),
                                                                                                  in1=bk[:, jj * 128:(jj + 1) * 128], op0=ALU.mult, op1=ALU.add),
                                        reads=[xbc_res, bres, self.cols_res], writes=[yg_res])
                                self.op(self.pool, lambda j=j: nc.gpsimd.tensor_tensor(out=yg[:, j, :], in0=yg[:, j, :], in1=zs[:, j, cs:cs + 128], op=ALU.mult),
                                        reads=[yg_res, zs_res], writes=[yg_res])
                        self.rms_fm(lambda j: yg[:, j, :], 16, 128, lambda j: self.col("gate_norm", j), lambda j: ysT[:, j, cs:cs + 128], [yg_res], [ysT_res], group=4)
                    if d == 1:
                        self.dma(self.sp, self.MO[1024:3072, t0:t0 + ST].rearrange("(j p) t -> p j t", p=128), ysT[:, :, :], sto, reads=[ysT_res], pwrites=[self.MO_res])
            self.barrier()

    def mixer_in_cd(self, key, t0, T):
        nc = self.nc
        x, xres = self.x, self.x_res
        self.rms_fm(lambda j: x[:, j, 0:T], 8, T, lambda j: self.col("mix_norm1", j), lambda j: self.xn[:, j, 0:T], [xres], [self.xn_res])
        rhs = lambda kc: self.xn[:, kc, 0:T]
        self.dma(self.sp, self.ropec[:, 0:T], self.din["rope_cos"][:, t0:t0 + T], self.ropeclk, writes=[self.rope_res])
        self.dma(self.sp, self.ropes[:, 0:T], self.din["rope_sin"][:, t0:t0 + T], self.ropeclk, writes=[self.rope_res])
        big, big_res = self.big, self.big_res

        def ev_up(j, bk, bres, mwid):
            self.op(self.act, lambda: nc.scalar.copy(out=big[:, j, 0:T], in_=bk[:, 0:T]), reads=[bres], writes=[big_res])
        self.linear_fm(self.wstream("cd_in", 0, 1024), rhs, [self.xn_res], T, ev_up)
        self.dma(self.sp, self.UP[:, t0:t0 + T].rearrange("(j p) t -> p j t", p=128), big[:, :, 0:T], self.stclk[0], reads=[big_res], pwrites=[self.UP_res])
        for which, c0, gname, scale, dst, dres, ci in (("q", 1024, "cd_q_norm", 0.125, self.QT, self.QT_res, 1), ("k", 2048, "cd_k_norm", 1.0, self.KTd, self.KTd_res, 2)):
            b2, b2_res = self.big2, self.big2_res

            def ev_q(j, bk, bres, mwid):
                self.op(self.act, lambda: nc.scalar.copy(out=b2[:, j, 0:T], in_=bk[:, 0:T]), reads=[bres], writes=[b2_res])
            self.linear_fm(self.wstream("cd_in", c0, c0 + 1024), rhs, [self.xn_res], T, ev_q)
            qn, qn_res = self.h, self.h_res
            self.rms_fm(lambda j: b2[:, j, 0:T], 8, T, lambda j: self.col(gname, 0), lambda j: qn[:, j, 0:T], [b2_res], [qn_res],
                        group=1, scale=scale, lhsT=self.bd64_bf, n_norm=64)
            for j in range(8):
                bk, bres = self.bank()
                self.op(self.pe, lambda j=j: nc.tensor.matmul(bk[:, 0:T], self.perm_bf[:], qn[:, j, 0:T], start=True, stop=True),
                        reads=[qn_res, self.const_res], writes=[bres])
                self.op(self.dve, lambda: nc.vector.tensor_tensor(out=self.tmpn[:, 0:T], in0=bk[:, 0:T], in1=self.ropes[:, 0:T], op=ALU.mult),
                        reads=[bres, self.rope_res], writes=[self.tmpn_res])
                self.op(self.pool, lambda j=j: nc.gpsimd.tensor_tensor(out=self.tmpp[:, 0:T], in0=qn[:, j, 0:T], in1=self.ropec[:, 0:T], op=ALU.mult),
                        reads=[qn_res, self.rope_res], writes=[self.tmpp_res])
                self.op(self.dve, lambda j=j: nc.vector.tensor_tensor(out=qn[:, 8 + j, 0:T], in0=self.tmpn[:, 0:T], in1=self.tmpp[:, 0:T], op=ALU.add),
                        reads=[self.tmpn_res, self.tmpp_res], writes=[qn_res])
            self.dma(self.sp, dst[:, t0:t0 + T].rearrange("(j p) t -> p j t", p=128), qn[:, 8:16, 0:T], self.stclk[ci], reads=[qn_res], pwrites=[dres])
        wv = self.wstream("cd_in", 3072, 4096, 512)
        vt, vt_res = self.h, self.h_res
        nb = T // 128
        vview = bass.AP(self.h, 16 * T, [[24 * T, 128], [1024, nb], [1, 1024]])
        for pi in range(wv.npanels):
            wt, wres, mw = wv.get(pi)
            for tb in range(nb):
                bk, bres = self.bank()
                for kc in range(8):
                    self.op(self.pe, lambda kc=kc, tb=tb: nc.tensor.matmul(bk[:, 0:512], self.xn[:, kc, tb * 128:(tb + 1) * 128], wt[:, kc, 0:512],
                                                                         start=(kc == 0), stop=(kc == 7)),
                            reads=[wres, self.xn_res], writes=[bres], inc=(kc == 7))
                self.op(self.act, lambda tb=tb, pi=pi: nc.scalar.copy(out=vview[:, tb, pi * 512:(pi + 1) * 512], in_=bk[:, 0:512]),
                        reads=[bres], writes=[vt_res])
        wv.close()
        self.dma(self.sp, self.Vd[t0:t0 + T, :].rearrange("(tb p) e -> p tb e", p=128), vview, self.stclk[3], reads=[vt_res], pwrites=[self.Vd_res])

    def pass_d(self, key, L):
        nc = self.nc
        T = min(self.T, L)
        nt = L // T
        with contextlib.ExitStack() as st:
            W = T + 16
            pu = self.sb(st, "pu", [128, 8, W], F32); pu_res = Res("pu")
            sa = self.sb(st, "sa", [128, W], F32); sa_res = Res("sa")
            sbb = self.sb(st, "sbb", [128, W], F32); sbb_res = Res("sbb")
            icn = self.sb(st, "icn", [128, 4, T], F32); icn_res = Res("icn")
            dd = self.sb(st, "dd", [128, 8, T], BF16); dd_res = Res("dd")
            po = self.sb(st, "po", [128, 8, T], BF16); po_res = Res("po")
            ldc = self.clock("pld"); stc = self.clock("pst")
            self.alloc_wring(st, 1)
            wp = self.wstream("pool_w", 0, 256, 256)
            pw, pwres, _ = wp.get(0)
            for ti in range(nt):
                t0 = ti * T
                lo = max(0, t0 - 8); hi = min(L, t0 + T + 8)
                if lo != t0 - 8 or hi != t0 + T + 8:
                    self.op(self.dve, lambda: nc.vector.memset(pu[:], 0.0), writes=[pu_res])
                self.dma(self.sp, pu[:, :, lo - (t0 - 8):hi - (t0 - 8)], self.UP[:, lo:hi].rearrange("(j p) t -> p j t", p=128), ldc,
                         reads=[self.UP_res], writes=[pu_res])
                self.dma(self.sp, icn[:], self.din["invcnt_" + key][:, t0:t0 + T].partition_broadcast(128), self.clock("picn"), writes=[icn_res])
                for j in range(8):
                    g = j // 2
                    u = pu[:, j, :]
                    E1, E2 = (self.dve, self.pool) if j % 2 == 0 else (self.pool, self.dve)

                    def tt(E, out, a, b, op, reads, writes):
                        if E is self.dve:
                            self.op(E, lambda: nc.vector.tensor_tensor(out=out, in0=a, in1=b, op=op), reads=reads, writes=writes)
                        else:
                            self.op(E, lambda: nc.gpsimd.tensor_tensor(out=out, in0=a, in1=b, op=op), reads=reads, writes=writes)
                    E = self.dve
                    tt(E, sa[:, 1:W], u[:, 0:W - 1], u[:, 1:W], ALU.add, [pu_res], [sa_res])
                    cur, cur_res, oth, oth_res = sa, sa_res, sbb, sbb_res
                    lo_v, hi_v = 1, W
                    for sh in (1, 2, 4)[:g]:
                        a0, a1 = lo_v + sh, hi_v - sh
                        tt(E, oth[:, a0:a1], cur[:, a0 - sh:a1 - sh], cur[:, a0 + sh:a1 + sh], ALU.add, [cur_res], [oth_res])
                        cur, cur_res, oth, oth_res = oth, oth_res, cur, cur_res
                        lo_v, hi_v = a0, a1
                    assert lo_v <= 8 and hi_v >= 8 + T
                    tt(E, oth[:, 8:8 + T], cur[:, 8:8 + T], icn[:, g, :], ALU.mult, [cur_res, icn_res], [oth_res])
                    tt(E, dd[:, j, :], oth[:, 8:8 + T], u[:, 8:8 + T], ALU.subtract, [oth_res, pu_res], [dd_res])
                for g in range(4):
                    for m in range(2):
                        bk, bres = self.bank()
                        for kc in range(2):
                            self.op(self.pe, lambda kc=kc, g=g, m=m: nc.tensor.matmul(bk[:, 0:T], pw[:, 2 * g + kc, m * 128:(m + 1) * 128], dd[:, 2 * g + kc, :],
                                                                                   start=(kc == 0), stop=(kc == 1)),
                                    reads=[pwres, dd_res], writes=[bres], inc=(kc == 1))
                        jj = 2 * g + m
                        self.op(self.act, lambda jj=jj: nc.scalar.activation(out=po[:, jj, :], in_=bk[:, 0:T], func=AF.Copy, scale=self.col("pool_scale", jj)),
                                reads=[bres, self.cols_res], writes=[po_res])
                self.dma(self.sp, self.MO[0:1024, t0:t0 + T].rearrange("(j p) t -> p j t", p=128), po[:], stc, reads=[po_res], pwrites=[self.MO_res])
            wp.close()
            self.barrier()
        if self.flags.get("noattn"):
            return
        nk = L // 128
        with contextlib.ExitStack() as st:
            kth = self.sb(st, "kth", [128, L], BF16); kth_res = Res("kth")
            va = self.sb(st, "va", [128, nk, 132], BF16); va_res = Res("va")
            qh = [self.sb(st, "qh", [128, T], BF16) for _ in range(2)]; qh_res = [Res("qh0"), Res("qh1")]
            NE = 4
            qzb = [[self.sb(st, "qz", [128, T], BF16) for _ in range(2)] for _ in range(2)]
            for a_ in range(2):
                for b_ in range(2):
                    self.op(self.dve, lambda a_=a_, b_=b_: nc.vector.memset(qzb[a_][b_][:], 0.0), writes=[qh_res[a_]])
            eb = [self.sb(st, "eb", [128, T], BF16) for _ in range(NE)]; eb_res = [Res(f"eb{i}") for i in range(NE)]
            esA = [self.sb(st, "esA", [128, T], F32) for _ in range(2)]; esA_res = [Res("esA0"), Res("esA1")]
            esB = [self.sb(st, "esB", [128, T], F32) for _ in range(2)]; esB_res = [Res("esB0"), Res("esB1")]
            rden = [self.sb(st, "rden", [128, T], F32) for _ in range(2)]; rden_res = [Res("rden0"), Res("rden1")]
            of = self.sb(st, "of", [128, 1, T], F32); of_res = Res("of")
            o1 = self.sb(st, "o1", [128, T], F32); o1_res = Res("o1")
            ot = [self.sb(st, "ot", [128, T], BF16) for _ in range(2)]; ot_res = [Res("ot0"), Res("ot1")]
            kclk = self.clock("kld"); vclk = self.clock("vld"); qclk = [self.clock("qld0"), self.clock("qld1")]
            oclk = [self.clock("ost0"), self.clock("ost1")]
            acc = [(self.banks[i], self.bank_res[i]) for i in range(2)]
            den = [(self.banks[2 + i], self.bank_res[2 + i]) for i in range(2)]
            srot = [(self.banks[4 + i], self.bank_res[4 + i]) for i in range(4)]
            si = 0
            ei = 0
            qi = 0
            for hd in range(8):
                self.dma(self.sp, kth[:], self.KTd[hd * 128:(hd + 1) * 128, 0:L], kclk, reads=[self.KTd_res], writes=[kth_res])
                deps_done = False
                for k0 in range(0, nk, 16):
                    k1 = min(nk, k0 + 16)
                    self._deps(self.sp, [self.Vd_res], [va_res] if not deps_done else [], ())
                    deps_done = True
                    ins = nc.sync.dma_start(out=va[:, k0:k1, 0:128], in_=self.Vd[k0 * 128:k1 * 128, hd * 128:(hd + 1) * 128].rearrange("(kt p) e -> p kt e", p=128))
                    ins.then_inc(vclk.sem, 16)
                    vclk.n += 16
                va_res.w = {vclk: vclk.n}
                va_res.r = {}
                for qt in range(L // T):
                    qz = qzb[qi % 2]; qres = qh_res[qi % 2]
                    self.dma(self.sp, qz[0][0:64, :], self.QT[hd * 128:hd * 128 + 64, qt * T:(qt + 1) * T], qclk[qi % 2], reads=[self.QT_res], writes=[qres])
                    self.dma(self.sp, qz[1][64:128, :], self.QT[hd * 128 + 64:(hd + 1) * 128, qt * T:(qt + 1) * T], qclk[qi % 2], reads=[self.QT_res], writes=[qres])
                    steps = [(c, kt) for c in range(2) for kt in range(nk)]
                    LA = 2
                    pend = {}
                    for i in range(len(steps) + LA):
                        if i < len(steps):
                            c, kt = steps[i]
                            sbk, sres = srot[si % 4]; si += 1
                            self.op(self.pe, lambda c=c, kt=kt, sbk=sbk, qz=qz: nc.tensor.matmul(sbk[:, 0:T], kth[:, kt * 128:(kt + 1) * 128], qz[c][:, 0:T],
                                                                                             start=True, stop=True),
                                    reads=[kth_res, qres], writes=[sres])
                            e = eb[ei % NE]; eres = eb_res[ei % NE]; ei += 1
                            self.op(self.act, lambda e=e, sbk=sbk: nc.scalar.activation(out=e[:, 0:T], in_=sbk[:, 0:T], func=AF.Exp), reads=[sres], writes=[eres])
                            pend[i] = (e, eres)
                        j = i - LA
                        if j < 0:
                            continue
                        c, kt = steps[j]
                        e, eres = pend.pop(j)
                        ab, ares = acc[c]
                        self.op(self.pe, lambda e=e, kt=kt, ab=ab: nc.tensor.matmul(ab[:, 0:T], va[:, kt, 0:128], e[:, 0:T], start=(kt == 0), stop=(kt == nk - 1)),
                                reads=[eres, va_res], writes=[ares])
                        db, dres = den[c]
                        self.op(self.pe, lambda e=e, kt=kt, db=db: nc.tensor.matmul(db[:, 0:T], self.ones_bf[:], e[:, 0:T], start=(kt == 0), stop=(kt == nk - 1)),
                                reads=[eres, self.const_res], writes=[dres])
                        if kt == nk - 1:
                            self.op(self.dve, lambda c=c, db=db: nc.vector.reciprocal(out=rden[c][:, 0:T], in_=db[:, 0:T]), reads=[dres], writes=[rden_res[c]])
                    self.op(self.dve, lambda: nc.vector.tensor_tensor(out=of[:, 0, 0:T], in0=acc[0][0][:, 0:T], in1=rden[0][:, 0:T], op=ALU.mult),
                            reads=[acc[0][1], rden_res[0]], writes=[of_res])
                    self.op(self.dve, lambda: nc.vector.tensor_tensor(out=o1[:, 0:T], in0=acc[1][0][:, 0:T], in1=rden[1][:, 0:T], op=ALU.mult),
                            reads=[acc[1][1], rden_res[1]], writes=[o1_res])
                    self.op(self.dve, lambda: nc.vector.scalar_tensor_tensor(out=of[:, 0, 0:T], in0=o1[:, 0:T], scalar=self.neglam[:, 0:1], in1=of[:, 0, 0:T], op0=ALU.mult, op1=ALU.add),
                            reads=[o1_res, of_res, self.small_res], writes=[of_res])
                    o_t = ot[qi % 2]; o_res = ot_res[qi % 2]
                    self.bank_i = 4
                    self.rms_fm(lambda j: of[:, 0, 0:T], 1, T, lambda j: self.col("sub_norm_col", 0), lambda j: o_t[:, 0:T], [of_res], [o_res], scale=1.0 - LAMBDA_INIT1)
                    self.dma(self.sp, self.MO[1024 + hd * 128:1024 + (hd + 1) * 128, qt * T:(qt + 1) * T], o_t[:, 0:T], oclk[qi % 2], reads=[o_res], pwrites=[self.MO_res])
                    qi += 1
            self.barrier()

    def alloc_wring(self, stack, n=6):
        self.wring = [self.sb(stack, "wring", [128, 4096], BF16) for _ in range(n)]
        self.wring_res = [Res(f"wring{i}") for i in range(n)]
        self.wring_clk = [self.clock(f"wring{i}") for i in range(n)]
        self.wring_owner = [None] * n
        self.wring_i = 0

    def run_seq(self, key, L):
        nc = self.nc
        T = min(self.T, L)
        nt = L // T
        xin = self.din["xT_" + key]
        fl = self.flags
        with contextlib.ExitStack() as st0:
            self.sq = self.sb(st0, "sq", [128, 2, T], BF16); self.sq_res = [Res("sq0"), Res("sq1")]
            self.rstd = self.sb(st0, "rstd", [128, T], F32); self.rstd_res = Res("rstd")
            self.tmpn = self.sb(st0, "tmpn", [128, T], F32); self.tmpn_res = Res("tmpn")
            self.tmpp = self.sb(st0, "tmpp", [128, T], F32); self.tmpp_res = Res("tmpp")
            xld = self.clock("xld")
            xst = self.clock("xst")
            self.stclk = [self.clock(f"st{i}") for i in range(4)]
            self.ropeclk = self.clock("rope")

            def token_pass(stages, src, src_res, dst, dst_res, prep_layer=None, ab=False):
                with contextlib.ExitStack() as st:
                    self.alloc_wring(st, 6)
                    if ab:
                        self.cb = self.sb(st, "cb", [128, 24, T + 6], BF16); self.cb_res = [Res(f"cb{j}") for j in range(24)]
                        self.xo = self.sb(st, "xo", [128, 8, T + 2], BF16); self.xo_res = Res("xo")
                        self.cacc = [self.sb(st, "cacc", [128, T + 2], F32) for _ in range(2)]; self.cacc_res = [Res("cacc0"), Res("cacc1")]
                    xs = [self.sb(st, "x", [128, 8, T], F32) for _ in range(2)]; xs_res = [Res("x0"), Res("x1")]
                    xldc = [self.clock("xld0"), self.clock("xld1")]; xstc = [self.clock("xst0"), self.clock("xst1")]
                    self.xn = self.sb(st, "xn", [128, 8, T], BF16); self.xn_res = Res("xn")
                    self.h = self.sb(st, "h", [128, 24, T], BF16); self.h_res = Res("h")
                    self.big = self.sb(st, "big", [128, 8, T], F32); self.big_res = Res("big")
                    if not ab:
                        self.big2 = self.sb(st, "big2", [128, 8, T], F32); self.big2_res = Res("big2")
                    self.sg = self.sb(st, "sg", [128, 2, T], F32); self.sg_res = [Res("sg0"), Res("sg1")]
                    self.ex = self.sb(st, "ex", [128, 2, T], BF16); self.ex_res = [Res("ex0"), Res("ex1")]
                    if not ab:
                        self.ropec = self.sb(st, "ropec", [128, T], F32); self.ropes = self.sb(st, "ropes", [128, T], F32); self.rope_res = Res("rope")
                        self.KT = self.sb(st, "KT", [128, 8, NMEM], BF16); self.KT_res = Res("KT")
                        self.Vm = self.sb(st, "Vm", [128, 2, D], BF16); self.Vm_res = Res("Vm")
                    if prep_layer is not None and not fl.get("skip_cross"):
                        with contextlib.ExitStack() as st2:
                            self.cross_prep(key, prep_layer, st2)
                            self.barrier()
                    def load_x(ti):
                        b = ti % 2
                        self.dma(self.sp, xs[b][:], src[:, ti * T:(ti + 1) * T].rearrange("(j p) t -> p j t", p=128), xldc[b], reads=[src_res], writes=[xs_res[b]])
                    load_x(0)
                    for ti in range(nt):
                        t0 = ti * T
                        b = ti % 2
                        self.x, self.x_res = xs[b], xs_res[b]
                        for si_, stg in enumerate(stages):
                            stg(t0)
                            if si_ == 0 and ti + 1 < nt:
                                load_x(ti + 1)
                        if not stages and ti + 1 < nt:
                            load_x(ti + 1)
                        self.dma(self.sp, dst[:, t0:t0 + T].rearrange("(j p) t -> p j t", p=128), xs[b][:], xstc[b], reads=[xs_res[b]], pwrites=[dst_res])
                    self.barrier()

            sk_ffn = fl.get("skip_ffn"); sk_cr = fl.get("skip_cross")
            A = []
            if not sk_ffn: A.append(lambda t0: self.ffn(0, 1, T))
            if not fl.get("noab"): A.append(lambda t0: self.mixer_in_ab(key, t0, T, t0 == 0, t0 + T == L))
            token_pass(A, xin, Res("xin"), self.XR, self.XR_res, ab=True)
            if not fl.get("noab"):
                if not fl.get("nofnet"):
                    self.pass_fnet(key, L)
                if not fl.get("nossd"):
                    self.pass_ssd(key, L)
            C = []
            if not fl.get("noab"): C.append(lambda t0: self.mixer_out("ab_out", 24, t0, T))
            if not sk_cr: C.append(lambda t0: self.cross(0, T))
            if not sk_ffn:
                C.append(lambda t0: self.ffn(0, 2, T)); C.append(lambda t0: self.ffn(1, 1, T))
            if not fl.get("nocd"): C.append(lambda t0: self.mixer_in_cd(key, t0, T))
            token_pass(C, self.XR, self.XR_res, self.XR, self.XR_res, prep_layer=0)
            if not fl.get("nocd") and not fl.get("cd_in_only"):
                self.pass_d(key, L)
            E = []
            if not fl.get("nocd") and not fl.get("cd_in_only"): E.append(lambda t0: self.mixer_out("cd_out", 16, t0, T))
            if not sk_cr: E.append(lambda t0: self.cross(1, T))
            if not sk_ffn: E.append(lambda t0: self.ffn(1, 2, T))
            token_pass(E, self.XR, self.XR_res, self.yT[key], self.y_res[key], prep_layer=1)
            self.barrier()


class WStream:
    DEPTH = 2

    def __init__(self, b, nm, c0, c1, mw=None):
        self.b = b
        self.nm = nm
        K, N = [(w[3], w[4]) for w in WEIGHTS if w[0] == nm][0]
        self.KC = K // 128
        self.N = N
        MW = panel_w(nm)
        assert c0 % MW == 0, (nm, c0, MW)
        assert self.KC * MW <= 4096
        self.panels = [(c, min(MW, c1 - c, N - c)) for c in range(c0, c1, MW)]
        for c, w in self.panels:
            assert w == MW or c + w == N, (nm, c, w)
        self.npanels = len(self.panels)
        self.issued = 0
        self.cur = -1
        self.closed = False
        self.slots = {}

    def _issue(self, i):
        b = self.b
        s = b.wring_i
        b.wring_i = (b.wring_i + 1) % len(b.wring)
        own = b.wring_owner[s]
        if own is not None:
            st, idx = own
            assert st.closed or st.cur > idx, "weight ring over-subscribed"
        b.wring_owner[s] = (self, i)
        c, mw = self.panels[i]
        KC = self.KC
        src = bass.AP(b.ws[self.nm], KC * c, [[KC * self.N, 128], [1, KC * mw]])
        b.dma(b.sp, b.wring[s][:, 0:KC * mw], src, b.wring_clk[s], reads=[b.ws_res[self.nm]], writes=[b.wring_res[s]])
        view = bass.AP(b.wring[s], 0, [[4096, 128], [mw, KC], [1, mw]])
        self.slots[i] = (view, b.wring_res[s], mw)

    def prime(self):
        while self.issued < min(self.DEPTH, self.npanels):
            self._issue(self.issued)
            self.issued += 1

    def get(self, i):
        self.cur = i
        while self.issued < min(i + self.DEPTH, self.npanels):
            self._issue(self.issued)
            self.issued += 1
        return self.slots.pop(i)

    def close(self):
        self.closed = True


_CACHE = {}


def make_in_maps(inp, Lp, Ls, ncores):
    cols = pack_cols(inp)
    shared = {"cols": cols}
    shared.update(make_consts(Lp, Ls))
    for nm, src, l, K, N in WEIGHTS:
        w = np.asarray(inp[src][l], np.float32)
        shared["w_" + nm] = np.ascontiguousarray(w.reshape(K, N))
    maps = []
    for c in range(ncores):
        m = dict(shared)
        m["xT_p"] = np.ascontiguousarray(np.asarray(inp["x_prompt"][c], np.float32).T)
        m["xT_s"] = np.ascontiguousarray(np.asarray(inp["x_sample"][c], np.float32).T)
        m["memT_p"] = np.ascontiguousarray(np.asarray(inp["mem_prompt"][c], np.float32).T)
        m["memT_s"] = np.ascontiguousarray(np.asarray(inp["mem_sample"][c], np.float32).T)
        maps.append(m)
    return maps


def run(inp, Lp, Ls, ncores, flags=None):
    key = (Lp, Ls, tuple(sorted((flags or {}).items())))
    if key not in _CACHE:
        _CACHE[key] = Builder(Lp, Ls, flags).build()
    nc = _CACHE[key]
    maps = make_in_maps(inp, Lp, Ls, ncores)
    res = run_bass_kernel_spmd(nc, maps, core_ids=list(range(ncores)))
    yp = np.stack([np.ascontiguousarray(r["yT_p"].T) for r in res.results], 0)
    ys = np.stack([np.ascontiguousarray(r["yT_s"].T) for r in res.results], 0)
    return yp.astype(np.float32), ys.astype(np.float32)


def kernel(**inputs):
    inp = {k: np.asarray(v) for k, v in inputs.items()}
    return run(inp, LP_FULL, LS_FULL, 8)
```
